# Optimizing a Trainium2 kernel written in Bass

```python
import math
import jax
import jax.numpy as jnp
from jax import lax
import numpy as np

D_MODEL = 1024
BATCH = 8
SEQ = 2048
DEPTH = 2

GRID_W = 64
HEAD_DIM = 64
ATTN_HEADS = 8
ATTN_KV_HEADS = 2
ATTN_GROUP = ATTN_HEADS // ATTN_KV_HEADS
ATTN_WIDTH = ATTN_HEADS * HEAD_DIM
KV_WIDTH = ATTN_KV_HEADS * HEAD_DIM
ATTN_IN = ATTN_WIDTH + 2 * KV_WIDTH
Q_BLOCK = 128
ROPE_THETA = 10000.0
RWKV_HEADS = 8
RWKV_HEAD = 64
RWKV_WIDTH = RWKV_HEADS * RWKV_HEAD
DECAY_LORA = 64
AAA_LORA = 64
GATE_LORA = 160
RWKV_IN = 3 * RWKV_WIDTH + 2 * DECAY_LORA + 2 * AAA_LORA + GATE_LORA
MIX_IN = ATTN_IN + RWKV_IN
GN_EPS = 64e-5
HYENA_ORDER = 2
HYENA_BANDS = 16
HYENA_EMB = 2 * HYENA_BANDS + 1
HYENA_FILTER_HIDDEN = 64
HYENA_TARGET = 1e-2
HYENA_FAST_DECAY = 0.3
HYENA_SLOW_DECAY = 1.5
SHORT_CONV = 3
D_FF = -(-(8 * D_MODEL) // (3 * 256)) * 256
NORM_EPS = 1e-6

kernel_name = 'hybrid_attn_rwkv7_hyena_encoder'


def rms_norm(x, gain):
    xf = x.astype(jnp.float32)
    y = xf * lax.rsqrt(jnp.mean(xf * xf, axis=-1, keepdims=True) + NORM_EPS)
    return (y * gain.astype(jnp.float32)).astype(x.dtype)


def axial_rope_angles(n_tok):
    rows = n_tok // GRID_W
    row = jnp.repeat(jnp.arange(rows), GRID_W).astype(jnp.float32)
    col = jnp.tile(jnp.arange(GRID_W), rows).astype(jnp.float32)
    half = HEAD_DIM // 2
    inv_freq = ROPE_THETA ** (-jnp.arange(0, half, 2, dtype=jnp.float32) / half)
    ang = jnp.concatenate([row[:, None] * inv_freq, col[:, None] * inv_freq], axis=-1)
    return jnp.cos(ang), jnp.sin(ang)


def apply_rope(x, cos, sin):
    b, l, h, d = x.shape
    xf = x.astype(jnp.float32).reshape(b, l, h, d // 2, 2)
    x0, x1 = xf[..., 0], xf[..., 1]
    cb = cos[None, :, None, :]
    sb = sin[None, :, None, :]
    out = jnp.stack([x0 * cb - x1 * sb, x0 * sb + x1 * cb], axis=-1)
    return out.reshape(b, l, h, d).astype(x.dtype)


def blocked_gqa(q, k, v):
    b, l, _, _ = q.shape
    nb = l // Q_BLOCK
    qb = q.reshape(b, nb, Q_BLOCK, ATTN_KV_HEADS, ATTN_GROUP, HEAD_DIM).transpose(1, 0, 2, 3, 4, 5)
    scale = HEAD_DIM ** -0.5

    def one_block(q_blk):
        s = jnp.einsum('bqhgd,bkhd->bhgqk', q_blk, k, preferred_element_type=jnp.float32) * scale
        p = jax.nn.softmax(s, axis=-1)
        return jnp.einsum('bhgqk,bkhd->bqhgd', p.astype(v.dtype), v)

    out = lax.map(one_block, qb)
    return out.transpose(1, 0, 2, 3, 4, 5).reshape(b, l, ATTN_WIDTH)


def centred_shift(p):
    prev = jnp.pad(p[:, :-1], ((0, 0), (1, 0), (0, 0)))
    nxt = jnp.pad(p[:, 1:], ((0, 0), (0, 1), (0, 0)))
    return 0.5 * (prev + nxt)


def rwkv7_scan(r, w, k, v, z, bb, reverse):
    b, l, h, n = r.shape
    xs = tuple(t.transpose(1, 0, 2, 3) for t in (r, w, k, v, z, bb))

    def step(state, inp):
        r_t, w_t, k_t, v_t, z_t, b_t = inp
        sz = jnp.einsum('bhvk,bhk->bhv', state, z_t)
        state = (state * w_t[:, :, None, :] + sz[..., None] * b_t[:, :, None, :]
                 + v_t[..., None] * k_t[:, :, None, :])
        return state, jnp.einsum('bhvk,bhk->bhv', state, r_t)

    s0 = jnp.zeros((b, h, n, n), jnp.float32)
    _, ys = lax.scan(step, s0, xs, reverse=reverse)
    return ys.transpose(1, 0, 2, 3)


def rwkv7_bidirectional(p, mu, w0, w_up, a0, a_up, g_up, k_k, k_a, r_k, ln_g, ln_b):
    out_dtype = p.dtype
    p = p.astype(jnp.float32)
    b, l, _ = p.shape
    H, N, C = RWKV_HEADS, RWKV_HEAD, RWKV_WIDTH
    p = p + (centred_shift(p) - p) * mu
    o_w = 3 * C
    o_a = o_w + 2 * DECAY_LORA
    o_g = o_a + 2 * AAA_LORA
    r = p[..., :C]
    k = p[..., C:2 * C]
    v = p[..., 2 * C:o_w]
    w_lo = p[..., o_w:o_a].reshape(b, l, 2, DECAY_LORA)
    a_lo = p[..., o_a:o_g].reshape(b, l, 2, AAA_LORA)
    g_lo = p[..., o_g:]
    decay = jnp.exp(-math.exp(-0.5) * jax.nn.sigmoid(
        w0 + jnp.einsum('bldr,drc->bldc', jnp.tanh(w_lo), w_up)))
    a = jax.nn.sigmoid(a0 + jnp.einsum('bldr,drc->bldc', a_lo, a_up))
    kk = (k * k_k).reshape(b, l, H, N)
    kk = kk * lax.rsqrt(jnp.maximum(jnp.sum(kk * kk, axis=-1, keepdims=True), 1e-24))
    k_dir = k[:, :, None, :] * (1.0 + (a - 1.0) * k_a)

    def heads(t):
        return t.reshape(b, l, H, N)

    rh, vh = heads(r), heads(v)
    y_fwd = rwkv7_scan(rh, heads(decay[:, :, 0]), heads(k_dir[:, :, 0]), vh, -kk,
                       kk * heads(a[:, :, 0]), reverse=False)
    y_bwd = rwkv7_scan(rh, heads(decay[:, :, 1]), heads(k_dir[:, :, 1]), vh, -kk,
                       kk * heads(a[:, :, 1]), reverse=True)
    y = y_fwd + y_bwd
    mean = jnp.mean(y, axis=-1, keepdims=True)
    var = jnp.mean(jnp.square(y - mean), axis=-1, keepdims=True)
    y = ((y - mean) * lax.rsqrt(var + GN_EPS)).reshape(b, l, C) * ln_g + ln_b
    bonus = (jnp.sum(rh * heads(k) * r_k, axis=-1, keepdims=True) * vh).reshape(b, l, C)
    g = jax.nn.sigmoid(g_lo) @ g_up
    return ((y + bonus) * g).astype(out_dtype)


def attention_rwkv_mixer(h, cos, sin, w_in, w_out, q_norm, k_norm, mu, w0, w_up, a0, a_up,
                         g_up, k_k, k_a, r_k, ln_g, ln_b):
    b, l, _ = h.shape
    p = h @ w_in
    q = p[..., :ATTN_WIDTH].reshape(b, l, ATTN_HEADS, HEAD_DIM)
    k = p[..., ATTN_WIDTH:ATTN_WIDTH + KV_WIDTH].reshape(b, l, ATTN_KV_HEADS, HEAD_DIM)
    v = p[..., ATTN_WIDTH + KV_WIDTH:ATTN_IN].reshape(b, l, ATTN_KV_HEADS, HEAD_DIM)
    q = apply_rope(rms_norm(q, q_norm), cos, sin)
    k = apply_rope(rms_norm(k, k_norm), cos, sin)
    y_attn = blocked_gqa(q, k, v)
    y_rwkv = rwkv7_bidirectional(p[..., ATTN_IN:], mu, w0, w_up, a0, a_up, g_up,
                                 k_k, k_a, r_k, ln_g, ln_b)
    return jnp.concatenate([y_attn, y_rwkv.astype(y_attn.dtype)], axis=-1) @ w_out


def hyena_filters(n_tok, w1, b1, w2, b2, w3, b3, sin_freq, w_out):
    f32 = jnp.float32
    t = jnp.linspace(0.0, 1.0, n_tok, dtype=f32)[:, None]
    omega = (2.0 * math.pi / n_tok) * jnp.arange(n_tok, dtype=f32)[:, None]
    bands = jnp.linspace(1e-4, HYENA_BANDS - 1, HYENA_BANDS, dtype=f32)[None, :]
    feats = jnp.concatenate([t, jnp.cos(bands * omega), -jnp.sin(bands * omega)], axis=-1)
    fr = sin_freq.astype(f32)
    hid = jnp.sin(fr * (feats @ w1.astype(f32) + b1))
    hid = jnp.sin(fr * (hid @ w2.astype(f32) + b2))
    hid = jnp.sin(fr * (hid @ w3.astype(f32) + b3))
    filt = (hid @ w_out.astype(f32)).reshape(n_tok, HYENA_ORDER, 2, D_MODEL)
    min_decay = math.log(HYENA_TARGET) / HYENA_SLOW_DECAY
    max_decay = math.log(HYENA_TARGET) / HYENA_FAST_DECAY
    deltas = jnp.abs(jnp.linspace(min_decay, max_decay, D_MODEL, dtype=f32))
    window = jnp.exp(-t * deltas)
    return filt * window[:, None, None, :]


def bidirectional_fft_conv(u, h_fwd, h_bwd, skip):
    n = u.shape[1]
    uf = u.astype(jnp.float32)
    taps = jnp.concatenate([h_fwd, jnp.zeros_like(h_fwd[:1]), h_bwd[:0:-1]], axis=0)
    spec = jnp.fft.rfft(uf, n=2 * n, axis=1) * jnp.fft.rfft(taps, n=2 * n, axis=0)[None]
    y = jnp.fft.irfft(spec, n=2 * n, axis=1)[:, :n]
    return (y + uf * skip.astype(jnp.float32)).astype(u.dtype)


def hyena_mixer(h, w_in, conv_w, conv_b, f_w1, f_b1, f_w2, f_b2, f_w3, f_b3, sin_freq,
                f_out, skip, w_out):
    b, l, d = h.shape
    p = h @ w_in
    p = lax.conv_general_dilated(
        p, conv_w[:, None, :].astype(p.dtype), window_strides=(1,),
        padding=((SHORT_CONV // 2, SHORT_CONV // 2),),
        dimension_numbers=('NWC', 'WIO', 'NWC'), feature_group_count=p.shape[-1]) + conv_b
    v, x1, x2 = jnp.split(p, 3, axis=-1)
    filt = hyena_filters(l, f_w1, f_b1, f_w2, f_b2, f_w3, f_b3, sin_freq, f_out)
    z = v
    for o, gate in enumerate((x1, x2)):
        z = gate * bidirectional_fft_conv(z, filt[:, o, 0], filt[:, o, 1], skip[o])
    return z @ w_out


def swiglu(h, w1, w3, w2):
    return (jax.nn.silu(h @ w1) * (h @ w3)) @ w2


def setup_inputs(seed: int = 0) -> dict:
    key = jax.random.key(seed)
    keys = iter(jax.random.split(key, 64))
    f32 = jnp.float32
    D = D_MODEL
    HF = HYENA_FILTER_HIDDEN
    ne = (DEPTH + 1) // 2
    no = DEPTH // 2

    def normal(shape, std):
        return jax.random.normal(next(keys), shape, f32) * std

    def uniform(shape, lo, hi):
        return jax.random.uniform(next(keys), shape, f32, lo, hi)

    return {
        'x': normal((BATCH, SEQ, D), 1.0),
        'c': normal((BATCH, D), 1.0),
        'mix_w_in': normal((ne, D, MIX_IN), D ** -0.5),
        'mix_w_out': normal((ne, D, D), D ** -0.5),
        'attn_q_norm': 1.0 + normal((ne, HEAD_DIM), 0.02),
        'attn_k_norm': 1.0 + normal((ne, HEAD_DIM), 0.02),
        'rwkv_mu': uniform((ne, RWKV_IN), 0.1, 0.9),
        'rwkv_w0': normal((ne, 2, RWKV_WIDTH), 1.0) - 0.5,
        'rwkv_w_up': normal((ne, 2, DECAY_LORA, RWKV_WIDTH), 0.5 * DECAY_LORA ** -0.5),
        'rwkv_a0': normal((ne, 2, RWKV_WIDTH), 0.5),
        'rwkv_a_up': normal((ne, 2, AAA_LORA, RWKV_WIDTH), 0.5 * AAA_LORA ** -0.5),
        'rwkv_g_up': normal((ne, GATE_LORA, RWKV_WIDTH), GATE_LORA ** -0.5),
        'rwkv_k_k': 0.85 + normal((ne, RWKV_WIDTH), 0.02),
        'rwkv_k_a': 1.0 + normal((ne, RWKV_WIDTH), 0.02),
        'rwkv_r_k': normal((ne, RWKV_HEADS, RWKV_HEAD), 0.1),
        'rwkv_ln_g': 1.0 + normal((ne, RWKV_WIDTH), 0.02),
        'rwkv_ln_b': normal((ne, RWKV_WIDTH), 0.02),
        'hy_w_in': normal((no, D, 3 * D), D ** -0.5),
        'hy_conv_w': normal((no, SHORT_CONV, 3 * D), 0.6),
        'hy_conv_b': normal((no, 3 * D), 0.02),
        'hy_f_w1': normal((no, HYENA_EMB, HF), HYENA_EMB ** -0.5),
        'hy_f_b1': normal((no, HF), 0.1),
        'hy_f_w2': normal((no, HF, HF), HF ** -0.5),
        'hy_f_b2': normal((no, HF), 0.1),
        'hy_f_w3': normal((no, HF, HF), HF ** -0.5),
        'hy_f_b3': normal((no, HF), 0.1),
        'hy_sin_freq': 1.0 + normal((no, HF), 0.02),
        'hy_f_out': normal((no, HF, HYENA_ORDER * 2 * D), 0.05 * HF ** -0.5),
        'hy_skip': normal((no, HYENA_ORDER, D), 0.5),
        'hy_w_out': normal((no, D, D), D ** -0.5),
        'ada_w': normal((DEPTH, D, 6 * D), 0.5 * D ** -0.5),
        'ada_b': normal((DEPTH, 6 * D), 0.02),
        'norm_mix': 1.0 + normal((DEPTH, D), 0.02),
        'norm_ffn': 1.0 + normal((DEPTH, D), 0.02),
        'ffn_w1': normal((DEPTH, D, D_FF), D ** -0.5),
        'ffn_w3': normal((DEPTH, D, D_FF), D ** -0.5),
        'ffn_w2': normal((DEPTH, D_FF, D), D_FF ** -0.5),
        'final_norm': 1.0 + normal((D,), 0.02),
    }


def reference(x, c, mix_w_in, mix_w_out, attn_q_norm, attn_k_norm, rwkv_mu, rwkv_w0, rwkv_w_up,
              rwkv_a0, rwkv_a_up, rwkv_g_up, rwkv_k_k, rwkv_k_a, rwkv_r_k, rwkv_ln_g, rwkv_ln_b,
              hy_w_in, hy_conv_w, hy_conv_b, hy_f_w1, hy_f_b1, hy_f_w2, hy_f_b2, hy_f_w3, hy_f_b3,
              hy_sin_freq, hy_f_out, hy_skip, hy_w_out, ada_w, ada_b, norm_mix, norm_ffn,
              ffn_w1, ffn_w3, ffn_w2, final_norm):
    n_tok = x.shape[1]
    cos, sin = axial_rope_angles(n_tok)
    cond = jax.nn.silu(c)
    for layer in range(DEPTH):
        mod = (cond @ ada_w[layer] + ada_b[layer])[:, None, :]
        shift_m, scale_m, gate_m, shift_f, scale_f, gate_f = jnp.split(mod, 6, axis=-1)
        h = rms_norm(x, norm_mix[layer]) * (1.0 + scale_m) + shift_m
        i = layer // 2
        if layer % 2 == 0:
            y = attention_rwkv_mixer(h, cos, sin, mix_w_in[i], mix_w_out[i], attn_q_norm[i],
                                     attn_k_norm[i], rwkv_mu[i], rwkv_w0[i], rwkv_w_up[i],
                                     rwkv_a0[i], rwkv_a_up[i], rwkv_g_up[i], rwkv_k_k[i],
                                     rwkv_k_a[i], rwkv_r_k[i], rwkv_ln_g[i], rwkv_ln_b[i])
        else:
            y = hyena_mixer(h, hy_w_in[i], hy_conv_w[i], hy_conv_b[i], hy_f_w1[i], hy_f_b1[i],
                            hy_f_w2[i], hy_f_b2[i], hy_f_w3[i], hy_f_b3[i], hy_sin_freq[i],
                            hy_f_out[i], hy_skip[i], hy_w_out[i])
        x = x + gate_m * y
        h = rms_norm(x, norm_ffn[layer]) * (1.0 + scale_f) + shift_f
        x = x + gate_f * swiglu(h, ffn_w1[layer], ffn_w3[layer], ffn_w2[layer])
    return rms_norm(x, final_norm)
```

```python
import numpy as np
from contextlib import ExitStack
import concourse.bass as bass
import concourse.mybir as mybir
from concourse.bass_utils import run_bass_kernel_spmd

F32 = mybir.dt.float32
BF16 = mybir.dt.bfloat16
ALU = mybir.AluOpType
AF = mybir.ActivationFunctionType
DSZ = {F32: 4, BF16: 2}


class V:
    __slots__ = ("ap", "rect")

    def __init__(self, ap, rect):
        self.ap = ap
        self.rect = rect


class Buf:
    def __init__(self, space, base_ap2d, p, shape, dtype, byte0):
        self.space = space
        self.p = p
        self.shape = tuple(shape)
        self.dtype = dtype
        self.byte0 = byte0
        self.esz = DSZ[dtype]
        n = int(np.prod(shape))
        if len(shape) == 1:
            self.full = base_ap2d
        else:
            names = " ".join("d%d" % i for i in range(len(shape)))
            kw = {"d%d" % i: int(s) for i, s in enumerate(shape)}
            self.full = base_ap2d.rearrange("p (%s) -> p %s" % (names, names), **kw)
        self.strides = [int(np.prod(shape[i + 1:])) for i in range(len(shape))]

    def __getitem__(self, idx):
        if not isinstance(idx, tuple):
            idx = (idx,)
        idx = list(idx) + [slice(None)] * (1 + len(self.shape) - len(idx))
        ps = idx[0]
        if isinstance(ps, int):
            ps = slice(ps, ps + 1)
        p0 = ps.start or 0
        p1 = self.p if ps.stop is None else ps.stop
        lo = 0
        hi = 0
        for i, (ix, n) in enumerate(zip(idx[1:], self.shape)):
            if isinstance(ix, int):
                a, b = ix, ix + 1
            else:
                a = ix.start or 0
                b = n if ix.stop is None else ix.stop
                if ix.step is not None and ix.step < 0:
                    a, b = 0, n
            lo += a * self.strides[i]
            hi += (b - 1) * self.strides[i]
        rect = (self.space, p0, p1, self.byte0 + lo * self.esz, self.byte0 + (hi + 1) * self.esz)
        ap = self.full[tuple([slice(p0, p1)] + idx[1:])]
        return V(ap, rect)


class Arena:
    def __init__(self, prog, name, nbytes, psum=False):
        self.prog = prog
        self.name = name
        self.nbytes = nbytes
        self.psum = psum
        nc = prog.nc
        if psum:
            self.t = prog.stack.enter_context(nc.psum_tensor(name, [128, nbytes // 4], F32))
        else:
            self.t = prog.stack.enter_context(nc.sbuf_tensor(name, [128, nbytes // 4], F32))
        self.top = 0

    def alloc(self, shape, dtype, p=128, align=4):
        esz = DSZ[dtype]
        n = int(np.prod(shape))
        nb = (n * esz + align - 1) // align * align
        off = (self.top + align - 1) // align * align
        assert off + nb <= self.nbytes, "arena %s overflow: %d + %d > %d" % (self.name, off, nb, self.nbytes)
        self.top = off + nb
        return self.at(off, shape, dtype, p)

    def at(self, off, shape, dtype, p=128):
        esz = DSZ[dtype]
        n = int(np.prod(shape))
        nb = (n * esz + 3) // 4 * 4
        ap = self.t[0:p, off // 4:(off + nb) // 4]
        if dtype != F32:
            ap = ap.bitcast(dtype)
            ap = ap[:, 0:n]
        return Buf(self.name, ap, p, shape, dtype, off)

    def mark(self):
        return self.top

    def release(self, m):
        self.top = m


class Op:
    __slots__ = ("eng", "fn", "deps", "dma", "sem", "semval", "signal", "id", "prewait")


ENGS = ("pe", "act", "dve", "pool", "sp")


class Prog:
    def __init__(self, nc, n_dma_sems=(("sp", 16), ("act", 6), ("pool", 10))):
        self.nc = nc
        self.stack = ExitStack()
        self.ops = []
        self.eng_ops = {e: [] for e in ENGS}
        self.acc = {}
        self.dma_pools = {}
        self.dma_rr = {}
        self.sem_total = {}
        self.esem = {}
        for e in ("pe", "act", "dve", "pool"):
            self.esem[e] = self.stack.enter_context(nc.semaphore("s_" + e))
        for e, n in n_dma_sems:
            self.dma_pools[e] = [self.stack.enter_context(nc.semaphore("d_%s%d" % (e, i))) for i in range(n)]
            self.dma_rr[e] = 0
        self.n_dram = 0

    def dram_in(self, name, shape, dtype=F32):
        t = self.nc.dram_tensor(name, list(shape), dtype, kind="ExternalInput")
        return self._dram_buf(t, name, shape, dtype)

    def dram_out(self, name, shape, dtype=F32):
        t = self.nc.dram_tensor(name, list(shape), dtype, kind="ExternalOutput")
        return self._dram_buf(t, name, shape, dtype)

    def dram_scratch(self, name, shape, dtype=F32):
        t = self.nc.dram_tensor(name, list(shape), dtype, kind="Internal")
        return self._dram_buf(t, name, shape, dtype)

    def _dram_buf(self, t, name, shape, dtype):
        b = Buf.__new__(Buf)
        b.space = "dram:" + name
        b.p = shape[0]
        b.shape = tuple(shape[1:])
        b.dtype = dtype
        b.byte0 = 0
        b.esz = DSZ[dtype]
        b.full = t.ap()
        b.strides = [int(np.prod(shape[i + 1:])) for i in range(1, len(shape))]
        b.handle = t
        return b

    def dview(self, buf, ap):
        return V(ap, (buf.space, 0, 1 << 30, 0, 1 << 40))

    def _deps(self, reads, writes, opid):
        deps = set()
        for views, is_w in ((reads, False), (writes, True)):
            for v in views:
                sp, p0, p1, b0, b1 = v.rect if isinstance(v, V) else v
                lst = self.acc.setdefault(sp, [])
                keep = []
                for a in lst:
                    ap0, ap1, ab0, ab1, aid, aw = a
                    ov = not (ap1 <= p0 or p1 <= ap0 or ab1 <= b0 or b1 <= ab0)
                    if ov and (aw or is_w) and aid != opid:
                        deps.add(aid)
                    contained = ov and ap0 >= p0 and ap1 <= p1 and ab0 >= b0 and ab1 <= b1
                    if is_w and contained and aid != opid:
                        continue
                    keep.append(a)
                keep.append((p0, p1, b0, b1, opid, is_w))
                self.acc[sp] = keep
        return deps

    def op(self, eng, fn, reads=(), writes=(), dma=False):
        o = Op()
        o.eng = eng
        o.fn = fn
        o.dma = dma
        o.id = len(self.ops)
        o.signal = False
        o.sem = None
        o.semval = 0
        o.prewait = None
        o.deps = self._deps(reads, writes, o.id)
        self.ops.append(o)
        self.eng_ops[eng].append(o)
        return o

    def dma(self, eng, out, in_, **kw):
        q = {"sp": self.nc.sync, "act": self.nc.scalar, "pool": self.nc.gpsimd}[eng]
        return self.op(eng, lambda e: q.dma_start(out=out.ap, in_=in_.ap, **kw), [in_], [out], dma=True)

    def mm(self, out, lhsT, rhs, start=True, stop=True, **kw):
        return self.op("pe", lambda e: self.nc.tensor.matmul(out.ap, lhsT.ap, rhs.ap, start=start, stop=stop, **kw),
                       [lhsT, rhs], [out])

    def transpose(self, out, in_, ident):
        return self.op("pe", lambda e: self.nc.tensor.transpose(out.ap, in_.ap, ident.ap), [in_, ident], [out])

    def _ce(self, eng):
        return {"act": self.nc.scalar, "dve": self.nc.vector, "pool": self.nc.gpsimd}[eng]

    def tt(self, eng, out, in0, in1, op):
        E = self._ce(eng)
        return self.op(eng, lambda e: E.tensor_tensor(out=out.ap, in0=in0.ap, in1=in1.ap, op=op), [in0, in1], [out])

    def ts(self, eng, out, in0, s1, op0, s2=None, op1=None):
        E = self._ce(eng)
        rd = [in0] + [s for s in (s1, s2) if isinstance(s, V)]
        a1 = s1.ap if isinstance(s1, V) else s1
        a2 = s2.ap if isinstance(s2, V) else s2
        if op1 is None:
            return self.op(eng, lambda e: E.tensor_scalar(out=out.ap, in0=in0.ap, scalar1=a1, scalar2=None, op0=op0), rd, [out])
        return self.op(eng, lambda e: E.tensor_scalar(out=out.ap, in0=in0.ap, scalar1=a1, scalar2=a2, op0=op0, op1=op1), rd, [out])

    def stt(self, out, in0, s, in1, op0, op1):
        rd = [in0, in1] + ([s] if isinstance(s, V) else [])
        a = s.ap if isinstance(s, V) else s
        return self.op("dve", lambda e: self.nc.vector.scalar_tensor_tensor(out=out.ap, in0=in0.ap, scalar=a, in1=in1.ap, op0=op0, op1=op1), rd, [out])

    def copy(self, eng, out, in_):
        if eng == "act":
            return self.op("act", lambda e: self.nc.scalar.copy(out=out.ap, in_=in_.ap), [in_], [out])
        E = self._ce(eng)
        return self.op(eng, lambda e: E.tensor_copy(out=out.ap, in_=in_.ap), [in_], [out])

    def actf(self, out, in_, func, bias=None, scale=None):
        rd = [in_] + [s for s in (bias, scale) if isinstance(s, V)]
        kw = {}
        if bias is not None:
            kw["bias"] = bias.ap if isinstance(bias, V) else bias
        if scale is not None:
            kw["scale"] = scale.ap if isinstance(scale, V) else scale
        return self.op("act", lambda e: self.nc.scalar.activation(out=out.ap, in_=in_.ap, func=func, **kw), rd, [out])

    def recip(self, out, in_):
        return self.op("dve", lambda e: self.nc.vector.reciprocal(out=out.ap, in_=in_.ap), [in_], [out])

    def memset(self, eng, out, val):
        E = self._ce(eng)
        return self.op(eng, lambda e: E.memset(out.ap, val), [], [out])

    def finalize(self):
        for o in self.ops:
            for d in o.deps:
                p = self.ops[d]
                if p.eng == "pe" and o.eng == "pe" and not p.dma:
                    continue
                p.signal = True
        cnt = {e: 0 for e in self.esem}
        for o in self.ops:
            if o.dma:
                pool = self.dma_pools[o.eng]
                i = self.dma_rr[o.eng]
                self.dma_rr[o.eng] = (i + 1) % len(pool)
                s = pool[i]
                prev = self.sem_total.get(id(s), 0)
                o.prewait = (s, prev) if prev > 0 else None
                o.sem = s
                o.semval = prev + 16
                self.sem_total[id(s)] = o.semval
            elif o.signal:
                cnt[o.eng] += 1
                o.sem = self.esem[o.eng]
                o.semval = cnt[o.eng]
        self.counts = cnt

    def emit(self):
        self.finalize()
        nc = self.nc
        engobj = {"pe": nc.tensor, "act": nc.scalar, "dve": nc.vector, "pool": nc.gpsimd, "sp": nc.sync}
        with nc.Block() as block:
            def run(eng):
                E = engobj[eng]
                waited = {}
                for o in self.eng_ops[eng]:
                    need = {}
                    if o.prewait is not None:
                        need[id(o.prewait[0])] = o.prewait
                    for d in o.deps:
                        p = self.ops[d]
                        if p.sem is None:
                            continue
                        if p.eng == "pe" and eng == "pe" and not p.dma:
                            continue
                        k = id(p.sem)
                        if k not in need or need[k][1] < p.semval:
                            need[k] = (p.sem, p.semval)
                    for k, (s, v) in need.items():
                        if waited.get(k, 0) >= v:
                            continue
                        E.wait_ge(s, v)
                        waited[k] = v
                    ins = o.fn(E)
                    if o.dma:
                        ins.then_inc(o.sem, 16)
                    elif o.signal:
                        ins.then_inc(o.sem, 1)

            @block.tensor
            def _(e):
                run("pe")

            @block.scalar
            def _(e):
                run("act")

            @block.vector
            def _(e):
                run("dve")

            @block.gpsimd
            def _(e):
                run("pool")

            @block.sync
            def _(e):
                run("sp")
                for pool in self.dma_pools.values():
                    for s in pool:
                        tot = self.sem_total.get(id(s), 0)
                        if tot > 0:
                            nc.sync.wait_ge(s, tot)
        self.stack.close()
import ml_dtypes

D = 1024
T = 2048
DFF = 2816
NB = 8
EPS = 1e-6


class VecPack:
    def __init__(self):
        self.cols = {}
        self.n = 0
        self.parts = []

    def add(self, name, arr):
        a = np.asarray(arr, np.float32).reshape(-1)
        n = a.shape[0]
        nc_ = (n + 127) // 128
        pad = np.zeros(nc_ * 128, np.float32)
        pad[:n] = a
        self.cols[name] = (self.n, nc_)
        self.parts.append(pad.reshape(nc_, 128).T)
        self.n += nc_

    def build(self):
        return np.ascontiguousarray(np.concatenate(self.parts, axis=1))


def vec_layout(inp, b):
    vp = VecPack()
    vp.add("c", inp["c"][b])
    for l in range(2):
        vp.add("ada_b%d" % l, inp["ada_b"][l])
        vp.add("g_mix%d" % l, inp["norm_mix"][l])
        vp.add("g_ffn%d" % l, inp["norm_ffn"][l])
    vp.add("g_fin", inp["final_norm"])
    vp.add("qn", np.tile(inp["attn_q_norm"][0], 2))
    vp.add("kn", np.tile(inp["attn_k_norm"][0], 2))
    mu = inp["rwkv_mu"][0]
    vp.add("mu_rkv", mu[:1536])
    vp.add("mu_w", mu[1536:1664])
    vp.add("mu_a", mu[1664:1792])
    vp.add("mu_g", mu[1792:1952])
    vp.add("w0", inp["rwkv_w0"][0])
    vp.add("a0", inp["rwkv_a0"][0])
    vp.add("k_k", inp["rwkv_k_k"][0])
    vp.add("k_a", inp["rwkv_k_a"][0])
    vp.add("r_k", inp["rwkv_r_k"][0])
    vp.add("ln_g", inp["rwkv_ln_g"][0])
    vp.add("ln_b", inp["rwkv_ln_b"][0])
    cw = inp["hy_conv_w"][0]
    for j in range(3):
        vp.add("cw%d" % j, cw[j])
    vp.add("cb", inp["hy_conv_b"][0])
    vp.add("f_b1", inp["hy_f_b1"][0])
    vp.add("f_b2", inp["hy_f_b2"][0])
    vp.add("f_b3", inp["hy_f_b3"][0])
    vp.add("f_fr", inp["hy_sin_freq"][0])
    return vp


def host_consts():
    bf = ml_dtypes.bfloat16
    c = {}
    c["ident"] = np.eye(128, dtype=np.float32).astype(bf)
    c["ones"] = np.ones((128, 128), np.float32).astype(bf)
    bd = np.zeros((128, 128), np.float32)
    bd[:64, :64] = 1
    bd[64:, 64:] = 1
    c["bd"] = bd.astype(bf)
    rm = np.zeros((128, 128), np.float32)
    for i in range(64):
        rm[2 * i + 1, 2 * i] = -1.0
        rm[2 * i, 2 * i + 1] = 1.0
    c["rm"] = rm
    t = np.arange(T)
    row = (t // 64).astype(np.float32)
    col = (t % 64).astype(np.float32)
    inv = (10000.0 ** (-np.arange(0, 32, 2, dtype=np.float32) / 32)).astype(np.float32)
    ang = np.concatenate([row[:, None] * inv, col[:, None] * inv], -1)
    p = np.arange(128)
    fi = (p % 64) // 2
    c["cos"] = np.cos(ang)[:, fi].T.astype(np.float32).copy()
    c["sin"] = np.sin(ang)[:, fi].T.astype(np.float32).copy()
    j = np.arange(128)
    same = (j[:, None] // 64) == (j[None, :] // 64)
    c["m_strict"] = (same & (j[:, None] < j[None, :])).astype(np.float32)
    c["m_incl"] = (same & (j[:, None] <= j[None, :])).astype(np.float32)
    c["m4"] = np.ascontiguousarray(np.concatenate([c["m_strict"], c["m_incl"], c["m_strict"], c["m_incl"]], 1))
    c["m_strictT"] = np.ascontiguousarray(c["m_strict"].T)
    c["bd32"] = bd.copy()
    N = 2 * T
    f = np.arange(T, dtype=np.float64)[:, None]
    tt = np.arange(T, dtype=np.float64)[None, :]
    ang = 2 * np.pi * (f + 0.5) * tt / N
    C = np.cos(ang)
    S = -np.sin(ang)
    perm = np.concatenate([np.arange(0, T, 2), np.arange(1, T, 2)])
    Ch = C[:T // 2][:, perm].astype(np.float32)
    Sh = S[:T // 2][:, perm].astype(np.float32)

    def tile_ft(M):
        A = M.reshape(2, 4, 128, 4, 512)
        return np.ascontiguousarray(A.transpose(3, 0, 2, 1, 4)).reshape(4, 2, 128, 2048)

    def tile_tf(Mt):
        A = Mt.reshape(2, 8, 128, 8, 128)
        return np.ascontiguousarray(A.transpose(3, 0, 2, 1, 4)).reshape(8, 2, 128, 1024)
    c["Cft"] = tile_ft(Ch).astype(bf)
    c["Sft"] = tile_ft(Sh).astype(bf)
    c["Ctf"] = tile_tf(np.ascontiguousarray(Ch.T)).astype(bf)
    c["Stf"] = tile_tf(np.ascontiguousarray(Sh.T)).astype(bf)
    tl = np.linspace(0.0, 1.0, T, dtype=np.float32)[:, None]
    om = (2.0 * np.pi / T) * np.arange(T, dtype=np.float32)[:, None]
    bands = np.linspace(1e-4, 15, 16, dtype=np.float32)[None, :]
    feats = np.concatenate([tl, np.cos(bands * om), -np.sin(bands * om)], -1).astype(np.float32)
    c["featsT"] = np.ascontiguousarray(feats.T[:, perm])
    mn = np.log(1e-2) / 1.5
    mx = np.log(1e-2) / 0.3
    deltas = np.abs(np.linspace(mn, mx, D, dtype=np.float32))
    c["window"] = np.ascontiguousarray(np.exp(-tl * deltas).astype(np.float32)[perm])
    return c


CONST_DT = {"ident": BF16, "ones": BF16, "bd": BF16, "Cft": BF16, "Sft": BF16, "Ctf": BF16, "Stf": BF16}


class KB:
    def __init__(self, vcols, cfg):
        self.cfg = cfg
        nc = bass.Bass("TRN2", target_bir_lowering=False)
        self.nc = nc
        P = Prog(nc)
        self.P = P
        self.vcols = vcols
        self.sb = Arena(P, "sb", 206 * 1024)
        self.ps = Arena(P, "ps", 16 * 1024, psum=True)
        self.bank = [self.ps.at(i * 2048, [512], F32) for i in range(8)]
        self.din = {}
        self.pool_rr = 0

    def inp(self, name, shape, dtype=F32):
        b = self.P.dram_in(name, shape, dtype)
        self.din[name] = b
        return b

    def vcol(self, name, j=0, n=1, p=128):
        o, nc_ = self.vcols[name]
        assert j + n <= nc_, name
        return self.vecs[0:p, o + j:o + j + n]

    def wview(self, W, lead, k0, nk, c0, ncols):
        full = W.full
        if lead is not None:
            full = full[lead]
        ap = full[k0 * 128:(k0 + nk) * 128, c0:c0 + ncols].rearrange("(kc p) c -> p kc c", p=128)
        return self.P.dview(W, ap)


def build_program(vcols, nvec, cfg):
    K = KB(vcols, cfg)
    P, nc, sb = K.P, K.nc, K.sb
    bank = K.bank
    xT = K.inp("xT", [D, T])
    vecs_d = K.inp("vecs", [128, nvec])
    ada_w = K.inp("ada_w", [2, D, 6 * D])
    ffn_w1 = K.inp("ffn_w1", [2, D, DFF])
    ffn_w3 = K.inp("ffn_w3", [2, D, DFF])
    ffn_w2 = K.inp("ffn_w2", [2, DFF, D])
    cst = {}
    for nm, shp in (("ident", [128, 128]), ("ones", [128, 128]), ("bd", [128, 128]), ("rm", [128, 128]),
                    ("cos", [128, T]), ("sin", [128, T]), ("m_strict", [128, 128]), ("m_incl", [128, 128]),
                    ("Cft", [4, 2, 128, 2048]), ("Sft", [4, 2, 128, 2048]),
                    ("Ctf", [8, 2, 128, 1024]), ("Stf", [8, 2, 128, 1024]),
                    ("featsT", [33, T]), ("window", [T, D])):
        cst[nm] = K.inp("c_" + nm, shp, CONST_DT.get(nm, F32))
    K.cst = cst
    outT = P.dram_out("outT", [D, T])
    K.outT = outT
    X = sb.alloc([8, T], F32)
    hT = sb.alloc([8, T], BF16)
    vecs = sb.alloc([nvec], F32)
    K.X, K.hT, K.vecs = X, hT, vecs
    ident = sb.alloc([128], BF16)
    ones = sb.alloc([128], BF16)
    bd = sb.alloc([128], BF16)
    K.ident, K.ones, K.bd = ident, ones, bd
    cond = sb.alloc([8], F32)
    mod = sb.alloc([96], F32)
    am = sb.alloc([2, 8], F32)
    af = sb.alloc([2, 8], F32)
    K.mod, K.am, K.af = mod, am, af
    K.base_mark = sb.mark()

    P.dma("sp", vecs[:], vecs_d[:])
    P.dma("act", ident[:], cst["ident"][:])
    P.dma("act", ones[:], cst["ones"][:])
    P.dma("act", bd[:], cst["bd"][:])
    for kc in range(8):
        P.dma("sp" if kc % 2 == 0 else "act", X[:, kc, :], xT[kc * 128:(kc + 1) * 128, :])
    P.actf(cond[:], K.vcol("c", 0, 8), AF.Silu)

    def ada_phase(l):
        m = sb.mark()
        wt = [sb.alloc([8, 512], F32) for _ in range(2)]
        pm = bank[7]
        for jg in range(12):
            w = wt[jg % 2]
            P.dma("sp" if jg % 2 == 0 else "act", w[:], K.wview(ada_w, l, 0, 8, jg * 512, 512))
            for j in range(4):
                col = jg * 4 + j
                for kc in range(8):
                    P.mm(pm[:, col:col + 1], w[:, kc, j * 128:(j + 1) * 128], cond[:, kc:kc + 1],
                         start=(kc == 0), stop=(kc == 7))
        P.tt("dve", mod[:, l * 48:(l + 1) * 48], pm[:, 0:48], K.vcol("ada_b%d" % l, 0, 48), ALU.add)
        P.stt(am[:, l, :], mod[:, l * 48 + 8:l * 48 + 16], 1.0, K.vcol("g_mix%d" % l, 0, 8), ALU.add, ALU.mult)
        P.stt(af[:, l, :], mod[:, l * 48 + 32:l * 48 + 40], 1.0, K.vcol("g_ffn%d" % l, 0, 8), ALU.add, ALU.mult)
        sb.release(m)

    def rstd_block(tb, psb, rs):
        m = sb.mark()
        sq = [sb.alloc([512], BF16) for _ in range(3)]
        for kc in range(8):
            s = sq[kc % 3]
            if kc % 2 == 0:
                P.actf(s[:], X[:, kc, tb * 512:(tb + 1) * 512], AF.Square)
            else:
                P.tt("pool", s[:], X[:, kc, tb * 512:(tb + 1) * 512], X[:, kc, tb * 512:(tb + 1) * 512], ALU.mult)
            P.mm(psb[:], ones[:], s[:], start=(kc == 0), stop=(kc == 7))
        P.actf(rs[:], psb[:], AF.Sqrt, bias=K.epsc[:], scale=1.0 / D)
        P.recip(rs[:], rs[:])
        sb.release(m)

    epsc = sb.alloc([1], F32)
    K.epsc = epsc
    P.memset("dve", epsc[:], EPS)
    gnc = sb.alloc([1], F32)
    K.gnc = gnc
    P.memset("dve", gnc[:], 64e-5)
    K.base_mark = sb.mark()

    def norm_phase(a_cols, shift_cols):
        m = sb.mark()
        rs = [sb.alloc([512], F32) for _ in range(2)]
        tmp = [sb.alloc([512], F32) for _ in range(3)]
        for tb in range(4):
            r = rs[tb % 2]
            rstd_block(tb, bank[tb % 2], r)
            for kc in range(8):
                tm = tmp[kc % 3]
                P.stt(tm[:], X[:, kc, tb * 512:(tb + 1) * 512], a_cols[kc], r[:], ALU.mult, ALU.mult)
                P.actf(hT[:, kc, tb * 512:(tb + 1) * 512], tm[:], AF.Identity, bias=shift_cols[kc])
        sb.release(m)

    def ffn_phase(l):
        m = sb.mark()
        groups = [(g * 512, min(512, DFF - g * 512)) for g in range((DFF + 511) // 512)]
        w1t = [sb.alloc([8, 512], BF16) for _ in range(2)]
        w3t = [sb.alloc([8, 512], BF16) for _ in range(2)]
        w2t = [sb.alloc([4, 1024], BF16) for _ in range(2)]
        ug = sb.alloc([4, T], BF16)
        sil = [sb.alloc([512], F32) for _ in range(2)]
        gate = [mod[:, l * 48 + 40 + kc:l * 48 + 41 + kc] for kc in range(8)]

        def load(g):
            c0, w = groups[g]
            P.dma("pool", w1t[g % 2][:, :, 0:w], K.wview(ffn_w1, l, 0, 8, c0, w))
            P.dma("pool", w3t[g % 2][:, :, 0:w], K.wview(ffn_w3, l, 0, 8, c0, w))
            P.dma("pool", w2t[g % 2][:, 0:w // 128, :], K.wview(ffn_w2, l, c0 // 128, w // 128, 0, D))

        load(0)
        n = 0
        for g in range(len(groups)):
            if g + 1 < len(groups):
                load(g + 1)
            c0, w = groups[g]
            nfc = w // 128
            a, b, c2 = w1t[g % 2], w3t[g % 2], w2t[g % 2]
            for fc in range(nfc):
                for tb in range(4):
                    p1 = bank[(n % 2) * 2]
                    p3 = bank[(n % 2) * 2 + 1]
                    s = sil[n % 2]
                    n += 1
                    for kc in range(8):
                        P.mm(p1[:], a[:, kc, fc * 128:(fc + 1) * 128], hT[:, kc, tb * 512:(tb + 1) * 512],
                             start=(kc == 0), stop=(kc == 7))
                    for kc in range(8):
                        P.mm(p3[:], b[:, kc, fc * 128:(fc + 1) * 128], hT[:, kc, tb * 512:(tb + 1) * 512],
                             start=(kc == 0), stop=(kc == 7))
                    P.actf(s[:], p1[:], AF.Silu)
                    P.tt("dve", ug[:, fc, tb * 512:(tb + 1) * 512], s[:], p3[:], ALU.mult)
            k = 0
            for oc in range(8):
                for tb in range(4):
                    po = bank[4 + (k % 4)]
                    k += 1
                    for fc in range(nfc):
                        P.mm(po[:], c2[:, fc, oc * 128:(oc + 1) * 128], ug[:, fc, tb * 512:(tb + 1) * 512],
                             start=(fc == 0), stop=(fc == nfc - 1))
                    xs = X[:, oc, tb * 512:(tb + 1) * 512]
                    P.stt(xs, po[:], gate[oc], xs, ALU.mult, ALU.add)
        sb.release(m)

    def final_phase():
        m = sb.mark()
        rs = [sb.alloc([512], F32) for _ in range(2)]
        ot = [sb.alloc([512], F32) for _ in range(4)]
        n = 0
        for tb in range(4):
            r = rs[tb % 2]
            rstd_block(tb, bank[tb % 2], r)
            for kc in range(8):
                o = ot[n % 4]
                n += 1
                P.stt(o[:], X[:, kc, tb * 512:(tb + 1) * 512], K.vcol("g_fin", kc), r[:], ALU.mult, ALU.mult)
                P.dma("sp" if n % 2 == 0 else "act", outT[kc * 128:(kc + 1) * 128, tb * 512:(tb + 1) * 512], o[:])
        sb.release(m)

    def dump_x():
        for kc in range(8):
            P.dma("sp", outT[kc * 128:(kc + 1) * 128, :], X[:, kc, :])

    K.norm_phase = norm_phase
    ada_phase(0)
    stop = cfg.get("stop")
    for l in range(2):
        if l == 1:
            ada_phase(1)
        if cfg.get("mix%d" % l, True):
            norm_phase([am[:, l, kc:kc + 1] for kc in range(8)], [mod[:, l * 48 + kc:l * 48 + kc + 1] for kc in range(8)])
            if l == 0:
                mixer0_phase(K)
            else:
                mixer1_phase(K)
        if stop == "mix%d" % l:
            dump_x()
            break
        if cfg.get("ffn%d" % l, True):
            norm_phase([af[:, l, kc:kc + 1] for kc in range(8)], [mod[:, l * 48 + 24 + kc:l * 48 + 25 + kc] for kc in range(8)])
            ffn_phase(l)
        if stop == "ffn%d" % l:
            dump_x()
            break
    else:
        final_phase()
    P.emit()
    return K


def mixer0_phase(K):
    P, sb, bank, X, hT, cst = K.P, K.sb, K.bank, K.X, K.hT, K.cst
    cfg = K.cfg
    mix_w_in = K.inp("mix_w_in", [1, D, 2720])
    mix_w_out = K.inp("mix_w_out", [1, D, D])
    K.mix_w_in = mix_w_in
    m = sb.mark()
    yA = sb.alloc([4, T], BF16)
    K.yA = yA
    if cfg.get("attn", True):
        attention_phase(K)
    else:
        P.memset("pool", yA[:], 0.0)
    yR = sb.alloc([4, T], BF16)
    K.yR = yR
    if cfg.get("rwkv", True):
        rwkv_phase(K)
    else:
        P.memset("pool", yR[:], 0.0)
    m2 = sb.mark()
    wt = [sb.alloc([8, 512], BF16) for _ in range(2)]
    for og in range(2):
        P.dma("pool", wt[og][:], K.wview(mix_w_out, 0, 0, 8, og * 512, 512))
    n = 0
    for og in range(2):
        for oc in range(4):
            for tb in range(4):
                po = bank[n % 4]
                n += 1
                for kc in range(8):
                    src = yA if kc < 4 else yR
                    P.mm(po[:], wt[og][:, kc, oc * 128:(oc + 1) * 128], src[:, kc % 4, tb * 512:(tb + 1) * 512],
                         start=(kc == 0), stop=(kc == 7))
                xs = X[:, og * 4 + oc, tb * 512:(tb + 1) * 512]
                P.stt(xs, po[:], K.mod[:, 16 + og * 4 + oc:17 + og * 4 + oc], xs, ALU.mult, ALU.add)
    sb.release(m)


def attention_phase(K):
    P, sb, bank, X, hT, cst, yT = K.P, K.sb, K.bank, K.X, K.hT, K.cst, K.yA
    mix_w_in = K.mix_w_in
    m = sb.mark()
    qT = sb.alloc([4, T], BF16)
    kT = sb.alloc([2, T], BF16)
    vaug = sb.alloc([16, 2, 128], BF16)
    m1 = sb.mark()
    cos = sb.alloc([T], F32)
    sin = sb.alloc([T], F32)
    rm = sb.alloc([128], F32)
    P.dma("sp", cos[:], cst["cos"][:])
    P.dma("act", sin[:], cst["sin"][:])
    P.dma("sp", rm[:], cst["rm"][:])
    P.memset("pool", vaug[:, :, :, 64:128], 1.0)
    wq = [sb.alloc([8, 128], BF16) for _ in range(3)]
    sqb = [sb.alloc([512], BF16) for _ in range(2)]
    rsb = [sb.alloc([512], F32) for _ in range(2)]
    qnb = [sb.alloc([512], F32) for _ in range(2)]
    t1b = [sb.alloc([512], F32) for _ in range(2)]
    t2b = [sb.alloc([512], F32) for _ in range(2)]
    chunks = [("q", c) for c in range(4)] + [("k", g) for g in range(2)]
    n = 0
    for ci, (kind, idx) in enumerate(chunks):
        w = wq[ci % 3]
        if kind == "q":
            P.dma("pool", w[:], K.wview(mix_w_in, 0, 0, 8, idx * 128, 128))
            gain = K.vcol("qn")
            dst = qT
        else:
            P.dma("pool", w[:, :, 0:64], K.wview(mix_w_in, 0, 0, 8, 512 + idx * 64, 64))
            P.dma("pool", w[:, :, 64:128], K.wview(mix_w_in, 0, 0, 8, 512 + idx * 64, 64))
            gain = K.vcol("kn")
            dst = kT
        for tb in range(4):
            ts_ = slice(tb * 512, (tb + 1) * 512)
            pa = bank[(n % 2) * 3]
            pb = bank[(n % 2) * 3 + 1]
            pc = bank[(n % 2) * 3 + 2]
            sq, rs, qn, t1, t2 = sqb[n % 2], rsb[n % 2], qnb[n % 2], t1b[n % 2], t2b[n % 2]
            n += 1
            for kc in range(8):
                P.mm(pa[:], w[:, kc, :], hT[:, kc, ts_], start=(kc == 0), stop=(kc == 7))
            P.actf(sq[:], pa[:], AF.Square)
            P.mm(pb[:], K.bd[:], sq[:])
            P.actf(rs[:], pb[:], AF.Sqrt, bias=K.epsc[:], scale=1.0 / 64)
            P.recip(rs[:], rs[:])
            P.stt(qn[:], pa[:], gain, rs[:], ALU.mult, ALU.mult)
            P.mm(pc[:], rm[:], qn[:])
            P.tt("dve", t1[:], qn[:], cos[:, ts_], ALU.mult)
            P.tt("dve", t2[:], pc[:], sin[:, ts_], ALU.mult)
            P.tt("pool", dst[:, idx, ts_], t1[:], t2[:], ALU.add)
    wv = wq[0]
    P.dma("pool", wv[:], K.wview(mix_w_in, 0, 0, 8, 640, 128))
    for tc in range(16):
        pa = bank[6 + (tc % 2)]
        for kc in range(8):
            P.mm(pa[:, 0:128], hT[:, kc, tc * 128:(tc + 1) * 128], wv[:, kc, :], start=(kc == 0), stop=(kc == 7))
        for g in range(2):
            P.copy("act", vaug[:, tc, g, 0:64], pa[:, g * 64:(g + 1) * 64])
    sb.release(m1)
    NPB = 3
    pbuf = [sb.alloc([2, 512], BF16) for _ in range(NPB)]
    rec = [sb.alloc([512], F32) for _ in range(2)]
    prs = [(0, 1), (2, 3), (6, 7)]
    pview = [K.ps.at(a * 2048, [2, 512], F32) for a, _ in prs]
    items = [(h, qb, k2) for h in range(8) for qb in range(4) for k2 in range(8)]
    SK = 2
    for i in range(len(items) + SK):
        if i < len(items):
            h, qb, k2 = items[i]
            c, par, g = h // 2, h % 2, h // 4
            pr = slice(par * 64, par * 64 + 64)
            qs = slice(qb * 512, (qb + 1) * 512)
            pv = pview[i % 3]
            pe = pbuf[i % NPB]
            for j in range(2):
                kc = 2 * k2 + j
                P.mm(pv[:, j, :], kT[pr, g, kc * 128:(kc + 1) * 128], qT[pr, c, qs])
            P.actf(V(pe.full.rearrange("p a b -> p (a b)"), pe[:].rect),
                   V(pv.full.rearrange("p a b -> p (a b)"), pv[:].rect), AF.Exp, scale=0.125)
        if i >= SK:
            h, qb, k2 = items[i - SK]
            c, par, g = h // 2, h % 2, h // 4
            pr = slice(par * 64, par * 64 + 64)
            qs = slice(qb * 512, (qb + 1) * 512)
            n1 = (i - SK) // 8
            po = bank[4 + (n1 % 2)]
            rc = rec[n1 % 2]
            pe = pbuf[(i - SK) % NPB]
            for j in range(2):
                kc = 2 * k2 + j
                P.mm(po[:], vaug[:, kc, g, :], pe[:, j, :], start=(kc == 0), stop=(kc == 15))
            if k2 == 7:
                P.recip(rc[0:64], po[64:128])
                P.tt("dve", yT[pr, c, qs], po[0:64], rc[0:64], ALU.mult)
    sb.release(m)


C_DEC = 0.6065306597126334
GN_EPS = 64e-5


class Sub:
    def __init__(self, arena, b0, nbytes):
        self.a, self.b0, self.top, self.end = arena, b0, b0, b0 + nbytes

    def alloc(self, shape, dtype, p=128):
        n = int(np.prod(shape)) * DSZ[dtype]
        n = (n + 3) // 4 * 4
        assert self.top + n <= self.end, "sub overflow"
        b = self.a.at(self.top, shape, dtype, p)
        self.top += n
        return b


def rev(v):
    return V(v.ap[:, ::-1], v.rect)


def rwkv_phase(K):
    P, sb, bank, ps, X, hT, cst = K.P, K.sb, K.bank, K.ps, K.X, K.hT, K.cst
    w_in = K.mix_w_in
    w_up_d = K.inp("rwkv_w_up", [1, 2, 64, 512])
    a_up_d = K.inp("rwkv_a_up", [1, 2, 64, 512])
    g_up_d = K.inp("rwkv_g_up", [1, 160, 512])
    m4_d = K.inp("c_m4", [128, 512])
    mT_d = K.inp("c_m_strictT", [128, 128])
    bd32_d = K.inp("c_bd32", [128, 128])
    xsp = P.dram_scratch("xspill", [128, 8, T], F32)
    for kc in range(8):
        P.dma("sp" if kc % 2 == 0 else "act", xsp[:, kc, :], X[:, kc, :])
    xa = Sub(sb, X.byte0, 8 * T * 4)
    m = sb.mark()
    tw = sb.alloc([T], BF16)
    al = sb.alloc([T], BF16)
    sg = sb.alloc([2, T], BF16)
    g_up = sb.alloc([2, 512], BF16)
    a_up = sb.alloc([512], BF16)
    w_up = sb.alloc([512], BF16)
    bd32 = sb.alloc([128], F32)
    m4 = sb.alloc([4, 128], F32)
    mT = sb.alloc([128], F32)
    mus = sb.alloc([2, 16], F32)
    omka = sb.alloc([4], F32)
    w0c = sb.alloc([8], F32)
    P.dma("pool", g_up[:, 0, :], P.dview(g_up_d, g_up_d.full[0, 0:128, :]))
    P.dma("pool", g_up[0:32, 1, :], P.dview(g_up_d, g_up_d.full[0, 128:160, :]))
    P.dma("pool", a_up[:], P.dview(a_up_d, a_up_d.full[0].rearrange("d r c -> (d r) c")))
    P.dma("pool", w_up[:], P.dview(w_up_d, w_up_d.full[0].rearrange("d r c -> (d r) c")))
    P.dma("sp", bd32[:], bd32_d[:])
    P.dma("sp", V(m4.full.rearrange("p a b -> p (a b)"), m4[:].rect), m4_d[:])
    P.dma("sp", mT[:], mT_d[:])
    mu_names = [("mu_rkv", j) for j in range(12)] + [("mu_w", 0), ("mu_a", 0), ("mu_g", 0), ("mu_g", 1)]
    for i, (nm, j) in enumerate(mu_names):
        P.ts("dve", mus[:, 0, i:i + 1], K.vcol(nm, j), 0.5, ALU.mult)
        P.ts("dve", mus[:, 1, i:i + 1], K.vcol(nm, j), -1.0, ALU.mult, 1.0, ALU.add)
    P.ts("dve", omka[:], K.vcol("k_a", 0, 4), -1.0, ALU.mult, 1.0, ALU.add)

    cnt = {"n": 0}

    PS = {"sets": None, "i": 0}

    def alloc_proj_sets():
        sets = []
        for _ in range(2):
            pre = sb.alloc([T + 2], F32)
            t1 = sb.alloc([T], F32)
            w = sb.alloc([8, 128], BF16)
            P.memset("pool", pre[:, 0:1], 0.0)
            P.memset("pool", pre[:, T + 1:T + 2], 0.0)
            sets.append((pre, t1, w))
        PS["sets"] = sets

    def proj_shift(col0, ncols, mui, dst_fn, np_=128):
        pre, t1, w = PS["sets"][PS["i"] % 2]
        PS["i"] += 1
        P.dma("pool", w[:, :, 0:ncols], K.wview(w_in, 0, 0, 8, col0, ncols))
        for tb in range(4):
            pb = bank[cnt["n"] % 2]
            cnt["n"] += 1
            for kc in range(8):
                P.mm(pb[0:ncols, :], w[:, kc, 0:ncols], hT[:, kc, tb * 512:(tb + 1) * 512], start=(kc == 0), stop=(kc == 7))
            P.copy("act", pre[0:ncols, 1 + tb * 512:1 + (tb + 1) * 512], pb[0:ncols, :])
        q = slice(0, ncols)
        P.tt("pool", t1[q], pre[q, 0:T], pre[q, 2:T + 2], ALU.add)
        P.actf(pre[q, 1:T + 1], pre[q, 1:T + 1], AF.Identity, scale=mus[q, 1, mui:mui + 1])
        P.stt(t1[q], t1[q], mus[q, 0, mui:mui + 1], pre[q, 1:T + 1], ALU.mult, ALU.add)
        dst_fn(t1)

    mps = sb.mark()
    alloc_proj_sets()
    proj_shift(768 + 1536, 128, 12, lambda t: P.actf(tw[:], t[:], AF.Tanh))
    proj_shift(768 + 1664, 128, 13, lambda t: P.copy("act", al[:], t[:]))
    proj_shift(768 + 1792, 128, 14, lambda t: P.actf(sg[:, 0, :], t[:], AF.Sigmoid))
    proj_shift(768 + 1920, 32, 15, lambda t: P.actf(sg[0:32, 1, :], t[0:32], AF.Sigmoid))
    sb.release(mps)

    class BB:
        pass

    def alloc_bundle(al):
        b = BB()
        b.Mm = al.alloc([8, 4, 128], BF16)
        b.NT = al.alloc([8, 128], BF16)
        b.Ab = al.alloc([8, 128], BF16)
        b.ATb = al.alloc([8, 128], BF16)
        b.Pm = al.alloc([8, 128], BF16)
        b.ZR = al.alloc([2, 512], BF16)
        b.Kt = al.alloc([512], BF16)
        b.Bt = al.alloc([512], BF16)
        b.Kh = al.alloc([512], BF16)
        b.Bh = al.alloc([512], BF16)
        b.Vt = al.alloc([512], BF16)
        b.KhT = al.alloc([4, 128], BF16)
        b.BhT = al.alloc([4, 128], BF16)
        b.VT = al.alloc([4, 128], BF16)
        b.VT2 = al.alloc([4, 2, 128], BF16)
        return b

    def alloc_state(al):
        s_ = BB()
        s_.wtot = al.alloc([8], F32)
        s_.ST = al.alloc([128], F32)
        s_.STb = al.alloc([128], BF16)
        s_.XT = al.alloc([128], BF16)
        s_.UT = al.alloc([128], BF16)
        s_.UT2 = al.alloc([2, 128], BF16)
        s_.stmp = al.alloc([128], F32)
        return s_

    r_b = xa.alloc([T], BF16)
    k_b = xa.alloc([T], BF16)
    v_b = xa.alloc([T], BF16)
    kkn = xa.alloc([T], BF16)
    Yacc = xa.alloc([T], F32)
    B0 = alloc_bundle(xa)
    St = [alloc_state(xa), alloc_state(xa)]
    fA1 = xa.alloc([512], F32)
    fL1 = xa.alloc([8, 64], F32)
    fP1 = xa.alloc([8, 64], F32)
    Mm, NT, Ab, ATb = B0.Mm, B0.NT, B0.Ab, B0.ATb
    psT = [ps.at(6 * 2048, [4, 128], BF16), ps.at(7 * 2048, [4, 128], BF16)]
    P.memset("pool", B0.VT2[:], 0.0)
    for s_ in St:
        P.memset("pool", s_.UT2[:], 0.0)
    nT = {"n": 0}

    def tposes(src, dsts):
        pt = psT[nT["n"] % 2]
        nT["n"] += 1
        for j in range(4):
            P.transpose(pt[:, j, :], src[:, j * 128:(j + 1) * 128], K.ident[:])
        for eng, dv, sf in dsts:
            P.copy(eng, dv, sf(pt))

    def flat(b):
        return V(b.full.rearrange("p a b -> p (a b)"), b[:].rect)

    for fc in range(K.cfg.get("rw_fcs", 4)):
        mps = sb.mark()
        alloc_proj_sets()
        proj_shift(768 + fc * 128, 128, fc, lambda t: P.copy("act", r_b[:], t[:]))
        m1 = sb.mark()
        kk32 = sb.at(Mm.byte0, [T], F32)
        sqb = [sb.at(NT.byte0, [512], BF16), sb.at(NT.byte0 + 1024, [512], BF16)]
        rsb = [sb.at(Ab.byte0, [512], F32), sb.at(ATb.byte0, [512], F32)]

        def kdst(t):
            P.copy("act", k_b[:], t[:])
            P.ts("dve", kk32[:], t[:], K.vcol("k_k", fc), ALU.mult)
        proj_shift(768 + 512 + fc * 128, 128, 4 + fc, kdst)
        for tb in range(4):
            ts_ = slice(tb * 512, (tb + 1) * 512)
            pb = bank[2 + tb % 2]
            P.actf(sqb[tb % 2][:], kk32[:, ts_], AF.Square)
            P.mm(pb[:], K.bd[:], sqb[tb % 2][:])
            P.ts("dve", rsb[tb % 2][:], pb[:], 1e-24, ALU.max)
            P.actf(rsb[tb % 2][:], rsb[tb % 2][:], AF.Sqrt)
            P.recip(rsb[tb % 2][:], rsb[tb % 2][:])
            P.tt("dve", kkn[:, ts_], kk32[:, ts_], rsb[tb % 2][:], ALU.mult)
        sb.release(m1)
        proj_shift(768 + 1024 + fc * 128, 128, 8 + fc, lambda t: P.copy("act", v_b[:], t[:]))
        sb.release(mps)

        mF = sb.mark()
        fA = sb.alloc([512], F32)
        fL = sb.alloc([8, 64], F32)
        fP = sb.alloc([8, 64], F32)
        fQ = sb.alloc([8, 64], F32)
        fE = [sb.alloc([512], F32) for _ in range(2)]
        fKd = sb.alloc([512], F32)
        fBb = sb.alloc([512], F32)
        al_l = sb.alloc([512], BF16)
        tw_l = sb.alloc([512], BF16)
        B1 = alloc_bundle(sb)
        P.memset("pool", B1.VT2[:], 0.0)
        BBs = [B0, B1]
        for d in range(2):
            P.memset("dve", St[d].ST[:], 0.0)
            P.memset("dve", St[d].STb[:], 0.0)

        def prep_elem(d, tb, fA0=fA, fL0=fL, fP0=fP):
            B, S = BBs[d], St[d]
            fA, fL, fP = (fA0, fL0, fP0) if d == 0 else (fA1, fL1, fP1)
            dsl = slice(d * 64, d * 64 + 64)
            if d == 0:
                tsl = slice(tb * 512, (tb + 1) * 512)
                S_ = lambda b, q=slice(None): b[q, tsl]
            else:
                tsl = slice((3 - tb) * 512, (4 - tb) * 512)
                S_ = lambda b, q=slice(None): rev(b[q, tsl])
            pa, pw = bank[0], bank[1]
            P.copy("dve", al_l[dsl], S_(al, dsl))
            P.copy("dve", tw_l[dsl], S_(tw, dsl))
            P.mm(pa[:], a_up[dsl, fc * 128:(fc + 1) * 128], al_l[dsl])
            P.mm(pw[:], w_up[dsl, fc * 128:(fc + 1) * 128], tw_l[dsl])
            P.actf(fA[:], pa[:], AF.Sigmoid, bias=K.vcol("a0", d * 4 + fc))
            P.actf(flat(fL), pw[:], AF.Sigmoid, bias=K.vcol("w0", d * 4 + fc))
            for c in range(8):
                P.op("dve", lambda e, c=c: K.nc.vector.tensor_tensor_scan(out=fP[:, c, :].ap, data0=fL[:, c, :].ap, data1=fL[:, c, :].ap,
                                                                            initial=0.0, op0=ALU.add, op1=ALU.bypass),
                     [fL[:, c, :]], [fP[:, c, :]])
            tot_b = V(fP.full[:, :, 63:64].broadcast_to([128, 8, 64]), fP[:].rect)
            P.tt("dve", fQ[:], tot_b, fP[:], ALU.subtract)
            P.tt("dve", fL[:], fP[:], fL[:], ALU.subtract)
            P.actf(S.wtot[:], V(fP.full[:, :, 63], fP[:].rect), AF.Exp, scale=-C_DEC)
            P.ts("dve", fKd[:], fA[:], K.vcol("k_a", fc), ALU.mult, omka[:, fc:fc + 1], ALU.add)
            P.tt("dve", fKd[:], fKd[:], S_(k_b), ALU.mult)
            P.tt("dve", fBb[:], fA[:], S_(kkn), ALU.mult)
            E0, E1 = fE
            P.actf(E0[:], flat(fL), AF.Exp, scale=-C_DEC)
            P.stt(B.ZR[:, 0, :], S_(kkn), -1.0, E0[:], ALU.mult, ALU.mult)
            P.actf(E1[:], flat(fP), AF.Exp, scale=-C_DEC)
            P.tt("dve", B.ZR[:, 1, :], S_(r_b), E1[:], ALU.mult)
            P.actf(E0[:], flat(fP), AF.Exp, scale=C_DEC)
            P.tt("dve", B.Kt[:], fKd[:], E0[:], ALU.mult)
            P.tt("dve", B.Bt[:], fBb[:], E0[:], ALU.mult)
            P.actf(E1[:], flat(fQ), AF.Exp, scale=-C_DEC)
            P.tt("dve", B.Kh[:], fKd[:], E1[:], ALU.mult)
            P.tt("dve", B.Bh[:], fBb[:], E1[:], ALU.mult)
            P.copy("dve", B.Vt[:], S_(v_b))
            tposes(B.Kh, [("act", B.KhT[:], lambda pt: pt[:])])
            tposes(B.Bh, [("act", B.BhT[:], lambda pt: pt[:])])
            tposes(B.Vt, [("act", B.VT[:], lambda pt: pt[:]),
                          ("act", V(B.VT2.full[:, :, 0, 0:64], B.VT2[:].rect), lambda pt: V(pt.full[:, :, 0:64], pt[:].rect)),
                          ("act", V(B.VT2.full[:, :, 1, 64:128], B.VT2[:].rect), lambda pt: V(pt.full[:, :, 64:128], pt[:].rect))])

        def m_stage():
            for pb_ in range(4):
                bs = slice(pb_ * 128, (pb_ + 1) * 128)
                for hd in range(2):
                    pr = slice(hd * 64, hd * 64 + 64)
                    i8 = pb_ * 2 + hd
                    for d in range(2):
                        B = BBs[d]
                        pm_ = bank[2 + d]
                        pn = bank[4 + d]
                        rhs = V(B.ZR.full[pr, :, bs], B.ZR[pr].rect)
                        P.mm(pm_[:, 0:256], B.Bt[pr, bs], rhs)
                        P.mm(pm_[:, 256:512], B.Kt[pr, bs], rhs)
                        P.mm(pn[:, 0:128], B.ZR[pr, 0, bs], B.Bt[pr, bs])
                    for d in range(2):
                        B = BBs[d]
                        P.tt("dve", V(B.Mm.full[:, i8, :, :].rearrange("p a b -> p (a b)"), B.Mm[:, i8].rect), bank[2 + d][:],
                             V(m4.full.rearrange("p a b -> p (a b)"), m4[:].rect), ALU.mult)
                        P.tt("dve", B.NT[:, i8, :], bank[4 + d][:, 0:128], mT[:], ALU.mult)

        def t_stage():
            for i8 in range(8):
                for d in range(2):
                    B = BBs[d]
                    P.copy("act", B.Ab[:, i8, :], B.Mm[:, i8, 0, :])
                    P.copy("act", B.ATb[:, i8, :], B.NT[:, i8, :])
                    P.tt("dve", B.Pm[:, i8, :], B.Mm[:, i8, 0, :], K.ident[:], ALU.add)
            for lvl in range(5):
                for hf in range(2):
                    q4 = slice(hf * 4, hf * 4 + 4)
                    for d in range(2):
                        B = BBs[d]
                        for i4 in range(4):
                            i8 = hf * 4 + i4
                            cs4 = slice(i4 * 128, (i4 + 1) * 128)
                            P.mm(bank[2 * d][:, cs4], B.ATb[:, i8, :], B.Ab[:, i8, :])
                            P.mm(bank[2 * d + 1][:, cs4], B.Ab[:, i8, :], B.ATb[:, i8, :])
                    for d in range(2):
                        B = BBs[d]
                        P.copy("act", flat4(B.Ab, q4), bank[2 * d][:])
                        P.copy("dve", flat4(B.ATb, q4), bank[2 * d + 1][:])
                    for d in range(2):
                        B = BBs[d]
                        for i4 in range(4):
                            i8 = hf * 4 + i4
                            P.mm(bank[4 + d][:, i4 * 128:(i4 + 1) * 128], B.ATb[:, i8, :], B.Pm[:, i8, :])
                    for d in range(2):
                        B = BBs[d]
                        pv = flat4(B.Pm, q4)
                        P.tt("dve", pv, pv, bank[4 + d][:], ALU.add)

        def flat4(buf, q4):
            return V(buf.full[:, q4, :].rearrange("p a b -> p (a b)"), buf[:, q4].rect)

        def chain_stage(tb):
            CB = [(bank[6], bank[7], bank[0], bank[1]), (bank[2], bank[3], bank[4], bank[5])]
            for c in range(8):
                pb_, e = c // 2, c % 2
                js = slice(e * 64, e * 64 + 64)
                cs = slice(c * 64, (c + 1) * 64)
                for d in range(2):
                    B, S = BBs[d], St[d]
                    px = CB[d][0]
                    P.mm(px[0:64, 0:128], B.ZR[:, 0, cs], S.STb[:], start=True, stop=False)
                    for hd in range(2):
                        i8 = pb_ * 2 + hd
                        hs_ = slice(hd * 64, hd * 64 + 64)
                        P.mm(px[0:64, hs_], B.Mm[js, i8, 2, js], B.VT[js, pb_, hs_], start=False, stop=(hd == 1))
                for d in range(2):
                    P.copy("act", St[d].XT[js], CB[d][0][0:64, 0:128])
                for d in range(2):
                    B, S = BBs[d], St[d]
                    pu = CB[d][1]
                    for hd in range(2):
                        i8 = pb_ * 2 + hd
                        hs_ = slice(hd * 64, hd * 64 + 64)
                        P.mm(pu[0:64, hs_], B.Pm[js, i8, js], S.XT[js, hs_])
                for d in range(2):
                    S = St[d]
                    pu = CB[d][1]
                    P.copy("act", S.UT[js], pu[0:64, 0:128])
                    P.copy("act", V(S.UT2.full[js, 0, 0:64], S.UT2[js].rect), pu[0:64, 0:64])
                    P.copy("act", V(S.UT2.full[js, 1, 64:128], S.UT2[js].rect), pu[0:64, 64:128])
                for d in range(2):
                    B, S = BBs[d], St[d]
                    py, pss = CB[d][2], CB[d][3]
                    P.mm(py[:, 0:64], S.STb[:], B.ZR[:, 1, cs], start=True, stop=False)
                    for hd in range(2):
                        i8 = pb_ * 2 + hd
                        P.mm(py[:, 0:64], S.UT2[js, hd, :], B.Mm[js, i8, 1, js], start=False, stop=False)
                        P.mm(py[:, 0:64], B.VT2[js, pb_, hd, :], B.Mm[js, i8, 3, js], start=False, stop=(hd == 1))
                    P.mm(pss[:, 0:128], B.BhT[js, pb_, :], S.UT[js], start=True, stop=False)
                    P.mm(pss[:, 0:128], B.KhT[js, pb_, :], B.VT[js, pb_, :], start=False, stop=True)
                for d in range(2):
                    S = St[d]
                    pss = CB[d][3]
                    P.tt("dve", S.stmp[:], pss[:, 0:128], bd32[:], ALU.mult)
                    P.stt(S.ST[:], S.ST[:], S.wtot[:, c:c + 1], S.stmp[:], ALU.mult, ALU.add)
                    P.copy("act", S.STb[:], S.ST[:])
                yv0 = Yacc[:, tb * 512 + c * 64:tb * 512 + (c + 1) * 64]
                P.tt("dve", yv0, yv0, CB[0][2][:, 0:64], ALU.add)
                t0 = T - (tb * 512 + (c + 1) * 64)
                yv1 = rev(Yacc[:, t0:t0 + 64])
                P.tt("dve", yv1, yv1, CB[1][2][:, 0:64], ALU.add)

        P.memset("pool", Yacc[:], 0.0)
        v4s = K.cfg.get("v4_stage", 4)
        for tb in range(4):
            for d in range(2):
                prep_elem(d, tb)
            if v4s >= 2:
                m_stage()
            if v4s >= 3:
                t_stage()
            if v4s >= 4:
                chain_stage(tb)
        sb.release(mF)
        m1 = sb.mark()
        pts = [[sb.alloc([512], F32) for _ in range(5)] for _ in range(2)]
        for tb in range(4):
            ts_ = slice(tb * 512, (tb + 1) * 512)
            cen, sq, rs, rk, bon = pts[tb % 2]
            p1, p2, p3, p4 = (bank[2], bank[3], bank[4], bank[5]) if tb % 2 == 0 else (bank[6], bank[7], bank[0], bank[1])
            P.mm(p1[:], bd32[:], Yacc[:, ts_])
            P.stt(cen[:], p1[:], -1.0 / 64, Yacc[:, ts_], ALU.mult, ALU.add)
            P.actf(sq[:], cen[:], AF.Square)
            P.mm(p2[:], bd32[:], sq[:])
            P.actf(rs[:], p2[:], AF.Sqrt, bias=K.gnc[:], scale=1.0 / 64)
            P.recip(rs[:], rs[:])
            P.tt("dve", cen[:], cen[:], rs[:], ALU.mult)
            P.ts("dve", cen[:], cen[:], K.vcol("ln_g", fc), ALU.mult, K.vcol("ln_b", fc), ALU.add)
            P.stt(rk[:], r_b[:, ts_], K.vcol("r_k", fc), k_b[:, ts_], ALU.mult, ALU.mult)
            P.mm(p3[:], bd32[:], rk[:])
            P.tt("dve", bon[:], p3[:], v_b[:, ts_], ALU.mult)
            P.tt("pool", cen[:], cen[:], bon[:], ALU.add)
            P.mm(p4[:], g_up[:, 0, fc * 128:(fc + 1) * 128], sg[:, 0, ts_], start=True, stop=False)
            P.mm(p4[:], g_up[0:32, 1, fc * 128:(fc + 1) * 128], sg[0:32, 1, ts_], start=False, stop=True)
            P.tt("dve", K.yR[:, fc, ts_], cen[:], p4[:], ALU.mult)
        sb.release(m1)
    sb.release(m)
    for kc in range(8):
        P.dma("sp" if kc % 2 == 0 else "act", X[:, kc, :], xsp[:, kc, :])


TWO_PI = 6.283185307179586
MAGIC = 12582912.0


def mixer1_phase(K):
    P, sb, bank, X, hT, cst, ps = K.P, K.sb, K.bank, K.X, K.hT, K.cst, K.ps
    cfg = K.cfg
    hy_w_in = K.inp("hy_w_in", [1, D, 3 * D])
    hy_w_out = K.inp("hy_w_out", [1, D, D])
    f_w1 = K.inp("hy_f_w1", [1, 33, 64])
    f_w2 = K.inp("hy_f_w2", [1, 64, 64])
    f_w3 = K.inp("hy_f_w3", [1, 64, 64])
    f_out = K.inp("hy_f_out", [1, 64, 4 * D])
    skip_d = K.inp("skip_bc", [128, 2 * D])
    hspec = P.dram_scratch("hspec", [4, 128, 32 * 512], BF16)
    SC = 2.0 / (2 * T)
    m = sb.mark()
    hid3b = sb.alloc([T], BF16)
    woutb = sb.alloc([4 * D], BF16)
    skip2 = sb.alloc([2 * D], F32)
    m0 = sb.alloc([1], F32)
    P.dma("sp", skip2[:], skip_d[:])
    P.ts("pool", skip2[:], skip2[:], SC, ALU.mult)
    mw = sb.mark()
    wf = sb.alloc([4 * D], F32)
    P.dma("sp", wf[0:64, :], P.dview(f_out, f_out.full[0]))
    for o_ in range(2):
        w0v = wf[0:64, o_ * 2048:o_ * 2048 + 1024]
        w1v = wf[0:64, o_ * 2048 + 1024:o_ * 2048 + 2048]
        P.tt("dve", woutb[0:64, o_ * 2048:o_ * 2048 + 1024], w0v, w1v, ALU.add)
        P.tt("dve", woutb[0:64, o_ * 2048 + 1024:o_ * 2048 + 2048], w0v, w1v, ALU.subtract)
    sb.release(mw)
    P.memset("dve", m0[:], 1.0)
    P.memset("dve", m0[0:1], 0.0)
    m1 = sb.mark()
    feats = sb.alloc([T], F32)
    hA = sb.alloc([T], F32)
    w1 = sb.alloc([64], F32)
    w2 = sb.alloc([64], F32)
    w3 = sb.alloc([64], F32)
    frb = sb.alloc([3], F32)
    arg = [sb.alloc([512], F32) for _ in range(2)]
    vv = [sb.alloc([512], F32) for _ in range(2)]
    P.dma("sp", feats[0:33, :], cst["featsT"][:])
    P.dma("act", w1[0:33, :], P.dview(f_w1, f_w1.full[0]))
    P.dma("act", w2[0:64, :], P.dview(f_w2, f_w2.full[0]))
    P.dma("act", w3[0:64, :], P.dview(f_w3, f_w3.full[0]))
    fr = K.vcol("f_fr", 0, 1, 64)
    for i, nm in enumerate(("f_b1", "f_b2", "f_b3")):
        P.tt("dve", frb[0:64, i:i + 1], K.vcol(nm, 0, 1, 64), fr, ALU.mult)
    n = 0
    for li, (w, kk_) in enumerate(((w1, 33), (w2, 64), (w3, 64))):
        for tb in range(4):
            ts_ = slice(tb * 512, (tb + 1) * 512)
            pb = bank[n % 2]
            a, v = arg[n % 2], vv[n % 2]
            n += 1
            src = feats if li == 0 else hA
            P.mm(pb[0:64, :], w[0:kk_, :], src[0:kk_, ts_])
            P.ts("dve", a[0:64], pb[0:64, :], fr, ALU.mult, frb[0:64, li:li + 1], ALU.add)
            P.ts("dve", v[0:64], a[0:64], 1.0 / TWO_PI, ALU.mult, MAGIC, ALU.add)
            P.ts("dve", v[0:64], v[0:64], MAGIC, ALU.subtract)
            P.stt(a[0:64], v[0:64], -TWO_PI, a[0:64], ALU.mult, ALU.add)
            P.ts("dve", a[0:64], a[0:64], 3.14159, ALU.min, -3.14159, ALU.max)
            if li < 2:
                P.actf(hA[0:64, ts_], a[0:64], AF.Sin)
            else:
                P.actf(hid3b[0:64, ts_], a[0:64], AF.Sin)
    sb.release(m1)

    rr = [0]

    def dq():
        rr[0] += 1
        return "sp" if rr[0] % 2 == 0 else "act"

    def flat3(b):
        return V(b.full.rearrange("p a b -> p (a b)"), b[:].rect)

    def fwd_dft(rhsC, rhsS, epilogue, tiles):
        NB = len(tiles)

        def load(fc):
            cE, cO, sE, sO = tiles[fc % NB]
            P.dma("sp", flat3(cE), cst["Ctf"][fc, 0])
            P.dma("act", flat3(sE), cst["Stf"][fc, 0])
            P.dma("sp", flat3(cO), cst["Ctf"][fc, 1])
            P.dma("act", flat3(sO), cst["Stf"][fc, 1])

        for fc in range(min(NB - 1, 8)):
            load(fc)
        for fc in range(8):
            if fc + NB - 1 < 8:
                load(fc + NB - 1)
            cE, cO, sE, sO = tiles[fc % NB]
            bk = [bank[4 * (fc % 2) + j] for j in range(4)]
            for t8 in range(8):
                P.mm(bk[0][:], cE[:, t8, :], rhsC[:, t8, :], start=(t8 == 0), stop=(t8 == 7))
                P.mm(bk[2][:], sE[:, t8, :], rhsS[:, t8, :], start=(t8 == 0), stop=(t8 == 7))
            for t8 in range(8):
                P.mm(bk[1][:], cO[:, t8, :], rhsC[:, 8 + t8, :], start=(t8 == 0), stop=(t8 == 7))
                P.mm(bk[3][:], sO[:, t8, :], rhsS[:, 8 + t8, :], start=(t8 == 0), stop=(t8 == 7))
            epilogue(fc, bk[0], bk[1], bk[2], bk[3])

    m1 = sb.mark()
    hs = sb.alloc([16, 512], BF16)
    hd = sb.alloc([16, 512], BF16)
    Hst = [sb.alloc([512], BF16) for _ in range(8)]
    hn = [0]
    win = [sb.alloc([512], F32) for _ in range(2)]
    f0 = [sb.alloc([512], F32) for _ in range(2)]
    f1 = [sb.alloc([512], F32) for _ in range(2)]
    etmp = [sb.alloc([512], F32) for _ in range(4)]
    tiles = [tuple(sb.alloc([8, 128], BF16) for _ in range(4)) for _ in range(2)]
    n = 0
    for o in range(2):
        for hh in range(2):
            for sc in range(16):
                w_ = win[n % 2]
                a0, a1 = f0[n % 2], f1[n % 2]
                pb0, pb1 = bank[4 + (n % 2) * 2], bank[5 + (n % 2) * 2]
                n += 1
                P.dma(dq(), w_[:], cst["window"][sc * 128:(sc + 1) * 128, hh * 512:(hh + 1) * 512])
                c0 = o * 2048 + hh * 512
                P.mm(pb0[:], hid3b[0:64, sc * 128:(sc + 1) * 128], woutb[0:64, c0:c0 + 512])
                P.mm(pb1[:], hid3b[0:64, sc * 128:(sc + 1) * 128], woutb[0:64, c0 + 1024:c0 + 1536])
                P.tt("dve", hs[:, sc, :], pb0[:], w_[:], ALU.mult)
                P.tt("dve", hd[:, sc, :], pb1[:], w_[:], ALU.mult)
                if sc == 0:
                    P.tt("dve", a0[0:1], hs[0:1, 0, :], hd[0:1, 0, :], ALU.add)
                    P.ts("dve", hs[0:1, 0, :], a0[0:1], 0.5, ALU.mult)
                    P.ts("dve", hd[0:1, 0, :], a0[0:1], 0.5, ALU.mult)

            def epi(fc, eR, oR_, eI, oI_, o=o, hh=hh):
                oR, oI, oRp, oRm = etmp
                sk = skip2[:, o * D + hh * 512:o * D + (hh + 1) * 512]
                hb = [Hst[(hn[0] + i) % 8] for i in range(4)]
                hn[0] += 4
                P.actf(oR[:], oR_[:], AF.Identity, scale=SC)
                P.actf(oI[:], oI_[:], AF.Identity, scale=SC)
                P.tt("dve", oRp[:], oR[:], sk, ALU.add)
                P.tt("pool", oRm[:], oR[:], sk, ALU.subtract)
                P.stt(hb[0][:], eR[:], SC, oRp[:], ALU.mult, ALU.add)
                P.stt(hb[1][:], eI[:], SC, oI[:], ALU.mult, ALU.add)
                P.stt(hb[2][:], eR[:], SC, oRm[:], ALU.mult, ALU.subtract)
                P.stt(hb[3][:], eI[:], -SC, oI[:], ALU.mult, ALU.add)
                for sl in range(4):
                    P.dma("sp", hspec[o * 2 + hh, :, (fc * 4 + sl) * 512:(fc * 4 + sl + 1) * 512], hb[sl][:])

            fwd_dft(hs, hd, epi, tiles)
    sb.release(m1)
    sb.release(m)

    m = sb.mark()
    xsp2 = P.dram_scratch("xspill2", [128, 8, T], F32)
    for kc in range(8):
        P.dma("sp" if kc % 2 == 0 else "act", xsp2[:, kc, :], X[:, kc, :])
    xa = Sub(sb, X.byte0, 8 * T * 4)
    tq2 = [xa.alloc([512], F32) for _ in range(12)]
    tq1x = [xa.alloc([512], F32) for _ in range(4)]
    xtiles = [tuple(xa.alloc([8, 128], BF16) for _ in range(4)) for _ in range(2)]
    xitiles = [(xa.alloc([4, 512], BF16), xa.alloc([4, 512], BF16)) for _ in range(2)]
    u = sb.alloc([16, 512], BF16)
    xgs = [sb.alloc([4, T], BF16) for _ in range(2)]
    cur = {"xg": xgs[0]}
    psT = [ps.at(6 * 2048, [4, 128], BF16), ps.at(7 * 2048, [4, 128], BF16)]
    cnt = {"n": 0, "nt": 0}

    def transposes_to_u(src_fn, dc, eng):
        for g4 in range(4):
            pt = psT[cnt["nt"] % 2]
            cnt["nt"] += 1
            for j in range(4):
                tc = g4 * 4 + j
                P.transpose(pt[:, j, :], src_fn(tc), K.ident[:])
            P.copy(eng, u[:, g4 * 4:(g4 + 1) * 4, dc * 128:(dc + 1) * 128], pt[:])

    def nat2(v):
        return V(v.ap.rearrange("p (n r) -> p n r", r=2), v.rect)

    def perm2(v):
        return V(v.ap.rearrange("p (r n) -> p n r", r=2), v.rect)

    def proj(hh, parts):
        m1 = sb.mark()
        HN = T // 2
        pe_ = sb.alloc([HN + 2], F32)
        po_ = sb.alloc([HN + 2], F32)
        tmp = sb.alloc([T], F32)
        vch = [sb.alloc([T], BF16) for _ in range(2)]
        wt = [sb.alloc([8, 128], BF16) for _ in range(3)]
        for b_ in (pe_, po_):
            P.memset("pool", b_[:, 0:1], 0.0)
            P.memset("pool", b_[:, HN + 1:HN + 2], 0.0)
        for part in parts:
            for dc in range(4):
                col0 = part * 1024 + hh * 512 + dc * 128
                vc = col0 // 128
                n = cnt["n"]
                w = wt[n % 3]
                P.dma("pool", w[:], K.wview(hy_w_in, 0, 0, 8, col0, 128))
                for tb in range(4):
                    pb = bank[4 + (cnt["n"] % 2)]
                    cnt["n"] += 1
                    for kc in range(8):
                        P.mm(pb[:], w[:, kc, :], hT[:, kc, tb * 512:(tb + 1) * 512], start=(kc == 0), stop=(kc == 7))
                    P.copy("act", pe_[:, 1 + tb * 256:1 + (tb + 1) * 256], V(pb.full[:, 0:512:2], pb[:].rect))
                    P.copy("act", po_[:, 1 + tb * 256:1 + (tb + 1) * 256], V(pb.full[:, 1:512:2], pb[:].rect))
                dst = vch[dc % 2] if part == 0 else None
                d_e = (dst[:, 0:HN] if dst is not None else cur["xg"][:, dc, 0:HN])
                d_o = (dst[:, HN:T] if dst is not None else cur["xg"][:, dc, HN:T])
                cb, c0_, c1_, c2_ = K.vcol("cb", vc), K.vcol("cw0", vc), K.vcol("cw1", vc), K.vcol("cw2", vc)
                te, to = tmp[:, 0:HN], tmp[:, HN:T]
                P.actf(te, pe_[:, 1:HN + 1], AF.Identity, bias=cb, scale=c1_)
                P.stt(te, po_[:, 0:HN], c0_, te, ALU.mult, ALU.add)
                P.stt(d_e, po_[:, 1:HN + 1], c2_, te, ALU.mult, ALU.add)
                P.actf(to, po_[:, 1:HN + 1], AF.Identity, bias=cb, scale=c1_)
                P.stt(to, pe_[:, 1:HN + 1], c0_, to, ALU.mult, ALU.add)
                P.stt(d_o, pe_[:, 2:HN + 2], c2_, to, ALU.mult, ALU.add)
                if part == 0:
                    vb = vch[dc % 2]
                    transposes_to_u(lambda tc, vb=vb: vb[:, tc * 128:(tc + 1) * 128], dc, "dve")
        sb.release(m1)

    def conv(hh, o):
        m1 = sb.mark()
        HY = sb.alloc([8, 4, 512], BF16)
        tq = [sb.alloc([512], F32) for _ in range(8)] + tq1x
        mt = sb.mark()
        tiles = xtiles
        xg = cur["xg"]
        P.dma("sp", HY[:], hspec[o * 2 + hh])

        def epi(fc, eR, oR_, eI, oI_):
            xr, xrp, xi, xip, a1, a2, b1, b2, c1, c2, d1, d2 = tq if fc % 2 == 0 else tq2
            P.copy("act", a1[:], oR_[:])
            P.copy("act", a2[:], oI_[:])
            P.tt("dve", xr[:], eR[:], a1[:], ALU.add)
            P.tt("dve", xrp[:], eR[:], a1[:], ALU.subtract)
            P.tt("dve", xi[:], eI[:], a2[:], ALU.add)
            P.stt(xip[:], eI[:], -1.0, a2[:], ALU.mult, ALU.add)
            Hre, Him, Hrp, Hip = (HY[:, fc, s_, :] for s_ in range(4))
            P.tt("dve", a1[:], xr[:], Hre, ALU.mult)
            P.tt("dve", a2[:], xi[:], Him, ALU.mult)
            P.tt("dve", b1[:], xr[:], Him, ALU.mult)
            P.tt("dve", b2[:], xi[:], Hre, ALU.mult)
            P.tt("dve", c1[:], xrp[:], Hrp, ALU.mult)
            P.tt("dve", c2[:], xip[:], Hip, ALU.mult)
            P.tt("dve", d1[:], xrp[:], Hip, ALU.mult)
            P.tt("dve", d2[:], xip[:], Hrp, ALU.mult)
            P.tt("pool", a1[:], a1[:], a2[:], ALU.subtract)
            P.tt("pool", b1[:], b1[:], b2[:], ALU.add)
            P.tt("pool", c1[:], c1[:], c2[:], ALU.subtract)
            P.tt("pool", d1[:], d1[:], d2[:], ALU.add)
            P.tt("pool", HY[:, fc, 0, :], a1[:], c1[:], ALU.add)
            P.tt("pool", HY[:, fc, 1, :], b1[:], d1[:], ALU.subtract)
            P.tt("dve", HY[:, fc, 2, :], a1[:], c1[:], ALU.subtract)
            P.tt("dve", HY[:, fc, 3, :], b1[:], d1[:], ALU.add)

        fwd_dft(u, u, epi, tiles)
        sb.release(mt)
        itiles = xitiles
        isteps = [(tb, fg) for tb in range(4) for fg in range(2)]

        def iload(k):
            tb, fg = isteps[k]
            ct, st = itiles[k % 2]
            P.dma("sp", flat3(ct), cst["Cft"][tb, fg])
            P.dma("act", flat3(st), cst["Sft"][tb, fg])

        iload(0)
        for k, (tb, fg) in enumerate(isteps):
            if k + 1 < len(isteps):
                iload(k + 1)
            ts_ = slice(tb * 512, (tb + 1) * 512)
            sR = 0 if tb < 2 else 2
            pbk = [bank[4 * (tb % 2) + dc] for dc in range(4)]
            ct, st = itiles[k % 2]
            for f4 in range(4):
                fc = fg * 4 + f4
                for dc in range(4):
                    P.mm(pbk[dc][:], HY[:, fc, sR, dc * 128:(dc + 1) * 128], ct[:, f4, :], start=(fc == 0), stop=False)
                for dc in range(4):
                    P.mm(pbk[dc][:], HY[:, fc, sR + 1, dc * 128:(dc + 1) * 128], st[:, f4, :], start=False, stop=(fc == 7))
            if fg == 1:
                for dc in range(4):
                    P.tt("dve", xg[:, dc, ts_], xg[:, dc, ts_], pbk[dc][:], ALU.mult)
        sb.release(m1)

    for hh in range(2):
        cur["xg"] = xgs[hh]
        proj(hh, [0, 1])
        conv(hh, 0)
        for dc in range(4):
            transposes_to_u(lambda tc, dc=dc, xg=xgs[hh]: xg[:, dc, tc * 128:(tc + 1) * 128], dc, "act")
        proj(hh, [2])
        conv(hh, 1)
    m1 = sb.mark()
    wo = [sb.alloc([8, 512], BF16) for _ in range(2)]
    for og in range(2):
        P.dma("pool", wo[og][:], K.wview(hy_w_out, 0, 0, 8, og * 512, 512))
    for kc in range(8):
        P.dma("sp" if kc % 2 == 0 else "act", X[:, kc, :], xsp2[:, kc, :])
    n = 0
    for og in range(2):
        for oc in range(4):
            for tb in range(4):
                po = bank[4 + (n % 4)]
                n += 1
                for kc in range(8):
                    P.mm(po[:], wo[og][:, kc, oc * 128:(oc + 1) * 128], xgs[kc // 4][:, kc % 4, tb * 512:(tb + 1) * 512],
                         start=(kc == 0), stop=(kc == 7))
                par, n0 = tb // 2, (tb % 2) * 512
                xrow = X[:, og * 4 + oc, :]
                xs = V(xrow.ap.rearrange("p (n r) -> p n r", r=2)[:, n0:n0 + 512, par], xrow.rect)
                P.stt(xs, po[:], K.mod[:, 48 + 16 + og * 4 + oc:48 + 17 + og * 4 + oc], xs, ALU.mult, ALU.add)
    sb.release(m1)
    sb.release(m)


_CACHE = {}


def _get_consts():
    if "c" not in _CACHE:
        _CACHE["c"] = host_consts()
    return _CACHE["c"]


def make_in_maps(inp):
    consts = _get_consts()
    maps = []
    vcols = None
    nvec = None
    shared = {}
    for k in ("ada_w", "ffn_w1", "ffn_w3", "ffn_w2", "mix_w_in", "mix_w_out", "hy_w_in", "hy_w_out",
              "hy_f_w1", "hy_f_w2", "hy_f_w3", "hy_f_out", "rwkv_w_up", "rwkv_a_up", "rwkv_g_up"):
        shared[k] = np.ascontiguousarray(inp[k])
    for b in range(NB):
        vp = vec_layout(inp, b)
        vecs = vp.build()
        vcols, nvec = vp.cols, vp.n
        m = {"xT": np.ascontiguousarray(inp["x"][b].T), "vecs": vecs,
             "skip_bc": np.ascontiguousarray(np.broadcast_to(inp["hy_skip"][0].reshape(1, 2 * D), (128, 2 * D)))}
        for k, v in consts.items():
            m["c_" + k] = v
        m.update(shared)
        maps.append(m)
    return maps, vcols, nvec


def kernel(**inputs):
    cfg = inputs.pop("_cfg", {})
    inp = {k: np.asarray(v) for k, v in inputs.items()}
    maps, vcols, nvec = make_in_maps(inp)
    K = build_program(vcols, nvec, cfg)
    names = set(K.din.keys())
    maps = [{k: v for k, v in m.items() if k in names} for m in maps]
    res = run_bass_kernel_spmd(K.nc, maps, core_ids=list(range(NB)))
    out = np.stack([np.ascontiguousarray(r["outT"].T) for r in res.results], 0)
    return out.astype(np.float32)
```

```python
import numpy as np
from contextlib import ExitStack
import concourse.bass as bass
import concourse.mybir as mybir
from concourse.bass_utils import run_bass_kernel_spmd

F32 = mybir.dt.float32
BF16 = mybir.dt.bfloat16
ALU = mybir.AluOpType
AF = mybir.ActivationFunctionType
DSZ = {F32: 4, BF16: 2}


class V:
    __slots__ = ("ap", "rect")

    def __init__(self, ap, rect):
        self.ap = ap
        self.rect = rect


class Buf:
    def __init__(self, space, base_ap2d, p, shape, dtype, byte0):
        self.space = space
        self.p = p
        self.shape = tuple(shape)
        self.dtype = dtype
        self.byte0 = byte0
        self.esz = DSZ[dtype]
        n = int(np.prod(shape))
        if len(shape) == 1:
            self.full = base_ap2d
        else:
            names = " ".join("d%d" % i for i in range(len(shape)))
            kw = {"d%d" % i: int(s) for i, s in enumerate(shape)}
            self.full = base_ap2d.rearrange("p (%s) -> p %s" % (names, names), **kw)
        self.strides = [int(np.prod(shape[i + 1:])) for i in range(len(shape))]

    def __getitem__(self, idx):
        if not isinstance(idx, tuple):
            idx = (idx,)
        idx = list(idx) + [slice(None)] * (1 + len(self.shape) - len(idx))
        ps = idx[0]
        if isinstance(ps, int):
            ps = slice(ps, ps + 1)
        p0 = ps.start or 0
        p1 = self.p if ps.stop is None else ps.stop
        lo = 0
        hi = 0
        for i, (ix, n) in enumerate(zip(idx[1:], self.shape)):
            if isinstance(ix, int):
                a, b = ix, ix + 1
            else:
                a = ix.start or 0
                b = n if ix.stop is None else ix.stop
                if ix.step is not None and ix.step < 0:
                    a, b = 0, n
            lo += a * self.strides[i]
            hi += (b - 1) * self.strides[i]
        rect = (self.space, p0, p1, self.byte0 + lo * self.esz, self.byte0 + (hi + 1) * self.esz)
        ap = self.full[tuple([slice(p0, p1)] + idx[1:])]
        return V(ap, rect)


class Arena:
    def __init__(self, prog, name, nbytes, psum=False):
        self.prog = prog
        self.name = name
        self.nbytes = nbytes
        self.psum = psum
        nc = prog.nc
        if psum:
            self.t = prog.stack.enter_context(nc.psum_tensor(name, [128, nbytes // 4], F32))
        else:
            self.t = prog.stack.enter_context(nc.sbuf_tensor(name, [128, nbytes // 4], F32))
        self.top = 0

    def alloc(self, shape, dtype, p=128, align=4):
        esz = DSZ[dtype]
        n = int(np.prod(shape))
        nb = (n * esz + align - 1) // align * align
        off = (self.top + align - 1) // align * align
        assert off + nb <= self.nbytes, "arena %s overflow: %d + %d > %d" % (self.name, off, nb, self.nbytes)
        self.top = off + nb
        return self.at(off, shape, dtype, p)

    def at(self, off, shape, dtype, p=128):
        esz = DSZ[dtype]
        n = int(np.prod(shape))
        nb = (n * esz + 3) // 4 * 4
        ap = self.t[0:p, off // 4:(off + nb) // 4]
        if dtype != F32:
            ap = ap.bitcast(dtype)
            ap = ap[:, 0:n]
        return Buf(self.name, ap, p, shape, dtype, off)

    def mark(self):
        return self.top

    def release(self, m):
        self.top = m


class Op:
    __slots__ = ("eng", "fn", "deps", "dma", "sem", "semval", "signal", "id", "prewait")


ENGS = ("pe", "act", "dve", "pool", "sp")


class Prog:
    def __init__(self, nc, n_dma_sems=(("sp", 16), ("act", 6), ("pool", 10))):
        self.nc = nc
        self.stack = ExitStack()
        self.ops = []
        self.eng_ops = {e: [] for e in ENGS}
        self.acc = {}
        self.dma_pools = {}
        self.dma_rr = {}
        self.sem_total = {}
        self.esem = {}
        for e in ("pe", "act", "dve", "pool"):
            self.esem[e] = self.stack.enter_context(nc.semaphore("s_" + e))
        for e, n in n_dma_sems:
            self.dma_pools[e] = [self.stack.enter_context(nc.semaphore("d_%s%d" % (e, i))) for i in range(n)]
            self.dma_rr[e] = 0
        self.n_dram = 0

    def dram_in(self, name, shape, dtype=F32):
        t = self.nc.dram_tensor(name, list(shape), dtype, kind="ExternalInput")
        return self._dram_buf(t, name, shape, dtype)

    def dram_out(self, name, shape, dtype=F32):
        t = self.nc.dram_tensor(name, list(shape), dtype, kind="ExternalOutput")
        return self._dram_buf(t, name, shape, dtype)

    def dram_scratch(self, name, shape, dtype=F32):
        t = self.nc.dram_tensor(name, list(shape), dtype, kind="Internal")
        return self._dram_buf(t, name, shape, dtype)

    def _dram_buf(self, t, name, shape, dtype):
        b = Buf.__new__(Buf)
        b.space = "dram:" + name
        b.p = shape[0]
        b.shape = tuple(shape[1:])
        b.dtype = dtype
        b.byte0 = 0
        b.esz = DSZ[dtype]
        b.full = t.ap()
        b.strides = [int(np.prod(shape[i + 1:])) for i in range(1, len(shape))]
        b.handle = t
        return b

    def dview(self, buf, ap):
        return V(ap, (buf.space, 0, 1 << 30, 0, 1 << 40))

    def _deps(self, reads, writes, opid):
        deps = set()
        for views, is_w in ((reads, False), (writes, True)):
            for v in views:
                sp, p0, p1, b0, b1 = v.rect if isinstance(v, V) else v
                lst = self.acc.setdefault(sp, [])
                keep = []
                for a in lst:
                    ap0, ap1, ab0, ab1, aid, aw = a
                    ov = not (ap1 <= p0 or p1 <= ap0 or ab1 <= b0 or b1 <= ab0)
                    if ov and (aw or is_w) and aid != opid:
                        deps.add(aid)
                    contained = ov and ap0 >= p0 and ap1 <= p1 and ab0 >= b0 and ab1 <= b1
                    if is_w and contained and aid != opid:
                        continue
                    keep.append(a)
                keep.append((p0, p1, b0, b1, opid, is_w))
                self.acc[sp] = keep
        return deps

    def op(self, eng, fn, reads=(), writes=(), dma=False):
        o = Op()
        o.eng = eng
        o.fn = fn
        o.dma = dma
        o.id = len(self.ops)
        o.signal = False
        o.sem = None
        o.semval = 0
        o.prewait = None
        o.deps = self._deps(reads, writes, o.id)
        self.ops.append(o)
        self.eng_ops[eng].append(o)
        return o

    def dma(self, eng, out, in_, **kw):
        q = {"sp": self.nc.sync, "act": self.nc.scalar, "pool": self.nc.gpsimd}[eng]
        return self.op(eng, lambda e: q.dma_start(out=out.ap, in_=in_.ap, **kw), [in_], [out], dma=True)

    def mm(self, out, lhsT, rhs, start=True, stop=True, **kw):
        return self.op("pe", lambda e: self.nc.tensor.matmul(out.ap, lhsT.ap, rhs.ap, start=start, stop=stop, **kw),
                       [lhsT, rhs], [out])

    def transpose(self, out, in_, ident):
        return self.op("pe", lambda e: self.nc.tensor.transpose(out.ap, in_.ap, ident.ap), [in_, ident], [out])

    def _ce(self, eng):
        return {"act": self.nc.scalar, "dve": self.nc.vector, "pool": self.nc.gpsimd}[eng]

    def tt(self, eng, out, in0, in1, op):
        E = self._ce(eng)
        return self.op(eng, lambda e: E.tensor_tensor(out=out.ap, in0=in0.ap, in1=in1.ap, op=op), [in0, in1], [out])

    def ts(self, eng, out, in0, s1, op0, s2=None, op1=None):
        E = self._ce(eng)
        rd = [in0] + [s for s in (s1, s2) if isinstance(s, V)]
        a1 = s1.ap if isinstance(s1, V) else s1
        a2 = s2.ap if isinstance(s2, V) else s2
        if op1 is None:
            return self.op(eng, lambda e: E.tensor_scalar(out=out.ap, in0=in0.ap, scalar1=a1, scalar2=None, op0=op0), rd, [out])
        return self.op(eng, lambda e: E.tensor_scalar(out=out.ap, in0=in0.ap, scalar1=a1, scalar2=a2, op0=op0, op1=op1), rd, [out])

    def stt(self, out, in0, s, in1, op0, op1):
        rd = [in0, in1] + ([s] if isinstance(s, V) else [])
        a = s.ap if isinstance(s, V) else s
        return self.op("dve", lambda e: self.nc.vector.scalar_tensor_tensor(out=out.ap, in0=in0.ap, scalar=a, in1=in1.ap, op0=op0, op1=op1), rd, [out])

    def copy(self, eng, out, in_):
        if eng == "act":
            return self.op("act", lambda e: self.nc.scalar.copy(out=out.ap, in_=in_.ap), [in_], [out])
        E = self._ce(eng)
        return self.op(eng, lambda e: E.tensor_copy(out=out.ap, in_=in_.ap), [in_], [out])

    def actf(self, out, in_, func, bias=None, scale=None):
        rd = [in_] + [s for s in (bias, scale) if isinstance(s, V)]
        kw = {}
        if bias is not None:
            kw["bias"] = bias.ap if isinstance(bias, V) else bias
        if scale is not None:
            kw["scale"] = scale.ap if isinstance(scale, V) else scale
        return self.op("act", lambda e: self.nc.scalar.activation(out=out.ap, in_=in_.ap, func=func, **kw), rd, [out])

    def recip(self, out, in_):
        return self.op("dve", lambda e: self.nc.vector.reciprocal(out=out.ap, in_=in_.ap), [in_], [out])

    def memset(self, eng, out, val):
        E = self._ce(eng)
        return self.op(eng, lambda e: E.memset(out.ap, val), [], [out])

    def finalize(self):
        for o in self.ops:
            for d in o.deps:
                p = self.ops[d]
                if p.eng == "pe" and o.eng == "pe" and not p.dma:
                    continue
                p.signal = True
        cnt = {e: 0 for e in self.esem}
        for o in self.ops:
            if o.dma:
                pool = self.dma_pools[o.eng]
                i = self.dma_rr[o.eng]
                self.dma_rr[o.eng] = (i + 1) % len(pool)
                s = pool[i]
                prev = self.sem_total.get(id(s), 0)
                o.prewait = (s, prev) if prev > 0 else None
                o.sem = s
                o.semval = prev + 16
                self.sem_total[id(s)] = o.semval
            elif o.signal:
                cnt[o.eng] += 1
                o.sem = self.esem[o.eng]
                o.semval = cnt[o.eng]
        self.counts = cnt

    def emit(self):
        self.finalize()
        nc = self.nc
        engobj = {"pe": nc.tensor, "act": nc.scalar, "dve": nc.vector, "pool": nc.gpsimd, "sp": nc.sync}
        with nc.Block() as block:
            def run(eng):
                E = engobj[eng]
                waited = {}
                for o in self.eng_ops[eng]:
                    need = {}
                    if o.prewait is not None:
                        need[id(o.prewait[0])] = o.prewait
                    for d in o.deps:
                        p = self.ops[d]
                        if p.sem is None:
                            continue
                        if p.eng == "pe" and eng == "pe" and not p.dma:
                            continue
                        k = id(p.sem)
                        if k not in need or need[k][1] < p.semval:
                            need[k] = (p.sem, p.semval)
                    for k, (s, v) in need.items():
                        if waited.get(k, 0) >= v:
                            continue
                        E.wait_ge(s, v)
                        waited[k] = v
                    ins = o.fn(E)
                    if o.dma:
                        ins.then_inc(o.sem, 16)
                    elif o.signal:
                        ins.then_inc(o.sem, 1)

            @block.tensor
            def _(e):
                run("pe")

            @block.scalar
            def _(e):
                run("act")

            @block.vector
            def _(e):
                run("dve")

            @block.gpsimd
            def _(e):
                run("pool")

            @block.sync
            def _(e):
                run("sp")
                for pool in self.dma_pools.values():
                    for s in pool:
                        tot = self.sem_total.get(id(s), 0)
                        if tot > 0:
                            nc.sync.wait_ge(s, tot)
        self.stack.close()
import ml_dtypes

D = 1024
T = 2048
DFF = 2816
NB = 8
EPS = 1e-6


class VecPack:
    def __init__(self):
        self.cols = {}
        self.n = 0
        self.parts = []

    def add(self, name, arr):
        a = np.asarray(arr, np.float32).reshape(-1)
        n = a.shape[0]
        nc_ = (n + 127) // 128
        pad = np.zeros(nc_ * 128, np.float32)
        pad[:n] = a
        self.cols[name] = (self.n, nc_)
        self.parts.append(pad.reshape(nc_, 128).T)
        self.n += nc_

    def build(self):
        return np.ascontiguousarray(np.concatenate(self.parts, axis=1))


def vec_layout(inp, b):
    vp = VecPack()
    vp.add("c", inp["c"][b])
    for l in range(2):
        vp.add("ada_b%d" % l, inp["ada_b"][l])
        vp.add("g_mix%d" % l, inp["norm_mix"][l])
        vp.add("g_ffn%d" % l, inp["norm_ffn"][l])
    vp.add("g_fin", inp["final_norm"])
    vp.add("qn", np.tile(inp["attn_q_norm"][0], 2))
    vp.add("kn", np.tile(inp["attn_k_norm"][0], 2))
    mu = inp["rwkv_mu"][0]
    vp.add("mu_rkv", mu[:1536])
    vp.add("mu_w", mu[1536:1664])
    vp.add("mu_a", mu[1664:1792])
    vp.add("mu_g", mu[1792:1952])
    vp.add("w0", inp["rwkv_w0"][0])
    vp.add("a0", inp["rwkv_a0"][0])
    vp.add("k_k", inp["rwkv_k_k"][0])
    vp.add("k_a", inp["rwkv_k_a"][0])
    vp.add("r_k", inp["rwkv_r_k"][0])
    vp.add("ln_g", inp["rwkv_ln_g"][0])
    vp.add("ln_b", inp["rwkv_ln_b"][0])
    cw = inp["hy_conv_w"][0]
    for j in range(3):
        vp.add("cw%d" % j, cw[j])
    vp.add("cb", inp["hy_conv_b"][0])
    vp.add("f_b1", inp["hy_f_b1"][0])
    vp.add("f_b2", inp["hy_f_b2"][0])
    vp.add("f_b3", inp["hy_f_b3"][0])
    vp.add("f_fr", inp["hy_sin_freq"][0])
    return vp


def host_consts():
    bf = ml_dtypes.bfloat16
    c = {}
    c["ident"] = np.eye(128, dtype=np.float32).astype(bf)
    c["ones"] = np.ones((128, 128), np.float32).astype(bf)
    bd = np.zeros((128, 128), np.float32)
    bd[:64, :64] = 1
    bd[64:, 64:] = 1
    c["bd"] = bd.astype(bf)
    rm = np.zeros((128, 128), np.float32)
    for i in range(64):
        rm[2 * i + 1, 2 * i] = -1.0
        rm[2 * i, 2 * i + 1] = 1.0
    c["rm"] = rm
    t = np.arange(T)
    row = (t // 64).astype(np.float32)
    col = (t % 64).astype(np.float32)
    inv = (10000.0 ** (-np.arange(0, 32, 2, dtype=np.float32) / 32)).astype(np.float32)
    ang = np.concatenate([row[:, None] * inv, col[:, None] * inv], -1)
    p = np.arange(128)
    fi = (p % 64) // 2
    c["cos"] = np.cos(ang)[:, fi].T.astype(np.float32).copy()
    c["sin"] = np.sin(ang)[:, fi].T.astype(np.float32).copy()
    j = np.arange(128)
    same = (j[:, None] // 64) == (j[None, :] // 64)
    c["m_strict"] = (same & (j[:, None] < j[None, :])).astype(np.float32)
    c["m_incl"] = (same & (j[:, None] <= j[None, :])).astype(np.float32)
    c["m4"] = np.ascontiguousarray(np.concatenate([c["m_strict"], c["m_incl"], c["m_strict"], c["m_incl"]], 1))
    c["m_strictT"] = np.ascontiguousarray(c["m_strict"].T)
    c["bd32"] = bd.copy()
    N = 2 * T
    f = np.arange(T, dtype=np.float64)[:, None]
    tt = np.arange(T, dtype=np.float64)[None, :]
    ang = 2 * np.pi * (f + 0.5) * tt / N
    C = np.cos(ang)
    S = -np.sin(ang)
    perm = np.concatenate([np.arange(0, T, 2), np.arange(1, T, 2)])
    Ch = C[:T // 2][:, perm].astype(np.float32)
    Sh = S[:T // 2][:, perm].astype(np.float32)

    def tile_ft(M):
        A = M.reshape(2, 4, 128, 4, 512)
        return np.ascontiguousarray(A.transpose(3, 0, 2, 1, 4)).reshape(4, 2, 128, 2048)

    def tile_tf(Mt):
        A = Mt.reshape(2, 8, 128, 8, 128)
        return np.ascontiguousarray(A.transpose(3, 0, 2, 1, 4)).reshape(8, 2, 128, 1024)
    c["Cft"] = tile_ft(Ch).astype(bf)
    c["Sft"] = tile_ft(Sh).astype(bf)
    c["Ctf"] = tile_tf(np.ascontiguousarray(Ch.T)).astype(bf)
    c["Stf"] = tile_tf(np.ascontiguousarray(Sh.T)).astype(bf)
    tl = np.linspace(0.0, 1.0, T, dtype=np.float32)[:, None]
    om = (2.0 * np.pi / T) * np.arange(T, dtype=np.float32)[:, None]
    bands = np.linspace(1e-4, 15, 16, dtype=np.float32)[None, :]
    feats = np.concatenate([tl, np.cos(bands * om), -np.sin(bands * om)], -1).astype(np.float32)
    c["featsT"] = np.ascontiguousarray(feats.T[:, perm])
    mn = np.log(1e-2) / 1.5
    mx = np.log(1e-2) / 0.3
    deltas = np.abs(np.linspace(mn, mx, D, dtype=np.float32))
    c["window"] = np.ascontiguousarray(np.exp(-tl * deltas).astype(np.float32)[perm])
    return c


CONST_DT = {"ident": BF16, "ones": BF16, "bd": BF16, "Cft": BF16, "Sft": BF16, "Ctf": BF16, "Stf": BF16}


class KB:
    def __init__(self, vcols, cfg):
        self.cfg = cfg
        nc = bass.Bass("TRN2", target_bir_lowering=False)
        self.nc = nc
        P = Prog(nc)
        self.P = P
        self.vcols = vcols
        self.sb = Arena(P, "sb", 206 * 1024)
        self.ps = Arena(P, "ps", 16 * 1024, psum=True)
        self.bank = [self.ps.at(i * 2048, [512], F32) for i in range(8)]
        self.din = {}
        self.pool_rr = 0

    def inp(self, name, shape, dtype=F32):
        b = self.P.dram_in(name, shape, dtype)
        self.din[name] = b
        return b

    def vcol(self, name, j=0, n=1, p=128):
        o, nc_ = self.vcols[name]
        assert j + n <= nc_, name
        return self.vecs[0:p, o + j:o + j + n]

    def wview(self, W, lead, k0, nk, c0, ncols):
        full = W.full
        if lead is not None:
            full = full[lead]
        ap = full[k0 * 128:(k0 + nk) * 128, c0:c0 + ncols].rearrange("(kc p) c -> p kc c", p=128)
        return self.P.dview(W, ap)


def build_program(vcols, nvec, cfg):
    K = KB(vcols, cfg)
    P, nc, sb = K.P, K.nc, K.sb
    bank = K.bank
    xT = K.inp("xT", [D, T])
    vecs_d = K.inp("vecs", [128, nvec])
    ada_w = K.inp("ada_w", [2, D, 6 * D])
    ffn_w1 = K.inp("ffn_w1", [2, D, DFF])
    ffn_w3 = K.inp("ffn_w3", [2, D, DFF])
    ffn_w2 = K.inp("ffn_w2", [2, DFF, D])
    cst = {}
    for nm, shp in (("ident", [128, 128]), ("ones", [128, 128]), ("bd", [128, 128]), ("rm", [128, 128]),
                    ("cos", [128, T]), ("sin", [128, T]), ("m_strict", [128, 128]), ("m_incl", [128, 128]),
                    ("Cft", [4, 2, 128, 2048]), ("Sft", [4, 2, 128, 2048]),
                    ("Ctf", [8, 2, 128, 1024]), ("Stf", [8, 2, 128, 1024]),
                    ("featsT", [33, T]), ("window", [T, D])):
        cst[nm] = K.inp("c_" + nm, shp, CONST_DT.get(nm, F32))
    K.cst = cst
    outT = P.dram_out("outT", [D, T])
    K.outT = outT
    X = sb.alloc([8, T], F32)
    hT = sb.alloc([8, T], BF16)
    vecs = sb.alloc([nvec], F32)
    K.X, K.hT, K.vecs = X, hT, vecs
    ident = sb.alloc([128], BF16)
    ones = sb.alloc([128], BF16)
    bd = sb.alloc([128], BF16)
    K.ident, K.ones, K.bd = ident, ones, bd
    cond = sb.alloc([8], F32)
    mod = sb.alloc([96], F32)
    am = sb.alloc([2, 8], F32)
    af = sb.alloc([2, 8], F32)
    K.mod, K.am, K.af = mod, am, af
    K.base_mark = sb.mark()

    P.dma("sp", vecs[:], vecs_d[:])
    P.dma("act", ident[:], cst["ident"][:])
    P.dma("act", ones[:], cst["ones"][:])
    P.dma("act", bd[:], cst["bd"][:])
    for kc in range(8):
        P.dma("sp" if kc % 2 == 0 else "act", X[:, kc, :], xT[kc * 128:(kc + 1) * 128, :])
    P.actf(cond[:], K.vcol("c", 0, 8), AF.Silu)

    def ada_phase(l):
        m = sb.mark()
        wt = [sb.alloc([8, 512], F32) for _ in range(2)]
        pm = bank[7]
        for jg in range(12):
            w = wt[jg % 2]
            P.dma("sp" if jg % 2 == 0 else "act", w[:], K.wview(ada_w, l, 0, 8, jg * 512, 512))
            for j in range(4):
                col = jg * 4 + j
                for kc in range(8):
                    P.mm(pm[:, col:col + 1], w[:, kc, j * 128:(j + 1) * 128], cond[:, kc:kc + 1],
                         start=(kc == 0), stop=(kc == 7))
        P.tt("dve", mod[:, l * 48:(l + 1) * 48], pm[:, 0:48], K.vcol("ada_b%d" % l, 0, 48), ALU.add)
        P.stt(am[:, l, :], mod[:, l * 48 + 8:l * 48 + 16], 1.0, K.vcol("g_mix%d" % l, 0, 8), ALU.add, ALU.mult)
        P.stt(af[:, l, :], mod[:, l * 48 + 32:l * 48 + 40], 1.0, K.vcol("g_ffn%d" % l, 0, 8), ALU.add, ALU.mult)
        sb.release(m)

    def rstd_block(tb, psb, rs):
        m = sb.mark()
        sq = [sb.alloc([512], BF16) for _ in range(3)]
        for kc in range(8):
            s = sq[kc % 3]
            if kc % 4 != 3:
                P.actf(s[:], X[:, kc, tb * 512:(tb + 1) * 512], AF.Square)
            else:
                P.tt("pool", s[:], X[:, kc, tb * 512:(tb + 1) * 512], X[:, kc, tb * 512:(tb + 1) * 512], ALU.mult)
            P.mm(psb[:], ones[:], s[:], start=(kc == 0), stop=(kc == 7))
        P.actf(rs[:], psb[:], AF.Sqrt, bias=K.epsc[:], scale=1.0 / D)
        P.recip(rs[:], rs[:])
        sb.release(m)

    epsc = sb.alloc([1], F32)
    K.epsc = epsc
    P.memset("dve", epsc[:], EPS)
    gnc = sb.alloc([1], F32)
    K.gnc = gnc
    P.memset("dve", gnc[:], 64e-5)
    K.base_mark = sb.mark()

    def norm_phase(a_cols, shift_cols):
        m = sb.mark()
        rs = [sb.alloc([512], F32) for _ in range(2)]
        tmp = [sb.alloc([512], F32) for _ in range(3)]
        for tb in range(4):
            r = rs[tb % 2]
            rstd_block(tb, bank[tb % 2], r)
            for kc in range(8):
                tm = tmp[kc % 3]
                P.stt(tm[:], X[:, kc, tb * 512:(tb + 1) * 512], a_cols[kc], r[:], ALU.mult, ALU.mult)
                P.actf(hT[:, kc, tb * 512:(tb + 1) * 512], tm[:], AF.Identity, bias=shift_cols[kc])
        sb.release(m)

    def ffn_phase(l):
        m = sb.mark()
        groups = [(g * 512, min(512, DFF - g * 512)) for g in range((DFF + 511) // 512)]
        w1t = [sb.alloc([8, 512], BF16) for _ in range(2)]
        w3t = [sb.alloc([8, 512], BF16) for _ in range(2)]
        w2t = [sb.alloc([4, 1024], BF16) for _ in range(2)]
        ug = sb.alloc([4, T], BF16)
        sil = [sb.alloc([512], F32) for _ in range(2)]
        gate = [mod[:, l * 48 + 40 + kc:l * 48 + 41 + kc] for kc in range(8)]

        def load(g):
            c0, w = groups[g]
            P.dma("pool", w1t[g % 2][:, :, 0:w], K.wview(ffn_w1, l, 0, 8, c0, w))
            P.dma("pool", w3t[g % 2][:, :, 0:w], K.wview(ffn_w3, l, 0, 8, c0, w))
            P.dma("pool", w2t[g % 2][:, 0:w // 128, :], K.wview(ffn_w2, l, c0 // 128, w // 128, 0, D))

        load(0)
        n = 0
        for g in range(len(groups)):
            if g + 1 < len(groups):
                load(g + 1)
            c0, w = groups[g]
            nfc = w // 128
            a, b, c2 = w1t[g % 2], w3t[g % 2], w2t[g % 2]
            for fc in range(nfc):
                for tb in range(4):
                    p1 = bank[(n % 2) * 2]
                    p3 = bank[(n % 2) * 2 + 1]
                    s = sil[n % 2]
                    n += 1
                    for kc in range(8):
                        P.mm(p1[:], a[:, kc, fc * 128:(fc + 1) * 128], hT[:, kc, tb * 512:(tb + 1) * 512],
                             start=(kc == 0), stop=(kc == 7))
                    for kc in range(8):
                        P.mm(p3[:], b[:, kc, fc * 128:(fc + 1) * 128], hT[:, kc, tb * 512:(tb + 1) * 512],
                             start=(kc == 0), stop=(kc == 7))
                    P.actf(s[:], p1[:], AF.Silu)
                    P.tt("dve", ug[:, fc, tb * 512:(tb + 1) * 512], s[:], p3[:], ALU.mult)
            k = 0
            for oc in range(8):
                for tb in range(4):
                    po = bank[4 + (k % 4)]
                    k += 1
                    for fc in range(nfc):
                        P.mm(po[:], c2[:, fc, oc * 128:(oc + 1) * 128], ug[:, fc, tb * 512:(tb + 1) * 512],
                             start=(fc == 0), stop=(fc == nfc - 1))
                    xs = X[:, oc, tb * 512:(tb + 1) * 512]
                    P.stt(xs, po[:], gate[oc], xs, ALU.mult, ALU.add)
        sb.release(m)

    def final_phase():
        m = sb.mark()
        rs = [sb.alloc([512], F32) for _ in range(2)]
        ot = [sb.alloc([512], F32) for _ in range(4)]
        n = 0
        for tb in range(4):
            r = rs[tb % 2]
            rstd_block(tb, bank[tb % 2], r)
            for kc in range(8):
                o = ot[n % 4]
                n += 1
                P.stt(o[:], X[:, kc, tb * 512:(tb + 1) * 512], K.vcol("g_fin", kc), r[:], ALU.mult, ALU.mult)
                P.dma("sp" if n % 2 == 0 else "act", outT[kc * 128:(kc + 1) * 128, tb * 512:(tb + 1) * 512], o[:])
        sb.release(m)

    def dump_x():
        for kc in range(8):
            P.dma("sp", outT[kc * 128:(kc + 1) * 128, :], X[:, kc, :])

    K.norm_phase = norm_phase
    ada_phase(0)
    stop = cfg.get("stop")
    for l in range(2):
        if l == 1:
            ada_phase(1)
        if cfg.get("mix%d" % l, True):
            norm_phase([am[:, l, kc:kc + 1] for kc in range(8)], [mod[:, l * 48 + kc:l * 48 + kc + 1] for kc in range(8)])
            if l == 0:
                mixer0_phase(K)
            else:
                mixer1_phase(K)
        if stop == "mix%d" % l:
            dump_x()
            break
        if cfg.get("ffn%d" % l, True):
            norm_phase([af[:, l, kc:kc + 1] for kc in range(8)], [mod[:, l * 48 + 24 + kc:l * 48 + 25 + kc] for kc in range(8)])
            ffn_phase(l)
        if stop == "ffn%d" % l:
            dump_x()
            break
    else:
        final_phase()
    P.emit()
    return K


def mixer0_phase(K):
    P, sb, bank, X, hT, cst = K.P, K.sb, K.bank, K.X, K.hT, K.cst
    cfg = K.cfg
    mix_w_in = K.inp("mix_w_in", [1, D, 2720])
    mix_w_out = K.inp("mix_w_out", [1, D, D])
    K.mix_w_in = mix_w_in
    m = sb.mark()
    yA = sb.alloc([4, T], BF16)
    K.yA = yA
    if cfg.get("attn", True):
        attention_phase(K)
    else:
        P.memset("pool", yA[:], 0.0)
    yR = sb.alloc([4, T], BF16)
    K.yR = yR
    if cfg.get("rwkv", True):
        rwkv_phase(K)
    else:
        P.memset("pool", yR[:], 0.0)
    m2 = sb.mark()
    wt = [sb.alloc([8, 512], BF16) for _ in range(2)]
    for og in range(2):
        P.dma("pool", wt[og][:], K.wview(mix_w_out, 0, 0, 8, og * 512, 512))
    n = 0
    for og in range(2):
        for oc in range(4):
            for tb in range(4):
                po = bank[n % 4]
                n += 1
                for kc in range(8):
                    src = yA if kc < 4 else yR
                    P.mm(po[:], wt[og][:, kc, oc * 128:(oc + 1) * 128], src[:, kc % 4, tb * 512:(tb + 1) * 512],
                         start=(kc == 0), stop=(kc == 7))
                xs = X[:, og * 4 + oc, tb * 512:(tb + 1) * 512]
                P.stt(xs, po[:], K.mod[:, 16 + og * 4 + oc:17 + og * 4 + oc], xs, ALU.mult, ALU.add)
    sb.release(m)


def attention_phase(K):
    P, sb, bank, X, hT, cst, yT = K.P, K.sb, K.bank, K.X, K.hT, K.cst, K.yA
    mix_w_in = K.mix_w_in
    m = sb.mark()
    qT = sb.alloc([4, T], BF16)
    kT = sb.alloc([2, T], BF16)
    vaug = sb.alloc([16, 2, 128], BF16)
    m1 = sb.mark()
    cos = sb.alloc([T], F32)
    sin = sb.alloc([T], F32)
    rm = sb.alloc([128], F32)
    P.dma("sp", cos[:], cst["cos"][:])
    P.dma("act", sin[:], cst["sin"][:])
    P.dma("sp", rm[:], cst["rm"][:])
    P.memset("pool", vaug[:, :, :, 64:128], 1.0)
    wq = [sb.alloc([8, 128], BF16) for _ in range(3)]
    sqb = [sb.alloc([512], BF16) for _ in range(2)]
    rsb = [sb.alloc([512], F32) for _ in range(2)]
    qnb = [sb.alloc([512], F32) for _ in range(2)]
    t1b = [sb.alloc([512], F32) for _ in range(2)]
    t2b = [sb.alloc([512], F32) for _ in range(2)]
    chunks = [("q", c) for c in range(4)] + [("k", g) for g in range(2)]
    n = 0
    for ci, (kind, idx) in enumerate(chunks):
        w = wq[ci % 3]
        if kind == "q":
            P.dma("pool", w[:], K.wview(mix_w_in, 0, 0, 8, idx * 128, 128))
            gain = K.vcol("qn")
            dst = qT
        else:
            P.dma("pool", w[:, :, 0:64], K.wview(mix_w_in, 0, 0, 8, 512 + idx * 64, 64))
            P.dma("pool", w[:, :, 64:128], K.wview(mix_w_in, 0, 0, 8, 512 + idx * 64, 64))
            gain = K.vcol("kn")
            dst = kT
        for tb in range(4):
            ts_ = slice(tb * 512, (tb + 1) * 512)
            pa = bank[(n % 2) * 3]
            pb = bank[(n % 2) * 3 + 1]
            pc = bank[(n % 2) * 3 + 2]
            sq, rs, qn, t1, t2 = sqb[n % 2], rsb[n % 2], qnb[n % 2], t1b[n % 2], t2b[n % 2]
            n += 1
            for kc in range(8):
                P.mm(pa[:], w[:, kc, :], hT[:, kc, ts_], start=(kc == 0), stop=(kc == 7))
            P.actf(sq[:], pa[:], AF.Square)
            P.mm(pb[:], K.bd[:], sq[:])
            P.actf(rs[:], pb[:], AF.Sqrt, bias=K.epsc[:], scale=1.0 / 64)
            P.recip(rs[:], rs[:])
            P.stt(qn[:], pa[:], gain, rs[:], ALU.mult, ALU.mult)
            P.mm(pc[:], rm[:], qn[:])
            P.tt("dve", t1[:], qn[:], cos[:, ts_], ALU.mult)
            P.tt("dve", t2[:], pc[:], sin[:, ts_], ALU.mult)
            P.tt("pool", dst[:, idx, ts_], t1[:], t2[:], ALU.add)
    wv = wq[0]
    P.dma("pool", wv[:], K.wview(mix_w_in, 0, 0, 8, 640, 128))
    for tc in range(16):
        pa = bank[6 + (tc % 2)]
        for kc in range(8):
            P.mm(pa[:, 0:128], hT[:, kc, tc * 128:(tc + 1) * 128], wv[:, kc, :], start=(kc == 0), stop=(kc == 7))
        for g in range(2):
            P.copy("act", vaug[:, tc, g, 0:64], pa[:, g * 64:(g + 1) * 64])
    sb.release(m1)
    NPB = 3
    pbuf = [sb.alloc([2, 512], BF16) for _ in range(NPB)]
    rec = [sb.alloc([512], F32) for _ in range(2)]
    prs = [(0, 1), (2, 3), (6, 7)]
    pview = [K.ps.at(a * 2048, [2, 512], F32) for a, _ in prs]
    items = [(h, qb, k2) for h in range(8) for qb in range(4) for k2 in range(8)]
    SK = 2
    for i in range(len(items) + SK):
        if i < len(items):
            h, qb, k2 = items[i]
            c, par, g = h // 2, h % 2, h // 4
            pr = slice(par * 64, par * 64 + 64)
            qs = slice(qb * 512, (qb + 1) * 512)
            pv = pview[i % 3]
            pe = pbuf[i % NPB]
            for j in range(2):
                kc = 2 * k2 + j
                P.mm(pv[:, j, :], kT[pr, g, kc * 128:(kc + 1) * 128], qT[pr, c, qs])
            P.actf(V(pe.full.rearrange("p a b -> p (a b)"), pe[:].rect),
                   V(pv.full.rearrange("p a b -> p (a b)"), pv[:].rect), AF.Exp, scale=0.125)
        if i >= SK:
            h, qb, k2 = items[i - SK]
            c, par, g = h // 2, h % 2, h // 4
            pr = slice(par * 64, par * 64 + 64)
            qs = slice(qb * 512, (qb + 1) * 512)
            n1 = (i - SK) // 8
            po = bank[4 + (n1 % 2)]
            rc = rec[n1 % 2]
            pe = pbuf[(i - SK) % NPB]
            for j in range(2):
                kc = 2 * k2 + j
                P.mm(po[:], vaug[:, kc, g, :], pe[:, j, :], start=(kc == 0), stop=(kc == 15))
            if k2 == 7:
                P.recip(rc[0:64], po[64:128])
                P.tt("dve", yT[pr, c, qs], po[0:64], rc[0:64], ALU.mult)
    sb.release(m)


C_DEC = 0.6065306597126334
GN_EPS = 64e-5


class Sub:
    def __init__(self, arena, b0, nbytes):
        self.a, self.b0, self.top, self.end = arena, b0, b0, b0 + nbytes

    def alloc(self, shape, dtype, p=128):
        n = int(np.prod(shape)) * DSZ[dtype]
        n = (n + 3) // 4 * 4
        assert self.top + n <= self.end, "sub overflow"
        b = self.a.at(self.top, shape, dtype, p)
        self.top += n
        return b


def rev(v):
    return V(v.ap[:, ::-1], v.rect)


def rwkv_phase(K):
    P, sb, bank, ps, X, hT, cst = K.P, K.sb, K.bank, K.ps, K.X, K.hT, K.cst
    w_in = K.mix_w_in
    w_up_d = K.inp("rwkv_w_up", [1, 2, 64, 512])
    a_up_d = K.inp("rwkv_a_up", [1, 2, 64, 512])
    g_up_d = K.inp("rwkv_g_up", [1, 160, 512])
    m4_d = K.inp("c_m4", [128, 512])
    mT_d = K.inp("c_m_strictT", [128, 128])
    bd32_d = K.inp("c_bd32", [128, 128])
    xsp = P.dram_scratch("xspill", [128, 8, T], F32)
    for kc in range(8):
        P.dma("sp" if kc % 2 == 0 else "act", xsp[:, kc, :], X[:, kc, :])
    xa = Sub(sb, X.byte0, 8 * T * 4)
    m = sb.mark()
    tw = sb.alloc([T], BF16)
    al = sb.alloc([T], BF16)
    sg = sb.alloc([2, T], BF16)
    g_up = sb.alloc([2, 512], BF16)
    a_up = sb.alloc([512], BF16)
    w_up = sb.alloc([512], BF16)
    bd32 = sb.alloc([128], F32)
    m4 = sb.alloc([4, 128], F32)
    mT = sb.alloc([128], F32)
    mus = sb.alloc([2, 16], F32)
    omka = sb.alloc([4], F32)
    w0c = sb.alloc([8], F32)
    P.dma("pool", g_up[:, 0, :], P.dview(g_up_d, g_up_d.full[0, 0:128, :]))
    P.dma("pool", g_up[0:32, 1, :], P.dview(g_up_d, g_up_d.full[0, 128:160, :]))
    P.dma("pool", a_up[:], P.dview(a_up_d, a_up_d.full[0].rearrange("d r c -> (d r) c")))
    P.dma("pool", w_up[:], P.dview(w_up_d, w_up_d.full[0].rearrange("d r c -> (d r) c")))
    P.dma("sp", bd32[:], bd32_d[:])
    P.dma("sp", V(m4.full.rearrange("p a b -> p (a b)"), m4[:].rect), m4_d[:])
    P.dma("sp", mT[:], mT_d[:])
    mu_names = [("mu_rkv", j) for j in range(12)] + [("mu_w", 0), ("mu_a", 0), ("mu_g", 0), ("mu_g", 1)]
    for i, (nm, j) in enumerate(mu_names):
        P.ts("dve", mus[:, 0, i:i + 1], K.vcol(nm, j), 0.5, ALU.mult)
        P.ts("dve", mus[:, 1, i:i + 1], K.vcol(nm, j), -1.0, ALU.mult, 1.0, ALU.add)
    P.ts("dve", omka[:], K.vcol("k_a", 0, 4), -1.0, ALU.mult, 1.0, ALU.add)

    cnt = {"n": 0}

    PS = {"sets": None, "i": 0}

    def alloc_proj_sets():
        sets = []
        for _ in range(2):
            pre = sb.alloc([T + 2], F32)
            t1 = sb.alloc([T], F32)
            w = sb.alloc([8, 128], BF16)
            P.memset("pool", pre[:, 0:1], 0.0)
            P.memset("pool", pre[:, T + 1:T + 2], 0.0)
            sets.append((pre, t1, w))
        PS["sets"] = sets

    def proj_shift(col0, ncols, mui, dst_fn, np_=128):
        pre, t1, w = PS["sets"][PS["i"] % 2]
        PS["i"] += 1
        P.dma("pool", w[:, :, 0:ncols], K.wview(w_in, 0, 0, 8, col0, ncols))
        for tb in range(4):
            pb = bank[cnt["n"] % 2]
            cnt["n"] += 1
            for kc in range(8):
                P.mm(pb[0:ncols, :], w[:, kc, 0:ncols], hT[:, kc, tb * 512:(tb + 1) * 512], start=(kc == 0), stop=(kc == 7))
            P.copy("act", pre[0:ncols, 1 + tb * 512:1 + (tb + 1) * 512], pb[0:ncols, :])
        q = slice(0, ncols)
        P.tt("dve", t1[q], pre[q, 0:T], pre[q, 2:T + 2], ALU.add)
        P.actf(pre[q, 1:T + 1], pre[q, 1:T + 1], AF.Identity, scale=mus[q, 1, mui:mui + 1])
        P.stt(t1[q], t1[q], mus[q, 0, mui:mui + 1], pre[q, 1:T + 1], ALU.mult, ALU.add)
        dst_fn(t1)

    mps = sb.mark()
    alloc_proj_sets()
    proj_shift(768 + 1536, 128, 12, lambda t: P.actf(tw[:], t[:], AF.Tanh))
    proj_shift(768 + 1664, 128, 13, lambda t: P.copy("act", al[:], t[:]))
    proj_shift(768 + 1792, 128, 14, lambda t: P.actf(sg[:, 0, :], t[:], AF.Sigmoid))
    proj_shift(768 + 1920, 32, 15, lambda t: P.actf(sg[0:32, 1, :], t[0:32], AF.Sigmoid))
    sb.release(mps)

    class BB:
        pass

    def alloc_bundle(al):
        b = BB()
        b.Mm = al.alloc([8, 4, 128], BF16)
        b.NT = al.alloc([8, 128], BF16)
        b.Ab = al.alloc([8, 128], BF16)
        b.ATb = al.alloc([8, 128], BF16)
        b.Pm = al.alloc([8, 128], BF16)
        b.ZR = al.alloc([2, 512], BF16)
        b.Kt = al.alloc([512], BF16)
        b.Bt = al.alloc([512], BF16)
        b.Kh = al.alloc([512], BF16)
        b.Bh = al.alloc([512], BF16)
        b.Vt = al.alloc([512], BF16)
        b.KhT = al.alloc([4, 128], BF16)
        b.BhT = al.alloc([4, 128], BF16)
        b.VT = al.alloc([4, 128], BF16)
        b.VT2 = al.alloc([4, 2, 128], BF16)
        return b

    def alloc_state(al):
        s_ = BB()
        s_.wtot = al.alloc([8], F32)
        s_.ST = al.alloc([128], F32)
        s_.STb = al.alloc([128], BF16)
        s_.XT = al.alloc([128], BF16)
        s_.UT = al.alloc([128], BF16)
        s_.UT2 = al.alloc([2, 128], BF16)
        s_.stmp = al.alloc([128], F32)
        return s_

    r_b = xa.alloc([T], BF16)
    k_b = xa.alloc([T], BF16)
    v_b = xa.alloc([T], BF16)
    kkn = xa.alloc([T], BF16)
    Yacc = xa.alloc([T], F32)
    B0 = alloc_bundle(xa)
    St = [alloc_state(xa), alloc_state(xa)]
    fA1 = xa.alloc([512], F32)
    fL1 = xa.alloc([8, 64], F32)
    fP1 = xa.alloc([8, 64], F32)
    Mm, NT, Ab, ATb = B0.Mm, B0.NT, B0.Ab, B0.ATb
    psT = [ps.at(6 * 2048, [4, 128], BF16), ps.at(7 * 2048, [4, 128], BF16)]
    P.memset("pool", B0.VT2[:], 0.0)
    for s_ in St:
        P.memset("pool", s_.UT2[:], 0.0)
    nT = {"n": 0}

    def tposes(src, dsts):
        pt = psT[nT["n"] % 2]
        nT["n"] += 1
        for j in range(4):
            P.transpose(pt[:, j, :], src[:, j * 128:(j + 1) * 128], K.ident[:])
        for eng, dv, sf in dsts:
            P.copy(eng, dv, sf(pt))

    def flat(b):
        return V(b.full.rearrange("p a b -> p (a b)"), b[:].rect)

    for fc in range(K.cfg.get("rw_fcs", 4)):
        mps = sb.mark()
        alloc_proj_sets()
        proj_shift(768 + fc * 128, 128, fc, lambda t: P.copy("act", r_b[:], t[:]))
        m1 = sb.mark()
        kk32 = sb.at(Mm.byte0, [T], F32)
        sqb = [sb.at(NT.byte0, [512], BF16), sb.at(NT.byte0 + 1024, [512], BF16)]
        rsb = [sb.at(Ab.byte0, [512], F32), sb.at(ATb.byte0, [512], F32)]

        def kdst(t):
            P.copy("act", k_b[:], t[:])
            P.ts("dve", kk32[:], t[:], K.vcol("k_k", fc), ALU.mult)
        proj_shift(768 + 512 + fc * 128, 128, 4 + fc, kdst)
        for tb in range(4):
            ts_ = slice(tb * 512, (tb + 1) * 512)
            pb = bank[2 + tb % 2]
            P.actf(sqb[tb % 2][:], kk32[:, ts_], AF.Square)
            P.mm(pb[:], K.bd[:], sqb[tb % 2][:])
            P.ts("dve", rsb[tb % 2][:], pb[:], 1e-24, ALU.max)
            P.actf(rsb[tb % 2][:], rsb[tb % 2][:], AF.Sqrt)
            P.recip(rsb[tb % 2][:], rsb[tb % 2][:])
            P.tt("dve", kkn[:, ts_], kk32[:, ts_], rsb[tb % 2][:], ALU.mult)
        sb.release(m1)
        proj_shift(768 + 1024 + fc * 128, 128, 8 + fc, lambda t: P.copy("act", v_b[:], t[:]))
        sb.release(mps)

        mF = sb.mark()
        fA = sb.alloc([512], F32)
        fL = sb.alloc([8, 64], F32)
        fP = sb.alloc([8, 64], F32)
        fQ = sb.alloc([8, 64], F32)
        fE = [sb.alloc([512], F32) for _ in range(2)]
        fKd = sb.alloc([512], F32)
        fBb = sb.alloc([512], F32)
        al_l = sb.alloc([512], BF16)
        tw_l = sb.alloc([512], BF16)
        B1 = alloc_bundle(sb)
        P.memset("pool", B1.VT2[:], 0.0)
        BBs = [B0, B1]
        for d in range(2):
            P.memset("dve", St[d].ST[:], 0.0)
            P.memset("dve", St[d].STb[:], 0.0)

        def prep_elem(d, tb, fA0=fA, fL0=fL, fP0=fP):
            B, S = BBs[d], St[d]
            fA, fL, fP = (fA0, fL0, fP0) if d == 0 else (fA1, fL1, fP1)
            dsl = slice(d * 64, d * 64 + 64)
            if d == 0:
                tsl = slice(tb * 512, (tb + 1) * 512)
                S_ = lambda b, q=slice(None): b[q, tsl]
            else:
                tsl = slice((3 - tb) * 512, (4 - tb) * 512)
                S_ = lambda b, q=slice(None): rev(b[q, tsl])
            pa, pw = bank[0], bank[1]
            P.copy("dve", al_l[dsl], S_(al, dsl))
            P.copy("dve", tw_l[dsl], S_(tw, dsl))
            P.mm(pa[:], a_up[dsl, fc * 128:(fc + 1) * 128], al_l[dsl])
            P.mm(pw[:], w_up[dsl, fc * 128:(fc + 1) * 128], tw_l[dsl])
            P.actf(fA[:], pa[:], AF.Sigmoid, bias=K.vcol("a0", d * 4 + fc))
            P.actf(flat(fL), pw[:], AF.Sigmoid, bias=K.vcol("w0", d * 4 + fc))
            for c in range(8):
                P.op("dve", lambda e, c=c: K.nc.vector.tensor_tensor_scan(out=fP[:, c, :].ap, data0=fL[:, c, :].ap, data1=fL[:, c, :].ap,
                                                                            initial=0.0, op0=ALU.add, op1=ALU.bypass),
                     [fL[:, c, :]], [fP[:, c, :]])
            tot_b = V(fP.full[:, :, 63:64].broadcast_to([128, 8, 64]), fP[:].rect)
            P.tt("dve", fQ[:], tot_b, fP[:], ALU.subtract)
            P.tt("dve", fL[:], fP[:], fL[:], ALU.subtract)
            P.actf(S.wtot[:], V(fP.full[:, :, 63], fP[:].rect), AF.Exp, scale=-C_DEC)
            P.ts("dve", fKd[:], fA[:], K.vcol("k_a", fc), ALU.mult, omka[:, fc:fc + 1], ALU.add)
            P.tt("dve", fKd[:], fKd[:], S_(k_b), ALU.mult)
            P.tt("dve", fBb[:], fA[:], S_(kkn), ALU.mult)
            E0, E1 = fE
            P.actf(E0[:], flat(fL), AF.Exp, scale=-C_DEC)
            P.stt(B.ZR[:, 0, :], S_(kkn), -1.0, E0[:], ALU.mult, ALU.mult)
            P.actf(E1[:], flat(fP), AF.Exp, scale=-C_DEC)
            P.tt("dve", B.ZR[:, 1, :], S_(r_b), E1[:], ALU.mult)
            P.actf(E0[:], flat(fP), AF.Exp, scale=C_DEC)
            P.tt("dve", B.Kt[:], fKd[:], E0[:], ALU.mult)
            P.tt("dve", B.Bt[:], fBb[:], E0[:], ALU.mult)
            P.actf(E1[:], flat(fQ), AF.Exp, scale=-C_DEC)
            P.tt("dve", B.Kh[:], fKd[:], E1[:], ALU.mult)
            P.tt("dve", B.Bh[:], fBb[:], E1[:], ALU.mult)
            P.copy("dve", B.Vt[:], S_(v_b))
            tposes(B.Kh, [("act", B.KhT[:], lambda pt: pt[:])])
            tposes(B.Bh, [("act", B.BhT[:], lambda pt: pt[:])])
            tposes(B.Vt, [("act", B.VT[:], lambda pt: pt[:]),
                          ("act", V(B.VT2.full[:, :, 0, 0:64], B.VT2[:].rect), lambda pt: V(pt.full[:, :, 0:64], pt[:].rect)),
                          ("act", V(B.VT2.full[:, :, 1, 64:128], B.VT2[:].rect), lambda pt: V(pt.full[:, :, 64:128], pt[:].rect))])

        def m_stage():
            for pb_ in range(4):
                bs = slice(pb_ * 128, (pb_ + 1) * 128)
                for hd in range(2):
                    pr = slice(hd * 64, hd * 64 + 64)
                    i8 = pb_ * 2 + hd
                    for d in range(2):
                        B = BBs[d]
                        pm_ = bank[2 + d]
                        pn = bank[4 + d]
                        rhs = V(B.ZR.full[pr, :, bs], B.ZR[pr].rect)
                        P.mm(pm_[:, 0:256], B.Bt[pr, bs], rhs)
                        P.mm(pm_[:, 256:512], B.Kt[pr, bs], rhs)
                        P.mm(pn[:, 0:128], B.ZR[pr, 0, bs], B.Bt[pr, bs])
                    for d in range(2):
                        B = BBs[d]
                        P.tt("dve", V(B.Mm.full[:, i8, :, :].rearrange("p a b -> p (a b)"), B.Mm[:, i8].rect), bank[2 + d][:],
                             V(m4.full.rearrange("p a b -> p (a b)"), m4[:].rect), ALU.mult)
                        P.tt("dve", B.NT[:, i8, :], bank[4 + d][:, 0:128], mT[:], ALU.mult)

        def t_stage():
            for i8 in range(8):
                for d in range(2):
                    B = BBs[d]
                    P.copy("act", B.Ab[:, i8, :], B.Mm[:, i8, 0, :])
                    P.copy("act", B.ATb[:, i8, :], B.NT[:, i8, :])
                    P.tt("dve", B.Pm[:, i8, :], B.Mm[:, i8, 0, :], K.ident[:], ALU.add)
            for lvl in range(5):
                for hf in range(2):
                    q4 = slice(hf * 4, hf * 4 + 4)
                    for d in range(2):
                        B = BBs[d]
                        for i4 in range(4):
                            i8 = hf * 4 + i4
                            cs4 = slice(i4 * 128, (i4 + 1) * 128)
                            P.mm(bank[2 * d][:, cs4], B.ATb[:, i8, :], B.Ab[:, i8, :])
                            P.mm(bank[2 * d + 1][:, cs4], B.Ab[:, i8, :], B.ATb[:, i8, :])
                    for d in range(2):
                        B = BBs[d]
                        P.copy("act", flat4(B.Ab, q4), bank[2 * d][:])
                        P.copy("dve", flat4(B.ATb, q4), bank[2 * d + 1][:])
                    for d in range(2):
                        B = BBs[d]
                        for i4 in range(4):
                            i8 = hf * 4 + i4
                            P.mm(bank[4 + d][:, i4 * 128:(i4 + 1) * 128], B.ATb[:, i8, :], B.Pm[:, i8, :])
                    for d in range(2):
                        B = BBs[d]
                        pv = flat4(B.Pm, q4)
                        P.tt("dve", pv, pv, bank[4 + d][:], ALU.add)

        def flat4(buf, q4):
            return V(buf.full[:, q4, :].rearrange("p a b -> p (a b)"), buf[:, q4].rect)

        def chain_stage(tb):
            CB = [(bank[6], bank[7], bank[0], bank[1]), (bank[2], bank[3], bank[4], bank[5])]
            for c in range(8):
                pb_, e = c // 2, c % 2
                js = slice(e * 64, e * 64 + 64)
                cs = slice(c * 64, (c + 1) * 64)
                for d in range(2):
                    B, S = BBs[d], St[d]
                    px = CB[d][0]
                    P.mm(px[0:64, 0:128], B.ZR[:, 0, cs], S.STb[:], start=True, stop=False)
                    for hd in range(2):
                        i8 = pb_ * 2 + hd
                        hs_ = slice(hd * 64, hd * 64 + 64)
                        P.mm(px[0:64, hs_], B.Mm[js, i8, 2, js], B.VT[js, pb_, hs_], start=False, stop=(hd == 1))
                for d in range(2):
                    P.copy("act", St[d].XT[js], CB[d][0][0:64, 0:128])
                for d in range(2):
                    B, S = BBs[d], St[d]
                    pu = CB[d][1]
                    for hd in range(2):
                        i8 = pb_ * 2 + hd
                        hs_ = slice(hd * 64, hd * 64 + 64)
                        P.mm(pu[0:64, hs_], B.Pm[js, i8, js], S.XT[js, hs_])
                for d in range(2):
                    S = St[d]
                    pu = CB[d][1]
                    P.copy("act", S.UT[js], pu[0:64, 0:128])
                    P.copy("act", V(S.UT2.full[js, 0, 0:64], S.UT2[js].rect), pu[0:64, 0:64])
                    P.copy("act", V(S.UT2.full[js, 1, 64:128], S.UT2[js].rect), pu[0:64, 64:128])
                for d in range(2):
                    B, S = BBs[d], St[d]
                    py, pss = CB[d][2], CB[d][3]
                    P.mm(py[:, 0:64], S.STb[:], B.ZR[:, 1, cs], start=True, stop=False)
                    for hd in range(2):
                        i8 = pb_ * 2 + hd
                        P.mm(py[:, 0:64], S.UT2[js, hd, :], B.Mm[js, i8, 1, js], start=False, stop=False)
                        P.mm(py[:, 0:64], B.VT2[js, pb_, hd, :], B.Mm[js, i8, 3, js], start=False, stop=(hd == 1))
                    P.mm(pss[:, 0:128], B.BhT[js, pb_, :], S.UT[js], start=True, stop=False)
                    P.mm(pss[:, 0:128], B.KhT[js, pb_, :], B.VT[js, pb_, :], start=False, stop=True)
                for d in range(2):
                    S = St[d]
                    pss = CB[d][3]
                    P.tt("dve", S.stmp[:], pss[:, 0:128], bd32[:], ALU.mult)
                    P.stt(S.ST[:], S.ST[:], S.wtot[:, c:c + 1], S.stmp[:], ALU.mult, ALU.add)
                    P.copy("act", S.STb[:], S.ST[:])
                yv0 = Yacc[:, tb * 512 + c * 64:tb * 512 + (c + 1) * 64]
                P.tt("dve", yv0, yv0, CB[0][2][:, 0:64], ALU.add)
                t0 = T - (tb * 512 + (c + 1) * 64)
                yv1 = rev(Yacc[:, t0:t0 + 64])
                P.tt("dve", yv1, yv1, CB[1][2][:, 0:64], ALU.add)

        P.memset("pool", Yacc[:], 0.0)
        v4s = K.cfg.get("v4_stage", 4)
        for tb in range(4):
            for d in range(2):
                prep_elem(d, tb)
            if v4s >= 2:
                m_stage()
            if v4s >= 3:
                t_stage()
            if v4s >= 4:
                chain_stage(tb)
        sb.release(mF)
        m1 = sb.mark()
        pts = [[sb.alloc([512], F32) for _ in range(5)] for _ in range(2)]
        for tb in range(4):
            ts_ = slice(tb * 512, (tb + 1) * 512)
            cen, sq, rs, rk, bon = pts[tb % 2]
            p1, p2, p3, p4 = (bank[2], bank[3], bank[4], bank[5]) if tb % 2 == 0 else (bank[6], bank[7], bank[0], bank[1])
            P.mm(p1[:], bd32[:], Yacc[:, ts_])
            P.stt(cen[:], p1[:], -1.0 / 64, Yacc[:, ts_], ALU.mult, ALU.add)
            P.actf(sq[:], cen[:], AF.Square)
            P.mm(p2[:], bd32[:], sq[:])
            P.actf(rs[:], p2[:], AF.Sqrt, bias=K.gnc[:], scale=1.0 / 64)
            P.recip(rs[:], rs[:])
            P.tt("dve", cen[:], cen[:], rs[:], ALU.mult)
            P.ts("dve", cen[:], cen[:], K.vcol("ln_g", fc), ALU.mult, K.vcol("ln_b", fc), ALU.add)
            P.stt(rk[:], r_b[:, ts_], K.vcol("r_k", fc), k_b[:, ts_], ALU.mult, ALU.mult)
            P.mm(p3[:], bd32[:], rk[:])
            P.tt("dve", bon[:], p3[:], v_b[:, ts_], ALU.mult)
            P.tt("pool", cen[:], cen[:], bon[:], ALU.add)
            P.mm(p4[:], g_up[:, 0, fc * 128:(fc + 1) * 128], sg[:, 0, ts_], start=True, stop=False)
            P.mm(p4[:], g_up[0:32, 1, fc * 128:(fc + 1) * 128], sg[0:32, 1, ts_], start=False, stop=True)
            P.tt("dve", K.yR[:, fc, ts_], cen[:], p4[:], ALU.mult)
        sb.release(m1)
    sb.release(m)
    for kc in range(8):
        P.dma("sp" if kc % 2 == 0 else "act", X[:, kc, :], xsp[:, kc, :])


TWO_PI = 6.283185307179586
MAGIC = 12582912.0


def mixer1_phase(K):
    P, sb, bank, X, hT, cst, ps = K.P, K.sb, K.bank, K.X, K.hT, K.cst, K.ps
    cfg = K.cfg
    hy_w_in = K.inp("hy_w_in", [1, D, 3 * D])
    hy_w_out = K.inp("hy_w_out", [1, D, D])
    f_w1 = K.inp("hy_f_w1", [1, 33, 64])
    f_w2 = K.inp("hy_f_w2", [1, 64, 64])
    f_w3 = K.inp("hy_f_w3", [1, 64, 64])
    f_out = K.inp("hy_f_out", [1, 64, 4 * D])
    skip_d = K.inp("skip_bc", [128, 2 * D])
    hspec = P.dram_scratch("hspec", [4, 128, 32 * 512], BF16)
    SC = 2.0 / (2 * T)
    m = sb.mark()
    hid3b = sb.alloc([T], BF16)
    woutb = sb.alloc([4 * D], BF16)
    skip2 = sb.alloc([2 * D], F32)
    m0 = sb.alloc([1], F32)
    P.dma("sp", skip2[:], skip_d[:])
    P.ts("dve", skip2[:], skip2[:], SC, ALU.mult)
    mw = sb.mark()
    wf = sb.alloc([4 * D], F32)
    P.dma("sp", wf[0:64, :], P.dview(f_out, f_out.full[0]))
    for o_ in range(2):
        w0v = wf[0:64, o_ * 2048:o_ * 2048 + 1024]
        w1v = wf[0:64, o_ * 2048 + 1024:o_ * 2048 + 2048]
        P.tt("dve", woutb[0:64, o_ * 2048:o_ * 2048 + 1024], w0v, w1v, ALU.add)
        P.tt("dve", woutb[0:64, o_ * 2048 + 1024:o_ * 2048 + 2048], w0v, w1v, ALU.subtract)
    sb.release(mw)
    P.memset("dve", m0[:], 1.0)
    P.memset("dve", m0[0:1], 0.0)
    m1 = sb.mark()
    feats = sb.alloc([T], F32)
    hA = sb.alloc([T], F32)
    w1 = sb.alloc([64], F32)
    w2 = sb.alloc([64], F32)
    w3 = sb.alloc([64], F32)
    frb = sb.alloc([3], F32)
    arg = [sb.alloc([512], F32) for _ in range(2)]
    vv = [sb.alloc([512], F32) for _ in range(2)]
    P.dma("sp", feats[0:33, :], cst["featsT"][:])
    P.dma("act", w1[0:33, :], P.dview(f_w1, f_w1.full[0]))
    P.dma("act", w2[0:64, :], P.dview(f_w2, f_w2.full[0]))
    P.dma("act", w3[0:64, :], P.dview(f_w3, f_w3.full[0]))
    fr = K.vcol("f_fr", 0, 1, 64)
    for i, nm in enumerate(("f_b1", "f_b2", "f_b3")):
        P.tt("dve", frb[0:64, i:i + 1], K.vcol(nm, 0, 1, 64), fr, ALU.mult)
    n = 0
    for li, (w, kk_) in enumerate(((w1, 33), (w2, 64), (w3, 64))):
        for tb in range(4):
            ts_ = slice(tb * 512, (tb + 1) * 512)
            pb = bank[n % 2]
            a, v = arg[n % 2], vv[n % 2]
            n += 1
            src = feats if li == 0 else hA
            P.mm(pb[0:64, :], w[0:kk_, :], src[0:kk_, ts_])
            P.ts("dve", a[0:64], pb[0:64, :], fr, ALU.mult, frb[0:64, li:li + 1], ALU.add)
            P.ts("dve", v[0:64], a[0:64], 1.0 / TWO_PI, ALU.mult, MAGIC, ALU.add)
            P.ts("dve", v[0:64], v[0:64], MAGIC, ALU.subtract)
            P.stt(a[0:64], v[0:64], -TWO_PI, a[0:64], ALU.mult, ALU.add)
            P.ts("dve", a[0:64], a[0:64], 3.14159, ALU.min, -3.14159, ALU.max)
            if li < 2:
                P.actf(hA[0:64, ts_], a[0:64], AF.Sin)
            else:
                P.actf(hid3b[0:64, ts_], a[0:64], AF.Sin)
    sb.release(m1)

    rr = [0]

    def dq():
        rr[0] += 1
        return "sp" if rr[0] % 2 == 0 else "act"

    def flat3(b):
        return V(b.full.rearrange("p a b -> p (a b)"), b[:].rect)

    def fwd_dft(rhsC, rhsS, epilogue, tiles):
        NB = len(tiles)

        def load(fc):
            cE, cO, sE, sO = tiles[fc % NB]
            P.dma("sp", flat3(cE), cst["Ctf"][fc, 0])
            P.dma("act", flat3(sE), cst["Stf"][fc, 0])
            P.dma("sp", flat3(cO), cst["Ctf"][fc, 1])
            P.dma("act", flat3(sO), cst["Stf"][fc, 1])

        for fc in range(min(NB - 1, 8)):
            load(fc)
        for fc in range(8):
            if fc + NB - 1 < 8:
                load(fc + NB - 1)
            cE, cO, sE, sO = tiles[fc % NB]
            bk = [bank[4 * (fc % 2) + j] for j in range(4)]
            for t8 in range(8):
                P.mm(bk[0][:], cE[:, t8, :], rhsC[:, t8, :], start=(t8 == 0), stop=(t8 == 7))
                P.mm(bk[2][:], sE[:, t8, :], rhsS[:, t8, :], start=(t8 == 0), stop=(t8 == 7))
            for t8 in range(8):
                P.mm(bk[1][:], cO[:, t8, :], rhsC[:, 8 + t8, :], start=(t8 == 0), stop=(t8 == 7))
                P.mm(bk[3][:], sO[:, t8, :], rhsS[:, 8 + t8, :], start=(t8 == 0), stop=(t8 == 7))
            epilogue(fc, bk[0], bk[1], bk[2], bk[3])

    m1 = sb.mark()
    hs = sb.alloc([16, 512], BF16)
    hd = sb.alloc([16, 512], BF16)
    Hst = [sb.alloc([512], BF16) for _ in range(8)]
    hn = [0]
    win = [sb.alloc([512], F32) for _ in range(2)]
    f0 = [sb.alloc([512], F32) for _ in range(2)]
    f1 = [sb.alloc([512], F32) for _ in range(2)]
    etmp = [sb.alloc([512], F32) for _ in range(4)]
    tiles = [tuple(sb.alloc([8, 128], BF16) for _ in range(4)) for _ in range(2)]
    n = 0
    for o in range(2):
        for hh in range(2):
            for sc in range(16):
                w_ = win[n % 2]
                a0, a1 = f0[n % 2], f1[n % 2]
                pb0, pb1 = bank[4 + (n % 2) * 2], bank[5 + (n % 2) * 2]
                n += 1
                P.dma(dq(), w_[:], cst["window"][sc * 128:(sc + 1) * 128, hh * 512:(hh + 1) * 512])
                c0 = o * 2048 + hh * 512
                P.mm(pb0[:], hid3b[0:64, sc * 128:(sc + 1) * 128], woutb[0:64, c0:c0 + 512])
                P.mm(pb1[:], hid3b[0:64, sc * 128:(sc + 1) * 128], woutb[0:64, c0 + 1024:c0 + 1536])
                P.tt("dve", hs[:, sc, :], pb0[:], w_[:], ALU.mult)
                P.tt("dve", hd[:, sc, :], pb1[:], w_[:], ALU.mult)
                if sc == 0:
                    P.tt("dve", a0[0:1], hs[0:1, 0, :], hd[0:1, 0, :], ALU.add)
                    P.ts("dve", hs[0:1, 0, :], a0[0:1], 0.5, ALU.mult)
                    P.ts("dve", hd[0:1, 0, :], a0[0:1], 0.5, ALU.mult)

            def epi(fc, eR, oR_, eI, oI_, o=o, hh=hh):
                oR, oI, oRp, oRm = etmp
                sk = skip2[:, o * D + hh * 512:o * D + (hh + 1) * 512]
                hb = [Hst[(hn[0] + i) % 8] for i in range(4)]
                hn[0] += 4
                P.actf(oR[:], oR_[:], AF.Identity, scale=SC)
                P.actf(oI[:], oI_[:], AF.Identity, scale=SC)
                P.tt("dve", oRp[:], oR[:], sk, ALU.add)
                P.tt("pool", oRm[:], oR[:], sk, ALU.subtract)
                P.stt(hb[0][:], eR[:], SC, oRp[:], ALU.mult, ALU.add)
                P.stt(hb[1][:], eI[:], SC, oI[:], ALU.mult, ALU.add)
                P.stt(hb[2][:], eR[:], SC, oRm[:], ALU.mult, ALU.subtract)
                P.stt(hb[3][:], eI[:], -SC, oI[:], ALU.mult, ALU.add)
                for sl in range(4):
                    P.dma("sp", hspec[o * 2 + hh, :, (fc * 4 + sl) * 512:(fc * 4 + sl + 1) * 512], hb[sl][:])

            fwd_dft(hs, hd, epi, tiles)
    sb.release(m1)
    sb.release(m)

    m = sb.mark()
    xsp2 = P.dram_scratch("xspill2", [128, 8, T], F32)
    for kc in range(8):
        P.dma("sp" if kc % 2 == 0 else "act", xsp2[:, kc, :], X[:, kc, :])
    xa = Sub(sb, X.byte0, 8 * T * 4)
    tq2 = [xa.alloc([512], F32) for _ in range(12)]
    tq1x = [xa.alloc([512], F32) for _ in range(4)]
    xtiles = [tuple(xa.alloc([8, 128], BF16) for _ in range(4)) for _ in range(2)]
    xitiles = [(xa.alloc([4, 512], BF16), xa.alloc([4, 512], BF16)) for _ in range(2)]
    u = sb.alloc([16, 512], BF16)
    xgs = [sb.alloc([4, T], BF16) for _ in range(2)]
    cur = {"xg": xgs[0]}
    psT = [ps.at(6 * 2048, [4, 128], BF16), ps.at(7 * 2048, [4, 128], BF16)]
    cnt = {"n": 0, "nt": 0}

    def transposes_to_u(src_fn, dc, eng):
        for g4 in range(4):
            pt = psT[cnt["nt"] % 2]
            cnt["nt"] += 1
            for j in range(4):
                tc = g4 * 4 + j
                P.transpose(pt[:, j, :], src_fn(tc), K.ident[:])
            P.copy(eng, u[:, g4 * 4:(g4 + 1) * 4, dc * 128:(dc + 1) * 128], pt[:])

    def nat2(v):
        return V(v.ap.rearrange("p (n r) -> p n r", r=2), v.rect)

    def perm2(v):
        return V(v.ap.rearrange("p (r n) -> p n r", r=2), v.rect)

    def proj(hh, parts):
        m1 = sb.mark()
        HN = T // 2
        pe_ = sb.alloc([HN + 2], F32)
        po_ = sb.alloc([HN + 2], F32)
        tmp = sb.alloc([T], F32)
        vch = [sb.alloc([T], BF16) for _ in range(2)]
        wt = [sb.alloc([8, 128], BF16) for _ in range(3)]
        for b_ in (pe_, po_):
            P.memset("pool", b_[:, 0:1], 0.0)
            P.memset("pool", b_[:, HN + 1:HN + 2], 0.0)
        for part in parts:
            for dc in range(4):
                col0 = part * 1024 + hh * 512 + dc * 128
                vc = col0 // 128
                n = cnt["n"]
                w = wt[n % 3]
                P.dma("pool", w[:], K.wview(hy_w_in, 0, 0, 8, col0, 128))
                for tb in range(4):
                    pb = bank[4 + (cnt["n"] % 2)]
                    cnt["n"] += 1
                    for kc in range(8):
                        P.mm(pb[:], w[:, kc, :], hT[:, kc, tb * 512:(tb + 1) * 512], start=(kc == 0), stop=(kc == 7))
                    P.copy("act", pe_[:, 1 + tb * 256:1 + (tb + 1) * 256], V(pb.full[:, 0:512:2], pb[:].rect))
                    P.copy("act", po_[:, 1 + tb * 256:1 + (tb + 1) * 256], V(pb.full[:, 1:512:2], pb[:].rect))
                dst = vch[dc % 2] if part == 0 else None
                d_e = (dst[:, 0:HN] if dst is not None else cur["xg"][:, dc, 0:HN])
                d_o = (dst[:, HN:T] if dst is not None else cur["xg"][:, dc, HN:T])
                cb, c0_, c1_, c2_ = K.vcol("cb", vc), K.vcol("cw0", vc), K.vcol("cw1", vc), K.vcol("cw2", vc)
                te, to = tmp[:, 0:HN], tmp[:, HN:T]
                P.actf(te, pe_[:, 1:HN + 1], AF.Identity, bias=cb, scale=c1_)
                P.stt(te, po_[:, 0:HN], c0_, te, ALU.mult, ALU.add)
                P.stt(d_e, po_[:, 1:HN + 1], c2_, te, ALU.mult, ALU.add)
                P.actf(to, po_[:, 1:HN + 1], AF.Identity, bias=cb, scale=c1_)
                P.stt(to, pe_[:, 1:HN + 1], c0_, to, ALU.mult, ALU.add)
                P.stt(d_o, pe_[:, 2:HN + 2], c2_, to, ALU.mult, ALU.add)
                if part == 0:
                    vb = vch[dc % 2]
                    transposes_to_u(lambda tc, vb=vb: vb[:, tc * 128:(tc + 1) * 128], dc, "dve")
        sb.release(m1)

    def conv(hh, o):
        m1 = sb.mark()
        HY = sb.alloc([8, 4, 512], BF16)
        tq = [sb.alloc([512], F32) for _ in range(8)] + tq1x
        mt = sb.mark()
        tiles = xtiles
        xg = cur["xg"]
        P.dma("sp", HY[:], hspec[o * 2 + hh])

        def epi(fc, eR, oR_, eI, oI_):
            xr, xrp, xi, xip, a1, a2, b1, b2, c1, c2, d1, d2 = tq if fc % 2 == 0 else tq2
            P.copy("act", a1[:], oR_[:])
            P.copy("act", a2[:], oI_[:])
            P.tt("dve", xr[:], eR[:], a1[:], ALU.add)
            P.tt("dve", xrp[:], eR[:], a1[:], ALU.subtract)
            P.tt("dve", xi[:], eI[:], a2[:], ALU.add)
            P.stt(xip[:], eI[:], -1.0, a2[:], ALU.mult, ALU.add)
            Hre, Him, Hrp, Hip = (HY[:, fc, s_, :] for s_ in range(4))
            P.tt("dve", a1[:], xr[:], Hre, ALU.mult)
            P.tt("dve", a2[:], xi[:], Him, ALU.mult)
            P.tt("dve", b1[:], xr[:], Him, ALU.mult)
            P.tt("dve", b2[:], xi[:], Hre, ALU.mult)
            P.tt("dve", c1[:], xrp[:], Hrp, ALU.mult)
            P.tt("dve", c2[:], xip[:], Hip, ALU.mult)
            P.tt("dve", d1[:], xrp[:], Hip, ALU.mult)
            P.tt("dve", d2[:], xip[:], Hrp, ALU.mult)
            P.tt("pool", a1[:], a1[:], a2[:], ALU.subtract)
            P.tt("pool", b1[:], b1[:], b2[:], ALU.add)
            P.tt("pool", c1[:], c1[:], c2[:], ALU.subtract)
            P.tt("pool", d1[:], d1[:], d2[:], ALU.add)
            P.tt("pool", HY[:, fc, 0, :], a1[:], c1[:], ALU.add)
            P.tt("pool", HY[:, fc, 1, :], b1[:], d1[:], ALU.subtract)
            P.tt("dve", HY[:, fc, 2, :], a1[:], c1[:], ALU.subtract)
            P.tt("dve", HY[:, fc, 3, :], b1[:], d1[:], ALU.add)

        fwd_dft(u, u, epi, tiles)
        sb.release(mt)
        itiles = xitiles
        isteps = [(tb, fg) for tb in range(4) for fg in range(2)]

        def iload(k):
            tb, fg = isteps[k]
            ct, st = itiles[k % 2]
            P.dma("sp", flat3(ct), cst["Cft"][tb, fg])
            P.dma("act", flat3(st), cst["Sft"][tb, fg])

        iload(0)
        for k, (tb, fg) in enumerate(isteps):
            if k + 1 < len(isteps):
                iload(k + 1)
            ts_ = slice(tb * 512, (tb + 1) * 512)
            sR = 0 if tb < 2 else 2
            pbk = [bank[4 * (tb % 2) + dc] for dc in range(4)]
            ct, st = itiles[k % 2]
            for f4 in range(4):
                fc = fg * 4 + f4
                for dc in range(4):
                    P.mm(pbk[dc][:], HY[:, fc, sR, dc * 128:(dc + 1) * 128], ct[:, f4, :], start=(fc == 0), stop=False)
                for dc in range(4):
                    P.mm(pbk[dc][:], HY[:, fc, sR + 1, dc * 128:(dc + 1) * 128], st[:, f4, :], start=False, stop=(fc == 7))
            if fg == 1:
                for dc in range(4):
                    P.tt("dve", xg[:, dc, ts_], xg[:, dc, ts_], pbk[dc][:], ALU.mult)
        sb.release(m1)

    for hh in range(2):
        cur["xg"] = xgs[hh]
        proj(hh, [0, 1])
        conv(hh, 0)
        for dc in range(4):
            transposes_to_u(lambda tc, dc=dc, xg=xgs[hh]: xg[:, dc, tc * 128:(tc + 1) * 128], dc, "act")
        proj(hh, [2])
        conv(hh, 1)
    m1 = sb.mark()
    wo = [sb.alloc([8, 512], BF16) for _ in range(2)]
    for og in range(2):
        P.dma("pool", wo[og][:], K.wview(hy_w_out, 0, 0, 8, og * 512, 512))
    for kc in range(8):
        P.dma("sp" if kc % 2 == 0 else "act", X[:, kc, :], xsp2[:, kc, :])
    n = 0
    for og in range(2):
        for oc in range(4):
            for tb in range(4):
                po = bank[4 + (n % 4)]
                n += 1
                for kc in range(8):
                    P.mm(po[:], wo[og][:, kc, oc * 128:(oc + 1) * 128], xgs[kc // 4][:, kc % 4, tb * 512:(tb + 1) * 512],
                         start=(kc == 0), stop=(kc == 7))
                par, n0 = tb // 2, (tb % 2) * 512
                xrow = X[:, og * 4 + oc, :]
                xs = V(xrow.ap.rearrange("p (n r) -> p n r", r=2)[:, n0:n0 + 512, par], xrow.rect)
                P.stt(xs, po[:], K.mod[:, 48 + 16 + og * 4 + oc:48 + 17 + og * 4 + oc], xs, ALU.mult, ALU.add)
    sb.release(m1)
    sb.release(m)


_CACHE = {}


def _get_consts():
    if "c" not in _CACHE:
        _CACHE["c"] = host_consts()
    return _CACHE["c"]


def make_in_maps(inp):
    consts = _get_consts()
    maps = []
    vcols = None
    nvec = None
    shared = {}
    for k in ("ada_w", "ffn_w1", "ffn_w3", "ffn_w2", "mix_w_in", "mix_w_out", "hy_w_in", "hy_w_out",
              "hy_f_w1", "hy_f_w2", "hy_f_w3", "hy_f_out", "rwkv_w_up", "rwkv_a_up", "rwkv_g_up"):
        shared[k] = np.ascontiguousarray(inp[k])
    for b in range(NB):
        vp = vec_layout(inp, b)
        vecs = vp.build()
        vcols, nvec = vp.cols, vp.n
        m = {"xT": np.ascontiguousarray(inp["x"][b].T), "vecs": vecs,
             "skip_bc": np.ascontiguousarray(np.broadcast_to(inp["hy_skip"][0].reshape(1, 2 * D), (128, 2 * D)))}
        for k, v in consts.items():
            m["c_" + k] = v
        m.update(shared)
        maps.append(m)
    return maps, vcols, nvec


def kernel(**inputs):
    cfg = inputs.pop("_cfg", {})
    inp = {k: np.asarray(v) for k, v in inputs.items()}
    maps, vcols, nvec = make_in_maps(inp)
    K = build_program(vcols, nvec, cfg)
    names = set(K.din.keys())
    maps = [{k: v for k, v in m.items() if k in names} for m in maps]
    res = run_bass_kernel_spmd(K.nc, maps, core_ids=list(range(NB)))
    out = np.stack([np.ascontiguousarray(r["outT"].T) for r in res.results], 0)
    return out.astype(np.float32)
```

```python
import numpy as np
from contextlib import ExitStack
import concourse.bass as bass
import concourse.mybir as mybir
from concourse.bass_utils import run_bass_kernel_spmd

F32 = mybir.dt.float32
BF16 = mybir.dt.bfloat16
ALU = mybir.AluOpType
AF = mybir.ActivationFunctionType
DSZ = {F32: 4, BF16: 2}


class V:
    __slots__ = ("ap", "rect")

    def __init__(self, ap, rect):
        self.ap = ap
        self.rect = rect


class Buf:
    def __init__(self, space, base_ap2d, p, shape, dtype, byte0):
        self.space = space
        self.p = p
        self.shape = tuple(shape)
        self.dtype = dtype
        self.byte0 = byte0
        self.esz = DSZ[dtype]
        n = int(np.prod(shape))
        if len(shape) == 1:
            self.full = base_ap2d
        else:
            names = " ".join("d%d" % i for i in range(len(shape)))
            kw = {"d%d" % i: int(s) for i, s in enumerate(shape)}
            self.full = base_ap2d.rearrange("p (%s) -> p %s" % (names, names), **kw)
        self.strides = [int(np.prod(shape[i + 1:])) for i in range(len(shape))]

    def __getitem__(self, idx):
        if not isinstance(idx, tuple):
            idx = (idx,)
        idx = list(idx) + [slice(None)] * (1 + len(self.shape) - len(idx))
        ps = idx[0]
        if isinstance(ps, int):
            ps = slice(ps, ps + 1)
        p0 = ps.start or 0
        p1 = self.p if ps.stop is None else ps.stop
        lo = 0
        hi = 0
        for i, (ix, n) in enumerate(zip(idx[1:], self.shape)):
            if isinstance(ix, int):
                a, b = ix, ix + 1
            else:
                a = ix.start or 0
                b = n if ix.stop is None else ix.stop
                if ix.step is not None and ix.step < 0:
                    a, b = 0, n
            lo += a * self.strides[i]
            hi += (b - 1) * self.strides[i]
        rect = (self.space, p0, p1, self.byte0 + lo * self.esz, self.byte0 + (hi + 1) * self.esz)
        ap = self.full[tuple([slice(p0, p1)] + idx[1:])]
        return V(ap, rect)


class Arena:
    def __init__(self, prog, name, nbytes, psum=False):
        self.prog = prog
        self.name = name
        self.nbytes = nbytes
        self.psum = psum
        nc = prog.nc
        if psum:
            self.t = prog.stack.enter_context(nc.psum_tensor(name, [128, nbytes // 4], F32))
        else:
            self.t = prog.stack.enter_context(nc.sbuf_tensor(name, [128, nbytes // 4], F32))
        self.top = 0

    def alloc(self, shape, dtype, p=128, align=4):
        esz = DSZ[dtype]
        n = int(np.prod(shape))
        nb = (n * esz + align - 1) // align * align
        off = (self.top + align - 1) // align * align
        assert off + nb <= self.nbytes, "arena %s overflow: %d + %d > %d" % (self.name, off, nb, self.nbytes)
        self.top = off + nb
        return self.at(off, shape, dtype, p)

    def at(self, off, shape, dtype, p=128):
        esz = DSZ[dtype]
        n = int(np.prod(shape))
        nb = (n * esz + 3) // 4 * 4
        ap = self.t[0:p, off // 4:(off + nb) // 4]
        if dtype != F32:
            ap = ap.bitcast(dtype)
            ap = ap[:, 0:n]
        return Buf(self.name, ap, p, shape, dtype, off)

    def mark(self):
        return self.top

    def release(self, m):
        self.top = m


class Op:
    __slots__ = ("eng", "fn", "deps", "dma", "sem", "semval", "signal", "id", "prewait")


ENGS = ("pe", "act", "dve", "pool", "sp")


class Prog:
    def __init__(self, nc, n_dma_sems=(("sp", 16), ("act", 6), ("pool", 10))):
        self.nc = nc
        self.stack = ExitStack()
        self.ops = []
        self.eng_ops = {e: [] for e in ENGS}
        self.acc = {}
        self.dma_pools = {}
        self.dma_rr = {}
        self.sem_total = {}
        self.esem = {}
        for e in ("pe", "act", "dve", "pool"):
            self.esem[e] = self.stack.enter_context(nc.semaphore("s_" + e))
        for e, n in n_dma_sems:
            self.dma_pools[e] = [self.stack.enter_context(nc.semaphore("d_%s%d" % (e, i))) for i in range(n)]
            self.dma_rr[e] = 0
        self.n_dram = 0

    def dram_in(self, name, shape, dtype=F32):
        t = self.nc.dram_tensor(name, list(shape), dtype, kind="ExternalInput")
        return self._dram_buf(t, name, shape, dtype)

    def dram_out(self, name, shape, dtype=F32):
        t = self.nc.dram_tensor(name, list(shape), dtype, kind="ExternalOutput")
        return self._dram_buf(t, name, shape, dtype)

    def dram_scratch(self, name, shape, dtype=F32):
        t = self.nc.dram_tensor(name, list(shape), dtype, kind="Internal")
        return self._dram_buf(t, name, shape, dtype)

    def _dram_buf(self, t, name, shape, dtype):
        b = Buf.__new__(Buf)
        b.space = "dram:" + name
        b.p = shape[0]
        b.shape = tuple(shape[1:])
        b.dtype = dtype
        b.byte0 = 0
        b.esz = DSZ[dtype]
        b.full = t.ap()
        b.strides = [int(np.prod(shape[i + 1:])) for i in range(1, len(shape))]
        b.handle = t
        return b

    def dview(self, buf, ap):
        return V(ap, (buf.space, 0, 1 << 30, 0, 1 << 40))

    def _deps(self, reads, writes, opid):
        deps = set()
        for views, is_w in ((reads, False), (writes, True)):
            for v in views:
                sp, p0, p1, b0, b1 = v.rect if isinstance(v, V) else v
                lst = self.acc.setdefault(sp, [])
                keep = []
                for a in lst:
                    ap0, ap1, ab0, ab1, aid, aw = a
                    ov = not (ap1 <= p0 or p1 <= ap0 or ab1 <= b0 or b1 <= ab0)
                    if ov and (aw or is_w) and aid != opid:
                        deps.add(aid)
                    contained = ov and ap0 >= p0 and ap1 <= p1 and ab0 >= b0 and ab1 <= b1
                    if is_w and contained and aid != opid:
                        continue
                    keep.append(a)
                keep.append((p0, p1, b0, b1, opid, is_w))
                self.acc[sp] = keep
        return deps

    def op(self, eng, fn, reads=(), writes=(), dma=False):
        o = Op()
        o.eng = eng
        o.fn = fn
        o.dma = dma
        o.id = len(self.ops)
        o.signal = False
        o.sem = None
        o.semval = 0
        o.prewait = None
        o.deps = self._deps(reads, writes, o.id)
        self.ops.append(o)
        self.eng_ops[eng].append(o)
        return o

    def dma(self, eng, out, in_, **kw):
        q = {"sp": self.nc.sync, "act": self.nc.scalar, "pool": self.nc.gpsimd}[eng]
        return self.op(eng, lambda e: q.dma_start(out=out.ap, in_=in_.ap, **kw), [in_], [out], dma=True)

    def mm(self, out, lhsT, rhs, start=True, stop=True, **kw):
        return self.op("pe", lambda e: self.nc.tensor.matmul(out.ap, lhsT.ap, rhs.ap, start=start, stop=stop, **kw),
                       [lhsT, rhs], [out])

    def transpose(self, out, in_, ident):
        return self.op("pe", lambda e: self.nc.tensor.transpose(out.ap, in_.ap, ident.ap), [in_, ident], [out])

    def _ce(self, eng):
        return {"act": self.nc.scalar, "dve": self.nc.vector, "pool": self.nc.gpsimd}[eng]

    def tt(self, eng, out, in0, in1, op):
        E = self._ce(eng)
        return self.op(eng, lambda e: E.tensor_tensor(out=out.ap, in0=in0.ap, in1=in1.ap, op=op), [in0, in1], [out])

    def ts(self, eng, out, in0, s1, op0, s2=None, op1=None):
        E = self._ce(eng)
        rd = [in0] + [s for s in (s1, s2) if isinstance(s, V)]
        a1 = s1.ap if isinstance(s1, V) else s1
        a2 = s2.ap if isinstance(s2, V) else s2
        if op1 is None:
            return self.op(eng, lambda e: E.tensor_scalar(out=out.ap, in0=in0.ap, scalar1=a1, scalar2=None, op0=op0), rd, [out])
        return self.op(eng, lambda e: E.tensor_scalar(out=out.ap, in0=in0.ap, scalar1=a1, scalar2=a2, op0=op0, op1=op1), rd, [out])

    def stt(self, out, in0, s, in1, op0, op1):
        rd = [in0, in1] + ([s] if isinstance(s, V) else [])
        a = s.ap if isinstance(s, V) else s
        return self.op("dve", lambda e: self.nc.vector.scalar_tensor_tensor(out=out.ap, in0=in0.ap, scalar=a, in1=in1.ap, op0=op0, op1=op1), rd, [out])

    def copy(self, eng, out, in_):
        if eng == "act":
            return self.op("act", lambda e: self.nc.scalar.copy(out=out.ap, in_=in_.ap), [in_], [out])
        E = self._ce(eng)
        return self.op(eng, lambda e: E.tensor_copy(out=out.ap, in_=in_.ap), [in_], [out])

    def actf(self, out, in_, func, bias=None, scale=None):
        rd = [in_] + [s for s in (bias, scale) if isinstance(s, V)]
        kw = {}
        if bias is not None:
            kw["bias"] = bias.ap if isinstance(bias, V) else bias
        if scale is not None:
            kw["scale"] = scale.ap if isinstance(scale, V) else scale
        return self.op("act", lambda e: self.nc.scalar.activation(out=out.ap, in_=in_.ap, func=func, **kw), rd, [out])

    def recip(self, out, in_):
        return self.op("dve", lambda e: self.nc.vector.reciprocal(out=out.ap, in_=in_.ap), [in_], [out])

    def memset(self, eng, out, val):
        E = self._ce(eng)
        return self.op(eng, lambda e: E.memset(out.ap, val), [], [out])

    def finalize(self):
        for o in self.ops:
            for d in o.deps:
                p = self.ops[d]
                if p.eng == "pe" and o.eng == "pe" and not p.dma:
                    continue
                p.signal = True
        cnt = {e: 0 for e in self.esem}
        for o in self.ops:
            if o.dma:
                pool = self.dma_pools[o.eng]
                i = self.dma_rr[o.eng]
                self.dma_rr[o.eng] = (i + 1) % len(pool)
                s = pool[i]
                prev = self.sem_total.get(id(s), 0)
                o.prewait = (s, prev) if prev > 0 else None
                o.sem = s
                o.semval = prev + 16
                self.sem_total[id(s)] = o.semval
            elif o.signal:
                cnt[o.eng] += 1
                o.sem = self.esem[o.eng]
                o.semval = cnt[o.eng]
        self.counts = cnt

    def emit(self):
        self.finalize()
        nc = self.nc
        engobj = {"pe": nc.tensor, "act": nc.scalar, "dve": nc.vector, "pool": nc.gpsimd, "sp": nc.sync}
        with nc.Block() as block:
            def run(eng):
                E = engobj[eng]
                waited = {}
                for o in self.eng_ops[eng]:
                    need = {}
                    if o.prewait is not None:
                        need[id(o.prewait[0])] = o.prewait
                    for d in o.deps:
                        p = self.ops[d]
                        if p.sem is None:
                            continue
                        if p.eng == "pe" and eng == "pe" and not p.dma:
                            continue
                        k = id(p.sem)
                        if k not in need or need[k][1] < p.semval:
                            need[k] = (p.sem, p.semval)
                    for k, (s, v) in need.items():
                        if waited.get(k, 0) >= v:
                            continue
                        E.wait_ge(s, v)
                        waited[k] = v
                    ins = o.fn(E)
                    if o.dma:
                        ins.then_inc(o.sem, 16)
                    elif o.signal:
                        ins.then_inc(o.sem, 1)

            @block.tensor
            def _(e):
                run("pe")

            @block.scalar
            def _(e):
                run("act")

            @block.vector
            def _(e):
                run("dve")

            @block.gpsimd
            def _(e):
                run("pool")

            @block.sync
            def _(e):
                run("sp")
                for pool in self.dma_pools.values():
                    for s in pool:
                        tot = self.sem_total.get(id(s), 0)
                        if tot > 0:
                            nc.sync.wait_ge(s, tot)
        self.stack.close()
import ml_dtypes

D = 1024
T = 2048
DFF = 2816
NB = 8
EPS = 1e-6


class VecPack:
    def __init__(self):
        self.cols = {}
        self.n = 0
        self.parts = []

    def add(self, name, arr):
        a = np.asarray(arr, np.float32).reshape(-1)
        n = a.shape[0]
        nc_ = (n + 127) // 128
        pad = np.zeros(nc_ * 128, np.float32)
        pad[:n] = a
        self.cols[name] = (self.n, nc_)
        self.parts.append(pad.reshape(nc_, 128).T)
        self.n += nc_

    def build(self):
        return np.ascontiguousarray(np.concatenate(self.parts, axis=1))


def vec_layout(inp, b):
    vp = VecPack()
    vp.add("c", inp["c"][b])
    for l in range(2):
        vp.add("ada_b%d" % l, inp["ada_b"][l])
        vp.add("g_mix%d" % l, inp["norm_mix"][l])
        vp.add("g_ffn%d" % l, inp["norm_ffn"][l])
    vp.add("g_fin", inp["final_norm"])
    vp.add("qn", np.tile(inp["attn_q_norm"][0], 2))
    vp.add("kn", np.tile(inp["attn_k_norm"][0], 2))
    mu = inp["rwkv_mu"][0]
    vp.add("mu_rkv", mu[:1536])
    vp.add("mu_w", mu[1536:1664])
    vp.add("mu_a", mu[1664:1792])
    vp.add("mu_g", mu[1792:1952])
    vp.add("w0", inp["rwkv_w0"][0])
    vp.add("a0", inp["rwkv_a0"][0])
    vp.add("k_k", inp["rwkv_k_k"][0])
    vp.add("k_a", inp["rwkv_k_a"][0])
    vp.add("r_k", inp["rwkv_r_k"][0])
    vp.add("ln_g", inp["rwkv_ln_g"][0])
    vp.add("ln_b", inp["rwkv_ln_b"][0])
    cw = inp["hy_conv_w"][0]
    for j in range(3):
        vp.add("cw%d" % j, cw[j])
    vp.add("cb", inp["hy_conv_b"][0])
    vp.add("f_b1", inp["hy_f_b1"][0])
    vp.add("f_b2", inp["hy_f_b2"][0])
    vp.add("f_b3", inp["hy_f_b3"][0])
    vp.add("f_fr", inp["hy_sin_freq"][0])
    return vp


def host_consts():
    bf = ml_dtypes.bfloat16
    c = {}
    c["ident"] = np.eye(128, dtype=np.float32).astype(bf)
    c["ones"] = np.ones((128, 128), np.float32).astype(bf)
    bd = np.zeros((128, 128), np.float32)
    bd[:64, :64] = 1
    bd[64:, 64:] = 1
    c["bd"] = bd.astype(bf)
    rm = np.zeros((128, 128), np.float32)
    for i in range(64):
        rm[2 * i + 1, 2 * i] = -1.0
        rm[2 * i, 2 * i + 1] = 1.0
    c["rm"] = rm
    t = np.arange(T)
    row = (t // 64).astype(np.float32)
    col = (t % 64).astype(np.float32)
    inv = (10000.0 ** (-np.arange(0, 32, 2, dtype=np.float32) / 32)).astype(np.float32)
    ang = np.concatenate([row[:, None] * inv, col[:, None] * inv], -1)
    p = np.arange(128)
    fi = (p % 64) // 2
    c["cos"] = np.cos(ang)[:, fi].T.astype(np.float32).copy()
    c["sin"] = np.sin(ang)[:, fi].T.astype(np.float32).copy()
    j = np.arange(128)
    same = (j[:, None] // 64) == (j[None, :] // 64)
    c["m_strict"] = (same & (j[:, None] < j[None, :])).astype(np.float32)
    c["m_incl"] = (same & (j[:, None] <= j[None, :])).astype(np.float32)
    c["m4"] = np.ascontiguousarray(np.concatenate([c["m_strict"], c["m_incl"], c["m_strict"], c["m_incl"]], 1))
    c["m_strictT"] = np.ascontiguousarray(c["m_strict"].T)
    c["bd32"] = bd.copy()
    N = 2 * T
    f = np.arange(T, dtype=np.float64)[:, None]
    tt = np.arange(T, dtype=np.float64)[None, :]
    ang = 2 * np.pi * (f + 0.5) * tt / N
    C = np.cos(ang)
    S = -np.sin(ang)
    perm = np.concatenate([np.arange(0, T, 2), np.arange(1, T, 2)])
    Ch = C[:T // 2][:, perm].astype(np.float32)
    Sh = S[:T // 2][:, perm].astype(np.float32)

    def tile_ft(M):
        A = M.reshape(2, 4, 128, 4, 512)
        return np.ascontiguousarray(A.transpose(3, 0, 2, 1, 4)).reshape(4, 2, 128, 2048)

    def tile_tf(Mt):
        A = Mt.reshape(2, 8, 128, 8, 128)
        return np.ascontiguousarray(A.transpose(3, 0, 2, 1, 4)).reshape(8, 2, 128, 1024)
    c["Cft"] = tile_ft(Ch).astype(bf)
    c["Sft"] = tile_ft(Sh).astype(bf)
    c["Ctf"] = tile_tf(np.ascontiguousarray(Ch.T)).astype(bf)
    c["Stf"] = tile_tf(np.ascontiguousarray(Sh.T)).astype(bf)
    tl = np.linspace(0.0, 1.0, T, dtype=np.float32)[:, None]
    om = (2.0 * np.pi / T) * np.arange(T, dtype=np.float32)[:, None]
    bands = np.linspace(1e-4, 15, 16, dtype=np.float32)[None, :]
    feats = np.concatenate([tl, np.cos(bands * om), -np.sin(bands * om)], -1).astype(np.float32)
    c["featsT"] = np.ascontiguousarray(feats.T[:, perm])
    mn = np.log(1e-2) / 1.5
    mx = np.log(1e-2) / 0.3
    deltas = np.abs(np.linspace(mn, mx, D, dtype=np.float32))
    c["window"] = np.ascontiguousarray(np.exp(-tl * deltas).astype(np.float32)[perm])
    return c


CONST_DT = {"ident": BF16, "ones": BF16, "bd": BF16, "Cft": BF16, "Sft": BF16, "Ctf": BF16, "Stf": BF16}


class KB:
    def __init__(self, vcols, cfg):
        self.cfg = cfg
        nc = bass.Bass("TRN2", target_bir_lowering=False)
        self.nc = nc
        P = Prog(nc)
        self.P = P
        self.vcols = vcols
        self.sb = Arena(P, "sb", 206 * 1024)
        self.ps = Arena(P, "ps", 16 * 1024, psum=True)
        self.bank = [self.ps.at(i * 2048, [512], F32) for i in range(8)]
        self.din = {}
        self.pool_rr = 0

    def inp(self, name, shape, dtype=F32):
        b = self.P.dram_in(name, shape, dtype)
        self.din[name] = b
        return b

    def vcol(self, name, j=0, n=1, p=128):
        o, nc_ = self.vcols[name]
        assert j + n <= nc_, name
        return self.vecs[0:p, o + j:o + j + n]

    def wview(self, W, lead, k0, nk, c0, ncols):
        full = W.full
        if lead is not None:
            full = full[lead]
        ap = full[k0 * 128:(k0 + nk) * 128, c0:c0 + ncols].rearrange("(kc p) c -> p kc c", p=128)
        return self.P.dview(W, ap)


def build_program(vcols, nvec, cfg):
    K = KB(vcols, cfg)
    P, nc, sb = K.P, K.nc, K.sb
    bank = K.bank
    xT = K.inp("xT", [D, T])
    vecs_d = K.inp("vecs", [128, nvec])
    ada_w = K.inp("ada_w", [2, D, 6 * D])
    ffn_w1 = K.inp("ffn_w1", [2, D, DFF])
    ffn_w3 = K.inp("ffn_w3", [2, D, DFF])
    ffn_w2 = K.inp("ffn_w2", [2, DFF, D])
    cst = {}
    for nm, shp in (("ident", [128, 128]), ("ones", [128, 128]), ("bd", [128, 128]), ("rm", [128, 128]),
                    ("cos", [128, T]), ("sin", [128, T]), ("m_strict", [128, 128]), ("m_incl", [128, 128]),
                    ("Cft", [4, 2, 128, 2048]), ("Sft", [4, 2, 128, 2048]),
                    ("Ctf", [8, 2, 128, 1024]), ("Stf", [8, 2, 128, 1024]),
                    ("featsT", [33, T]), ("window", [T, D])):
        cst[nm] = K.inp("c_" + nm, shp, CONST_DT.get(nm, F32))
    K.cst = cst
    outT = P.dram_out("outT", [D, T])
    K.outT = outT
    X = sb.alloc([8, T], F32)
    hT = sb.alloc([8, T], BF16)
    vecs = sb.alloc([nvec], F32)
    K.X, K.hT, K.vecs = X, hT, vecs
    ident = sb.alloc([128], BF16)
    ones = sb.alloc([128], BF16)
    bd = sb.alloc([128], BF16)
    K.ident, K.ones, K.bd = ident, ones, bd
    cond = sb.alloc([8], F32)
    mod = sb.alloc([96], F32)
    am = sb.alloc([2, 8], F32)
    af = sb.alloc([2, 8], F32)
    K.mod, K.am, K.af = mod, am, af
    K.base_mark = sb.mark()

    P.dma("sp", vecs[:], vecs_d[:])
    P.dma("act", ident[:], cst["ident"][:])
    P.dma("act", ones[:], cst["ones"][:])
    P.dma("act", bd[:], cst["bd"][:])
    for kc in range(8):
        P.dma("sp" if kc % 2 == 0 else "act", X[:, kc, :], xT[kc * 128:(kc + 1) * 128, :])
    P.actf(cond[:], K.vcol("c", 0, 8), AF.Silu)

    def ada_phase(l):
        m = sb.mark()
        wt = [sb.alloc([8, 512], F32) for _ in range(2)]
        pm = bank[7]
        for jg in range(12):
            w = wt[jg % 2]
            P.dma("sp" if jg % 2 == 0 else "act", w[:], K.wview(ada_w, l, 0, 8, jg * 512, 512))
            for j in range(4):
                col = jg * 4 + j
                for kc in range(8):
                    P.mm(pm[:, col:col + 1], w[:, kc, j * 128:(j + 1) * 128], cond[:, kc:kc + 1],
                         start=(kc == 0), stop=(kc == 7))
        P.tt("dve", mod[:, l * 48:(l + 1) * 48], pm[:, 0:48], K.vcol("ada_b%d" % l, 0, 48), ALU.add)
        P.stt(am[:, l, :], mod[:, l * 48 + 8:l * 48 + 16], 1.0, K.vcol("g_mix%d" % l, 0, 8), ALU.add, ALU.mult)
        P.stt(af[:, l, :], mod[:, l * 48 + 32:l * 48 + 40], 1.0, K.vcol("g_ffn%d" % l, 0, 8), ALU.add, ALU.mult)
        sb.release(m)

    def rstd_block(tb, psb, rs):
        m = sb.mark()
        sq = [sb.alloc([512], BF16) for _ in range(3)]
        for kc in range(8):
            s = sq[kc % 3]
            if kc % 4 != 3:
                P.actf(s[:], X[:, kc, tb * 512:(tb + 1) * 512], AF.Square)
            else:
                P.tt("pool", s[:], X[:, kc, tb * 512:(tb + 1) * 512], X[:, kc, tb * 512:(tb + 1) * 512], ALU.mult)
            P.mm(psb[:], ones[:], s[:], start=(kc == 0), stop=(kc == 7))
        P.actf(rs[:], psb[:], AF.Sqrt, bias=K.epsc[:], scale=1.0 / D)
        P.recip(rs[:], rs[:])
        sb.release(m)

    epsc = sb.alloc([1], F32)
    K.epsc = epsc
    P.memset("dve", epsc[:], EPS)
    gnc = sb.alloc([1], F32)
    K.gnc = gnc
    P.memset("dve", gnc[:], 64e-5)
    K.base_mark = sb.mark()

    def norm_phase(a_cols, shift_cols):
        m = sb.mark()
        rs = [sb.alloc([512], F32) for _ in range(2)]
        tmp = [sb.alloc([512], F32) for _ in range(3)]
        for tb in range(4):
            r = rs[tb % 2]
            rstd_block(tb, bank[tb % 2], r)
            for kc in range(8):
                tm = tmp[kc % 3]
                P.stt(tm[:], X[:, kc, tb * 512:(tb + 1) * 512], a_cols[kc], r[:], ALU.mult, ALU.mult)
                P.actf(hT[:, kc, tb * 512:(tb + 1) * 512], tm[:], AF.Identity, bias=shift_cols[kc])
        sb.release(m)

    def ffn_phase(l, norm_cb=None):
        m = sb.mark()
        groups = [(g * 512, min(512, DFF - g * 512)) for g in range((DFF + 511) // 512)]
        w1t = [sb.alloc([8, 512], BF16) for _ in range(2)]
        w3t = [sb.alloc([8, 512], BF16) for _ in range(2)]
        w2t = [sb.alloc([4, 1024], BF16) for _ in range(2)]
        ug = sb.alloc([4, T], BF16)
        sil = [sb.alloc([512], F32) for _ in range(2)]
        gate = [mod[:, l * 48 + 40 + kc:l * 48 + 41 + kc] for kc in range(8)]

        def load(g):
            c0, w = groups[g]
            P.dma("pool", w1t[g % 2][:, :, 0:w], K.wview(ffn_w1, l, 0, 8, c0, w))
            P.dma("pool", w3t[g % 2][:, :, 0:w], K.wview(ffn_w3, l, 0, 8, c0, w))
            P.dma("pool", w2t[g % 2][:, 0:w // 128, :], K.wview(ffn_w2, l, c0 // 128, w // 128, 0, D))

        load(0)
        if norm_cb is not None:
            norm_cb()
        n = 0
        for g in range(len(groups)):
            if g + 1 < len(groups):
                load(g + 1)
            c0, w = groups[g]
            nfc = w // 128
            a, b, c2 = w1t[g % 2], w3t[g % 2], w2t[g % 2]
            for fc in range(nfc):
                for tb in range(4):
                    p1 = bank[(n % 2) * 2]
                    p3 = bank[(n % 2) * 2 + 1]
                    s = sil[n % 2]
                    n += 1
                    for kc in range(8):
                        P.mm(p1[:], a[:, kc, fc * 128:(fc + 1) * 128], hT[:, kc, tb * 512:(tb + 1) * 512],
                             start=(kc == 0), stop=(kc == 7))
                    for kc in range(8):
                        P.mm(p3[:], b[:, kc, fc * 128:(fc + 1) * 128], hT[:, kc, tb * 512:(tb + 1) * 512],
                             start=(kc == 0), stop=(kc == 7))
                    P.actf(s[:], p1[:], AF.Silu)
                    P.tt("dve", ug[:, fc, tb * 512:(tb + 1) * 512], s[:], p3[:], ALU.mult)
            k = 0
            for oc in range(8):
                for tb in range(4):
                    po = bank[4 + (k % 4)]
                    k += 1
                    for fc in range(nfc):
                        P.mm(po[:], c2[:, fc, oc * 128:(oc + 1) * 128], ug[:, fc, tb * 512:(tb + 1) * 512],
                             start=(fc == 0), stop=(fc == nfc - 1))
                    xs = X[:, oc, tb * 512:(tb + 1) * 512]
                    P.stt(xs, po[:], gate[oc], xs, ALU.mult, ALU.add)
        sb.release(m)

    def final_phase():
        m = sb.mark()
        rs = [sb.alloc([512], F32) for _ in range(2)]
        ot = [sb.alloc([512], F32) for _ in range(4)]
        n = 0
        for tb in range(4):
            r = rs[tb % 2]
            rstd_block(tb, bank[tb % 2], r)
            for kc in range(8):
                o = ot[n % 4]
                n += 1
                P.stt(o[:], X[:, kc, tb * 512:(tb + 1) * 512], K.vcol("g_fin", kc), r[:], ALU.mult, ALU.mult)
                P.dma("sp" if n % 2 == 0 else "act", outT[kc * 128:(kc + 1) * 128, tb * 512:(tb + 1) * 512], o[:])
        sb.release(m)

    def dump_x():
        for kc in range(8):
            P.dma("sp", outT[kc * 128:(kc + 1) * 128, :], X[:, kc, :])

    K.norm_phase = norm_phase
    ada_phase(0)
    stop = cfg.get("stop")
    for l in range(2):
        if l == 1:
            ada_phase(1)
        if cfg.get("mix%d" % l, True):
            norm_phase([am[:, l, kc:kc + 1] for kc in range(8)], [mod[:, l * 48 + kc:l * 48 + kc + 1] for kc in range(8)])
            if l == 0:
                mixer0_phase(K)
            else:
                mixer1_phase(K)
        if stop == "mix%d" % l:
            dump_x()
            break
        if cfg.get("ffn%d" % l, True):
            ffn_phase(l, lambda l=l: norm_phase([af[:, l, kc:kc + 1] for kc in range(8)],
                                                [mod[:, l * 48 + 24 + kc:l * 48 + 25 + kc] for kc in range(8)]))
        if stop == "ffn%d" % l:
            dump_x()
            break
    else:
        final_phase()
    P.emit()
    return K


def mixer0_phase(K):
    P, sb, bank, X, hT, cst = K.P, K.sb, K.bank, K.X, K.hT, K.cst
    cfg = K.cfg
    mix_w_in = K.inp("mix_w_in", [1, D, 2720])
    mix_w_out = K.inp("mix_w_out", [1, D, D])
    K.mix_w_in = mix_w_in
    m = sb.mark()
    yA = sb.alloc([4, T], BF16)
    K.yA = yA
    if cfg.get("attn", True):
        attention_phase(K)
    else:
        P.memset("pool", yA[:], 0.0)
    yR = sb.alloc([4, T], BF16)
    K.yR = yR
    if cfg.get("rwkv", True):
        rwkv_phase(K)
    else:
        P.memset("pool", yR[:], 0.0)
    m2 = sb.mark()
    wt = [sb.alloc([8, 512], BF16) for _ in range(2)]
    for og in range(2):
        P.dma("pool", wt[og][:], K.wview(mix_w_out, 0, 0, 8, og * 512, 512))
    n = 0
    for og in range(2):
        for oc in range(4):
            for tb in range(4):
                po = bank[n % 4]
                n += 1
                for kc in range(8):
                    src = yA if kc < 4 else yR
                    P.mm(po[:], wt[og][:, kc, oc * 128:(oc + 1) * 128], src[:, kc % 4, tb * 512:(tb + 1) * 512],
                         start=(kc == 0), stop=(kc == 7))
                xs = X[:, og * 4 + oc, tb * 512:(tb + 1) * 512]
                P.stt(xs, po[:], K.mod[:, 16 + og * 4 + oc:17 + og * 4 + oc], xs, ALU.mult, ALU.add)
    sb.release(m)


def attention_phase(K):
    P, sb, bank, X, hT, cst, yT = K.P, K.sb, K.bank, K.X, K.hT, K.cst, K.yA
    mix_w_in = K.mix_w_in
    m = sb.mark()
    qT = sb.alloc([4, T], BF16)
    kT = sb.alloc([2, T], BF16)
    vaug = sb.alloc([16, 2, 128], BF16)
    m1 = sb.mark()
    cos = sb.alloc([T], F32)
    sin = sb.alloc([T], F32)
    rm = sb.alloc([128], F32)
    P.dma("sp", cos[:], cst["cos"][:])
    P.dma("act", sin[:], cst["sin"][:])
    P.dma("sp", rm[:], cst["rm"][:])
    P.memset("pool", vaug[:, :, :, 64:128], 1.0)
    wq = [sb.alloc([8, 128], BF16) for _ in range(3)]
    sqb = [sb.alloc([512], BF16) for _ in range(2)]
    rsb = [sb.alloc([512], F32) for _ in range(2)]
    qnb = [sb.alloc([512], F32) for _ in range(2)]
    t1b = [sb.alloc([512], F32) for _ in range(2)]
    t2b = [sb.alloc([512], F32) for _ in range(2)]
    chunks = [("q", c) for c in range(4)] + [("k", g) for g in range(2)]
    n = 0
    for ci, (kind, idx) in enumerate(chunks):
        w = wq[ci % 3]
        if kind == "q":
            P.dma("pool", w[:], K.wview(mix_w_in, 0, 0, 8, idx * 128, 128))
            gain = K.vcol("qn")
            dst = qT
        else:
            P.dma("pool", w[:, :, 0:64], K.wview(mix_w_in, 0, 0, 8, 512 + idx * 64, 64))
            P.dma("pool", w[:, :, 64:128], K.wview(mix_w_in, 0, 0, 8, 512 + idx * 64, 64))
            gain = K.vcol("kn")
            dst = kT
        for tb in range(4):
            ts_ = slice(tb * 512, (tb + 1) * 512)
            pa = bank[(n % 2) * 3]
            pb = bank[(n % 2) * 3 + 1]
            pc = bank[(n % 2) * 3 + 2]
            sq, rs, qn, t1, t2 = sqb[n % 2], rsb[n % 2], qnb[n % 2], t1b[n % 2], t2b[n % 2]
            n += 1
            for kc in range(8):
                P.mm(pa[:], w[:, kc, :], hT[:, kc, ts_], start=(kc == 0), stop=(kc == 7))
            P.actf(sq[:], pa[:], AF.Square)
            P.mm(pb[:], K.bd[:], sq[:])
            P.actf(rs[:], pb[:], AF.Sqrt, bias=K.epsc[:], scale=1.0 / 64)
            P.recip(rs[:], rs[:])
            P.stt(qn[:], pa[:], gain, rs[:], ALU.mult, ALU.mult)
            P.mm(pc[:], rm[:], qn[:])
            P.tt("dve", t1[:], qn[:], cos[:, ts_], ALU.mult)
            P.tt("dve", t2[:], pc[:], sin[:, ts_], ALU.mult)
            P.tt("pool", dst[:, idx, ts_], t1[:], t2[:], ALU.add)
    wv = wq[0]
    P.dma("pool", wv[:], K.wview(mix_w_in, 0, 0, 8, 640, 128))
    for tc in range(16):
        pa = bank[6 + (tc % 2)]
        for kc in range(8):
            P.mm(pa[:, 0:128], hT[:, kc, tc * 128:(tc + 1) * 128], wv[:, kc, :], start=(kc == 0), stop=(kc == 7))
        for g in range(2):
            P.copy("act", vaug[:, tc, g, 0:64], pa[:, g * 64:(g + 1) * 64])
    sb.release(m1)
    NPB = 3
    pbuf = [sb.alloc([2, 512], BF16) for _ in range(NPB)]
    rec = [sb.alloc([512], F32) for _ in range(2)]
    prs = [(0, 1), (2, 3), (6, 7)]
    pview = [K.ps.at(a * 2048, [2, 512], F32) for a, _ in prs]
    items = [(h, qb, k2) for h in range(8) for qb in range(4) for k2 in range(8)]
    SK = 2
    for i in range(len(items) + SK):
        if i < len(items):
            h, qb, k2 = items[i]
            c, par, g = h // 2, h % 2, h // 4
            pr = slice(par * 64, par * 64 + 64)
            qs = slice(qb * 512, (qb + 1) * 512)
            pv = pview[i % 3]
            pe = pbuf[i % NPB]
            for j in range(2):
                kc = 2 * k2 + j
                P.mm(pv[:, j, :], kT[pr, g, kc * 128:(kc + 1) * 128], qT[pr, c, qs])
            P.actf(V(pe.full.rearrange("p a b -> p (a b)"), pe[:].rect),
                   V(pv.full.rearrange("p a b -> p (a b)"), pv[:].rect), AF.Exp, scale=0.125)
        if i >= SK:
            h, qb, k2 = items[i - SK]
            c, par, g = h // 2, h % 2, h // 4
            pr = slice(par * 64, par * 64 + 64)
            qs = slice(qb * 512, (qb + 1) * 512)
            n1 = (i - SK) // 8
            po = bank[4 + (n1 % 2)]
            rc = rec[n1 % 2]
            pe = pbuf[(i - SK) % NPB]
            for j in range(2):
                kc = 2 * k2 + j
                P.mm(po[:], vaug[:, kc, g, :], pe[:, j, :], start=(kc == 0), stop=(kc == 15))
            if k2 == 7:
                P.recip(rc[0:64], po[64:128])
                P.tt("dve", yT[pr, c, qs], po[0:64], rc[0:64], ALU.mult)
    sb.release(m)


C_DEC = 0.6065306597126334
GN_EPS = 64e-5


class Sub:
    def __init__(self, arena, b0, nbytes):
        self.a, self.b0, self.top, self.end = arena, b0, b0, b0 + nbytes

    def alloc(self, shape, dtype, p=128):
        n = int(np.prod(shape)) * DSZ[dtype]
        n = (n + 3) // 4 * 4
        assert self.top + n <= self.end, "sub overflow"
        b = self.a.at(self.top, shape, dtype, p)
        self.top += n
        return b


def rev(v):
    return V(v.ap[:, ::-1], v.rect)


def rwkv_phase(K):
    P, sb, bank, ps, X, hT, cst = K.P, K.sb, K.bank, K.ps, K.X, K.hT, K.cst
    w_in = K.mix_w_in
    w_up_d = K.inp("rwkv_w_up", [1, 2, 64, 512])
    a_up_d = K.inp("rwkv_a_up", [1, 2, 64, 512])
    g_up_d = K.inp("rwkv_g_up", [1, 160, 512])
    m4_d = K.inp("c_m4", [128, 512])
    mT_d = K.inp("c_m_strictT", [128, 128])
    bd32_d = K.inp("c_bd32", [128, 128])
    xsp = P.dram_scratch("xspill", [128, 8, T], F32)
    for kc in range(8):
        P.dma("sp" if kc % 2 == 0 else "act", xsp[:, kc, :], X[:, kc, :])
    xa = Sub(sb, X.byte0, 8 * T * 4)
    m = sb.mark()
    tw = sb.alloc([T], BF16)
    al = sb.alloc([T], BF16)
    sg = sb.alloc([2, T], BF16)
    g_up = sb.alloc([2, 512], BF16)
    a_up = sb.alloc([512], BF16)
    w_up = sb.alloc([512], BF16)
    bd32 = sb.alloc([128], F32)
    m4 = sb.alloc([4, 128], F32)
    mT = sb.alloc([128], F32)
    mus = sb.alloc([2, 16], F32)
    omka = sb.alloc([4], F32)
    w0c = sb.alloc([8], F32)
    P.dma("pool", g_up[:, 0, :], P.dview(g_up_d, g_up_d.full[0, 0:128, :]))
    P.dma("pool", g_up[0:32, 1, :], P.dview(g_up_d, g_up_d.full[0, 128:160, :]))
    P.dma("pool", a_up[:], P.dview(a_up_d, a_up_d.full[0].rearrange("d r c -> (d r) c")))
    P.dma("pool", w_up[:], P.dview(w_up_d, w_up_d.full[0].rearrange("d r c -> (d r) c")))
    P.dma("sp", bd32[:], bd32_d[:])
    P.dma("sp", V(m4.full.rearrange("p a b -> p (a b)"), m4[:].rect), m4_d[:])
    P.dma("sp", mT[:], mT_d[:])
    mu_names = [("mu_rkv", j) for j in range(12)] + [("mu_w", 0), ("mu_a", 0), ("mu_g", 0), ("mu_g", 1)]
    for i, (nm, j) in enumerate(mu_names):
        P.ts("dve", mus[:, 0, i:i + 1], K.vcol(nm, j), 0.5, ALU.mult)
        P.ts("dve", mus[:, 1, i:i + 1], K.vcol(nm, j), -1.0, ALU.mult, 1.0, ALU.add)
    P.ts("dve", omka[:], K.vcol("k_a", 0, 4), -1.0, ALU.mult, 1.0, ALU.add)

    cnt = {"n": 0}

    PS = {"sets": None, "i": 0}

    def alloc_proj_sets():
        sets = []
        for _ in range(2):
            pre = sb.alloc([T + 2], F32)
            t1 = sb.alloc([T], F32)
            w = sb.alloc([8, 128], BF16)
            P.memset("pool", pre[:, 0:1], 0.0)
            P.memset("pool", pre[:, T + 1:T + 2], 0.0)
            sets.append((pre, t1, w))
        PS["sets"] = sets

    def proj_shift(col0, ncols, mui, dst_fn, np_=128):
        pre, t1, w = PS["sets"][PS["i"] % 2]
        PS["i"] += 1
        P.dma("pool", w[:, :, 0:ncols], K.wview(w_in, 0, 0, 8, col0, ncols))
        for tb in range(4):
            pb = bank[cnt["n"] % 2]
            cnt["n"] += 1
            for kc in range(8):
                P.mm(pb[0:ncols, :], w[:, kc, 0:ncols], hT[:, kc, tb * 512:(tb + 1) * 512], start=(kc == 0), stop=(kc == 7))
            P.copy("act", pre[0:ncols, 1 + tb * 512:1 + (tb + 1) * 512], pb[0:ncols, :])
        q = slice(0, ncols)
        P.tt("dve", t1[q], pre[q, 0:T], pre[q, 2:T + 2], ALU.add)
        P.actf(pre[q, 1:T + 1], pre[q, 1:T + 1], AF.Identity, scale=mus[q, 1, mui:mui + 1])
        P.stt(t1[q], t1[q], mus[q, 0, mui:mui + 1], pre[q, 1:T + 1], ALU.mult, ALU.add)
        dst_fn(t1)

    mps = sb.mark()
    alloc_proj_sets()
    proj_shift(768 + 1536, 128, 12, lambda t: P.actf(tw[:], t[:], AF.Tanh))
    proj_shift(768 + 1664, 128, 13, lambda t: P.copy("act", al[:], t[:]))
    proj_shift(768 + 1792, 128, 14, lambda t: P.actf(sg[:, 0, :], t[:], AF.Sigmoid))
    proj_shift(768 + 1920, 32, 15, lambda t: P.actf(sg[0:32, 1, :], t[0:32], AF.Sigmoid))
    sb.release(mps)

    class BB:
        pass

    def alloc_bundle(al):
        b = BB()
        b.Mm = al.alloc([8, 4, 128], BF16)
        b.NT = al.alloc([8, 128], BF16)
        b.Ab = al.alloc([8, 128], BF16)
        b.ATb = al.alloc([8, 128], BF16)
        b.Pm = al.alloc([8, 128], BF16)
        b.ZR = al.alloc([2, 512], BF16)
        b.Kt = al.alloc([512], BF16)
        b.Bt = al.alloc([512], BF16)
        b.Kh = al.alloc([512], BF16)
        b.Bh = al.alloc([512], BF16)
        b.Vt = al.alloc([512], BF16)
        b.KhT = al.alloc([4, 128], BF16)
        b.BhT = al.alloc([4, 128], BF16)
        b.VT = al.alloc([4, 128], BF16)
        b.VT2 = al.alloc([4, 2, 128], BF16)
        return b

    def alloc_state(al):
        s_ = BB()
        s_.wtot = al.alloc([8], F32)
        s_.ST = al.alloc([128], F32)
        s_.STb = al.alloc([128], BF16)
        s_.XT = al.alloc([128], BF16)
        s_.UT = al.alloc([128], BF16)
        s_.UT2 = al.alloc([2, 128], BF16)
        s_.stmp = al.alloc([128], F32)
        return s_

    r_b = xa.alloc([T], BF16)
    k_b = xa.alloc([T], BF16)
    v_b = xa.alloc([T], BF16)
    kkn = xa.alloc([T], BF16)
    Yacc = xa.alloc([T], F32)
    B0 = alloc_bundle(xa)
    St = [alloc_state(xa), alloc_state(xa)]
    fA1 = xa.alloc([512], F32)
    fL1 = xa.alloc([8, 64], F32)
    fP1 = xa.alloc([8, 64], F32)
    Mm, NT, Ab, ATb = B0.Mm, B0.NT, B0.Ab, B0.ATb
    psT = [ps.at(6 * 2048, [4, 128], BF16), ps.at(7 * 2048, [4, 128], BF16)]
    P.memset("pool", B0.VT2[:], 0.0)
    for s_ in St:
        P.memset("pool", s_.UT2[:], 0.0)
    nT = {"n": 0}

    def tposes(src, dsts):
        pt = psT[nT["n"] % 2]
        nT["n"] += 1
        for j in range(4):
            P.transpose(pt[:, j, :], src[:, j * 128:(j + 1) * 128], K.ident[:])
        for eng, dv, sf in dsts:
            P.copy(eng, dv, sf(pt))

    def flat(b):
        return V(b.full.rearrange("p a b -> p (a b)"), b[:].rect)

    for fc in range(K.cfg.get("rw_fcs", 4)):
        mps = sb.mark()
        alloc_proj_sets()
        proj_shift(768 + fc * 128, 128, fc, lambda t: P.copy("act", r_b[:], t[:]))
        m1 = sb.mark()
        kk32 = sb.at(Mm.byte0, [T], F32)
        sqb = [sb.at(NT.byte0, [512], BF16), sb.at(NT.byte0 + 1024, [512], BF16)]
        rsb = [sb.at(Ab.byte0, [512], F32), sb.at(ATb.byte0, [512], F32)]

        def kdst(t):
            P.copy("act", k_b[:], t[:])
            P.ts("dve", kk32[:], t[:], K.vcol("k_k", fc), ALU.mult)
        proj_shift(768 + 512 + fc * 128, 128, 4 + fc, kdst)
        for tb in range(4):
            ts_ = slice(tb * 512, (tb + 1) * 512)
            pb = bank[2 + tb % 2]
            P.actf(sqb[tb % 2][:], kk32[:, ts_], AF.Square)
            P.mm(pb[:], K.bd[:], sqb[tb % 2][:])
            P.ts("dve", rsb[tb % 2][:], pb[:], 1e-24, ALU.max)
            P.actf(rsb[tb % 2][:], rsb[tb % 2][:], AF.Sqrt)
            P.recip(rsb[tb % 2][:], rsb[tb % 2][:])
            P.tt("dve", kkn[:, ts_], kk32[:, ts_], rsb[tb % 2][:], ALU.mult)
        sb.release(m1)
        proj_shift(768 + 1024 + fc * 128, 128, 8 + fc, lambda t: P.copy("act", v_b[:], t[:]))
        sb.release(mps)

        mF = sb.mark()
        fA = sb.alloc([512], F32)
        fL = sb.alloc([8, 64], F32)
        fP = sb.alloc([8, 64], F32)
        fQ = sb.alloc([8, 64], F32)
        fE = [sb.alloc([512], F32) for _ in range(2)]
        fKd = sb.alloc([512], F32)
        fBb = sb.alloc([512], F32)
        al_l = sb.alloc([512], BF16)
        tw_l = sb.alloc([512], BF16)
        B1 = alloc_bundle(sb)
        P.memset("pool", B1.VT2[:], 0.0)
        BBs = [B0, B1]
        for d in range(2):
            P.memset("dve", St[d].ST[:], 0.0)
            P.memset("dve", St[d].STb[:], 0.0)

        def prep_elem(d, tb, fA0=fA, fL0=fL, fP0=fP):
            B, S = BBs[d], St[d]
            fA, fL, fP = (fA0, fL0, fP0) if d == 0 else (fA1, fL1, fP1)
            dsl = slice(d * 64, d * 64 + 64)
            if d == 0:
                tsl = slice(tb * 512, (tb + 1) * 512)
                S_ = lambda b, q=slice(None): b[q, tsl]
            else:
                tsl = slice((3 - tb) * 512, (4 - tb) * 512)
                S_ = lambda b, q=slice(None): rev(b[q, tsl])
            pa, pw = bank[0], bank[1]
            P.copy("dve", al_l[dsl], S_(al, dsl))
            P.copy("dve", tw_l[dsl], S_(tw, dsl))
            P.mm(pa[:], a_up[dsl, fc * 128:(fc + 1) * 128], al_l[dsl])
            P.mm(pw[:], w_up[dsl, fc * 128:(fc + 1) * 128], tw_l[dsl])
            P.actf(fA[:], pa[:], AF.Sigmoid, bias=K.vcol("a0", d * 4 + fc))
            P.actf(flat(fL), pw[:], AF.Sigmoid, bias=K.vcol("w0", d * 4 + fc))
            for c in range(8):
                P.op("dve", lambda e, c=c: K.nc.vector.tensor_tensor_scan(out=fP[:, c, :].ap, data0=fL[:, c, :].ap, data1=fL[:, c, :].ap,
                                                                            initial=0.0, op0=ALU.add, op1=ALU.bypass),
                     [fL[:, c, :]], [fP[:, c, :]])
            tot_b = V(fP.full[:, :, 63:64].broadcast_to([128, 8, 64]), fP[:].rect)
            P.tt("dve", fQ[:], tot_b, fP[:], ALU.subtract)
            P.tt("dve", fL[:], fP[:], fL[:], ALU.subtract)
            P.actf(S.wtot[:], V(fP.full[:, :, 63], fP[:].rect), AF.Exp, scale=-C_DEC)
            P.ts("dve", fKd[:], fA[:], K.vcol("k_a", fc), ALU.mult, omka[:, fc:fc + 1], ALU.add)
            P.tt("dve", fKd[:], fKd[:], S_(k_b), ALU.mult)
            P.tt("dve", fBb[:], fA[:], S_(kkn), ALU.mult)
            E0, E1 = fE
            P.actf(E0[:], flat(fL), AF.Exp, scale=-C_DEC)
            P.stt(B.ZR[:, 0, :], S_(kkn), -1.0, E0[:], ALU.mult, ALU.mult)
            P.actf(E1[:], flat(fP), AF.Exp, scale=-C_DEC)
            P.tt("dve", B.ZR[:, 1, :], S_(r_b), E1[:], ALU.mult)
            P.actf(E0[:], flat(fP), AF.Exp, scale=C_DEC)
            P.tt("dve", B.Kt[:], fKd[:], E0[:], ALU.mult)
            P.tt("dve", B.Bt[:], fBb[:], E0[:], ALU.mult)
            P.actf(E1[:], flat(fQ), AF.Exp, scale=-C_DEC)
            P.tt("dve", B.Kh[:], fKd[:], E1[:], ALU.mult)
            P.tt("dve", B.Bh[:], fBb[:], E1[:], ALU.mult)
            P.copy("dve", B.Vt[:], S_(v_b))
            tposes(B.Kh, [("act", B.KhT[:], lambda pt: pt[:])])
            tposes(B.Bh, [("act", B.BhT[:], lambda pt: pt[:])])
            tposes(B.Vt, [("act", B.VT[:], lambda pt: pt[:]),
                          ("act", V(B.VT2.full[:, :, 0, 0:64], B.VT2[:].rect), lambda pt: V(pt.full[:, :, 0:64], pt[:].rect)),
                          ("act", V(B.VT2.full[:, :, 1, 64:128], B.VT2[:].rect), lambda pt: V(pt.full[:, :, 64:128], pt[:].rect))])

        def m_stage():
            for pb_ in range(4):
                bs = slice(pb_ * 128, (pb_ + 1) * 128)
                for hd in range(2):
                    pr = slice(hd * 64, hd * 64 + 64)
                    i8 = pb_ * 2 + hd
                    for d in range(2):
                        B = BBs[d]
                        pm_ = bank[2 + d]
                        pn = bank[4 + d]
                        rhs = V(B.ZR.full[pr, :, bs], B.ZR[pr].rect)
                        P.mm(pm_[:, 0:256], B.Bt[pr, bs], rhs)
                        P.mm(pm_[:, 256:512], B.Kt[pr, bs], rhs)
                        P.mm(pn[:, 0:128], B.ZR[pr, 0, bs], B.Bt[pr, bs])
                    for d in range(2):
                        B = BBs[d]
                        P.tt("dve", V(B.Mm.full[:, i8, :, :].rearrange("p a b -> p (a b)"), B.Mm[:, i8].rect), bank[2 + d][:],
                             V(m4.full.rearrange("p a b -> p (a b)"), m4[:].rect), ALU.mult)
                        P.tt("dve", B.NT[:, i8, :], bank[4 + d][:, 0:128], mT[:], ALU.mult)

        def t_stage():
            for i8 in range(8):
                for d in range(2):
                    B = BBs[d]
                    P.copy("act", B.Ab[:, i8, :], B.Mm[:, i8, 0, :])
                    P.copy("act", B.ATb[:, i8, :], B.NT[:, i8, :])
                    P.tt("dve", B.Pm[:, i8, :], B.Mm[:, i8, 0, :], K.ident[:], ALU.add)
            for lvl in range(5):
                for hf in range(2):
                    q4 = slice(hf * 4, hf * 4 + 4)
                    for d in range(2):
                        B = BBs[d]
                        for i4 in range(4):
                            i8 = hf * 4 + i4
                            cs4 = slice(i4 * 128, (i4 + 1) * 128)
                            P.mm(bank[2 * d][:, cs4], B.ATb[:, i8, :], B.Ab[:, i8, :])
                            P.mm(bank[2 * d + 1][:, cs4], B.Ab[:, i8, :], B.ATb[:, i8, :])
                    for d in range(2):
                        B = BBs[d]
                        P.copy("act", flat4(B.Ab, q4), bank[2 * d][:])
                        P.copy("dve", flat4(B.ATb, q4), bank[2 * d + 1][:])
                    for d in range(2):
                        B = BBs[d]
                        for i4 in range(4):
                            i8 = hf * 4 + i4
                            P.mm(bank[4 + d][:, i4 * 128:(i4 + 1) * 128], B.ATb[:, i8, :], B.Pm[:, i8, :])
                    for d in range(2):
                        B = BBs[d]
                        pv = flat4(B.Pm, q4)
                        P.tt("dve", pv, pv, bank[4 + d][:], ALU.add)

        def flat4(buf, q4):
            return V(buf.full[:, q4, :].rearrange("p a b -> p (a b)"), buf[:, q4].rect)

        def chain_stage(tb):
            CB = [(bank[6], bank[7], bank[0], bank[1]), (bank[2], bank[3], bank[4], bank[5])]
            for c in range(8):
                pb_, e = c // 2, c % 2
                js = slice(e * 64, e * 64 + 64)
                cs = slice(c * 64, (c + 1) * 64)
                for d in range(2):
                    B, S = BBs[d], St[d]
                    px = CB[d][0]
                    P.mm(px[0:64, 0:128], B.ZR[:, 0, cs], S.STb[:], start=True, stop=False)
                    for hd in range(2):
                        i8 = pb_ * 2 + hd
                        hs_ = slice(hd * 64, hd * 64 + 64)
                        P.mm(px[0:64, hs_], B.Mm[js, i8, 2, js], B.VT[js, pb_, hs_], start=False, stop=(hd == 1))
                for d in range(2):
                    P.copy("act", St[d].XT[js], CB[d][0][0:64, 0:128])
                for d in range(2):
                    B, S = BBs[d], St[d]
                    pu = CB[d][1]
                    for hd in range(2):
                        i8 = pb_ * 2 + hd
                        hs_ = slice(hd * 64, hd * 64 + 64)
                        P.mm(pu[0:64, hs_], B.Pm[js, i8, js], S.XT[js, hs_])
                for d in range(2):
                    S = St[d]
                    pu = CB[d][1]
                    P.copy("act", S.UT[js], pu[0:64, 0:128])
                    P.copy("act", V(S.UT2.full[js, 0, 0:64], S.UT2[js].rect), pu[0:64, 0:64])
                    P.copy("act", V(S.UT2.full[js, 1, 64:128], S.UT2[js].rect), pu[0:64, 64:128])
                for d in range(2):
                    B, S = BBs[d], St[d]
                    py, pss = CB[d][2], CB[d][3]
                    P.mm(py[:, 0:64], S.STb[:], B.ZR[:, 1, cs], start=True, stop=False)
                    for hd in range(2):
                        i8 = pb_ * 2 + hd
                        P.mm(py[:, 0:64], S.UT2[js, hd, :], B.Mm[js, i8, 1, js], start=False, stop=False)
                        P.mm(py[:, 0:64], B.VT2[js, pb_, hd, :], B.Mm[js, i8, 3, js], start=False, stop=(hd == 1))
                    P.mm(pss[:, 0:128], B.BhT[js, pb_, :], S.UT[js], start=True, stop=False)
                    P.mm(pss[:, 0:128], B.KhT[js, pb_, :], B.VT[js, pb_, :], start=False, stop=True)
                for d in range(2):
                    S = St[d]
                    pss = CB[d][3]
                    P.tt("dve", S.stmp[:], pss[:, 0:128], bd32[:], ALU.mult)
                    P.stt(S.ST[:], S.ST[:], S.wtot[:, c:c + 1], S.stmp[:], ALU.mult, ALU.add)
                    P.copy("act", S.STb[:], S.ST[:])
                yv0 = Yacc[:, tb * 512 + c * 64:tb * 512 + (c + 1) * 64]
                P.tt("dve", yv0, yv0, CB[0][2][:, 0:64], ALU.add)
                t0 = T - (tb * 512 + (c + 1) * 64)
                yv1 = rev(Yacc[:, t0:t0 + 64])
                P.tt("dve", yv1, yv1, CB[1][2][:, 0:64], ALU.add)

        P.memset("pool", Yacc[:], 0.0)
        v4s = K.cfg.get("v4_stage", 4)
        for tb in range(4):
            for d in range(2):
                prep_elem(d, tb)
            if v4s >= 2:
                m_stage()
            if v4s >= 3:
                t_stage()
            if v4s >= 4:
                chain_stage(tb)
        sb.release(mF)
        m1 = sb.mark()
        pts = [[sb.alloc([512], F32) for _ in range(5)] for _ in range(2)]
        for tb in range(4):
            ts_ = slice(tb * 512, (tb + 1) * 512)
            cen, sq, rs, rk, bon = pts[tb % 2]
            p1, p2, p3, p4 = (bank[2], bank[3], bank[4], bank[5]) if tb % 2 == 0 else (bank[6], bank[7], bank[0], bank[1])
            P.mm(p1[:], bd32[:], Yacc[:, ts_])
            P.stt(cen[:], p1[:], -1.0 / 64, Yacc[:, ts_], ALU.mult, ALU.add)
            P.actf(sq[:], cen[:], AF.Square)
            P.mm(p2[:], bd32[:], sq[:])
            P.actf(rs[:], p2[:], AF.Sqrt, bias=K.gnc[:], scale=1.0 / 64)
            P.recip(rs[:], rs[:])
            P.tt("dve", cen[:], cen[:], rs[:], ALU.mult)
            P.ts("dve", cen[:], cen[:], K.vcol("ln_g", fc), ALU.mult, K.vcol("ln_b", fc), ALU.add)
            P.stt(rk[:], r_b[:, ts_], K.vcol("r_k", fc), k_b[:, ts_], ALU.mult, ALU.mult)
            P.mm(p3[:], bd32[:], rk[:])
            P.tt("dve", bon[:], p3[:], v_b[:, ts_], ALU.mult)
            P.tt("pool", cen[:], cen[:], bon[:], ALU.add)
            P.mm(p4[:], g_up[:, 0, fc * 128:(fc + 1) * 128], sg[:, 0, ts_], start=True, stop=False)
            P.mm(p4[:], g_up[0:32, 1, fc * 128:(fc + 1) * 128], sg[0:32, 1, ts_], start=False, stop=True)
            P.tt("dve", K.yR[:, fc, ts_], cen[:], p4[:], ALU.mult)
        sb.release(m1)
    sb.release(m)
    for kc in range(8):
        P.dma("sp" if kc % 2 == 0 else "act", X[:, kc, :], xsp[:, kc, :])


TWO_PI = 6.283185307179586
MAGIC = 12582912.0


def mixer1_phase(K):
    P, sb, bank, X, hT, cst, ps = K.P, K.sb, K.bank, K.X, K.hT, K.cst, K.ps
    cfg = K.cfg
    hy_w_in = K.inp("hy_w_in", [1, D, 3 * D])
    hy_w_out = K.inp("hy_w_out", [1, D, D])
    f_w1 = K.inp("hy_f_w1", [1, 33, 64])
    f_w2 = K.inp("hy_f_w2", [1, 64, 64])
    f_w3 = K.inp("hy_f_w3", [1, 64, 64])
    f_out = K.inp("hy_f_out", [1, 64, 4 * D])
    skip_d = K.inp("skip_bc", [128, 2 * D])
    hspec = P.dram_scratch("hspec", [4, 128, 32 * 512], BF16)
    SC = 2.0 / (2 * T)
    m = sb.mark()
    hid3b = sb.alloc([T], BF16)
    woutb = sb.alloc([4 * D], BF16)
    skip2 = sb.alloc([2 * D], F32)
    m0 = sb.alloc([1], F32)
    P.dma("sp", skip2[:], skip_d[:])
    P.ts("dve", skip2[:], skip2[:], SC, ALU.mult)
    mw = sb.mark()
    wf = sb.alloc([4 * D], F32)
    P.dma("sp", wf[0:64, :], P.dview(f_out, f_out.full[0]))
    for o_ in range(2):
        w0v = wf[0:64, o_ * 2048:o_ * 2048 + 1024]
        w1v = wf[0:64, o_ * 2048 + 1024:o_ * 2048 + 2048]
        P.tt("dve", woutb[0:64, o_ * 2048:o_ * 2048 + 1024], w0v, w1v, ALU.add)
        P.tt("dve", woutb[0:64, o_ * 2048 + 1024:o_ * 2048 + 2048], w0v, w1v, ALU.subtract)
    sb.release(mw)
    P.memset("dve", m0[:], 1.0)
    P.memset("dve", m0[0:1], 0.0)
    m1 = sb.mark()
    feats = sb.alloc([T], F32)
    hA = sb.alloc([T], F32)
    w1 = sb.alloc([64], F32)
    w2 = sb.alloc([64], F32)
    w3 = sb.alloc([64], F32)
    frb = sb.alloc([3], F32)
    arg = [sb.alloc([512], F32) for _ in range(2)]
    vv = [sb.alloc([512], F32) for _ in range(2)]
    P.dma("sp", feats[0:33, :], cst["featsT"][:])
    P.dma("act", w1[0:33, :], P.dview(f_w1, f_w1.full[0]))
    P.dma("act", w2[0:64, :], P.dview(f_w2, f_w2.full[0]))
    P.dma("act", w3[0:64, :], P.dview(f_w3, f_w3.full[0]))
    fr = K.vcol("f_fr", 0, 1, 64)
    for i, nm in enumerate(("f_b1", "f_b2", "f_b3")):
        P.tt("dve", frb[0:64, i:i + 1], K.vcol(nm, 0, 1, 64), fr, ALU.mult)
    n = 0
    for li, (w, kk_) in enumerate(((w1, 33), (w2, 64), (w3, 64))):
        for tb in range(4):
            ts_ = slice(tb * 512, (tb + 1) * 512)
            pb = bank[n % 2]
            a, v = arg[n % 2], vv[n % 2]
            n += 1
            src = feats if li == 0 else hA
            P.mm(pb[0:64, :], w[0:kk_, :], src[0:kk_, ts_])
            P.ts("dve", a[0:64], pb[0:64, :], fr, ALU.mult, frb[0:64, li:li + 1], ALU.add)
            P.ts("dve", v[0:64], a[0:64], 1.0 / TWO_PI, ALU.mult, MAGIC, ALU.add)
            P.ts("dve", v[0:64], v[0:64], MAGIC, ALU.subtract)
            P.stt(a[0:64], v[0:64], -TWO_PI, a[0:64], ALU.mult, ALU.add)
            P.ts("dve", a[0:64], a[0:64], 3.14159, ALU.min, -3.14159, ALU.max)
            if li < 2:
                P.actf(hA[0:64, ts_], a[0:64], AF.Sin)
            else:
                P.actf(hid3b[0:64, ts_], a[0:64], AF.Sin)
    sb.release(m1)

    rr = [0]

    def dq():
        rr[0] += 1
        return "sp" if rr[0] % 2 == 0 else "act"

    def flat3(b):
        return V(b.full.rearrange("p a b -> p (a b)"), b[:].rect)

    def fwd_dft(rhsC, rhsS, epilogue, tiles):
        NB = len(tiles)

        def load(fc):
            cE, cO, sE, sO = tiles[fc % NB]
            P.dma("sp", flat3(cE), cst["Ctf"][fc, 0])
            P.dma("act", flat3(sE), cst["Stf"][fc, 0])
            P.dma("sp", flat3(cO), cst["Ctf"][fc, 1])
            P.dma("act", flat3(sO), cst["Stf"][fc, 1])

        for fc in range(min(NB - 1, 8)):
            load(fc)
        for fc in range(8):
            if fc + NB - 1 < 8:
                load(fc + NB - 1)
            cE, cO, sE, sO = tiles[fc % NB]
            bk = [bank[4 * (fc % 2) + j] for j in range(4)]
            for t8 in range(8):
                P.mm(bk[0][:], cE[:, t8, :], rhsC[:, t8, :], start=(t8 == 0), stop=(t8 == 7))
                P.mm(bk[2][:], sE[:, t8, :], rhsS[:, t8, :], start=(t8 == 0), stop=(t8 == 7))
            for t8 in range(8):
                P.mm(bk[1][:], cO[:, t8, :], rhsC[:, 8 + t8, :], start=(t8 == 0), stop=(t8 == 7))
                P.mm(bk[3][:], sO[:, t8, :], rhsS[:, 8 + t8, :], start=(t8 == 0), stop=(t8 == 7))
            epilogue(fc, bk[0], bk[1], bk[2], bk[3])

    m1 = sb.mark()
    hs = sb.alloc([16, 512], BF16)
    hd = sb.alloc([16, 512], BF16)
    Hst = [sb.alloc([512], BF16) for _ in range(8)]
    hn = [0]
    win = [sb.alloc([512], F32) for _ in range(2)]
    f0 = [sb.alloc([512], F32) for _ in range(2)]
    f1 = [sb.alloc([512], F32) for _ in range(2)]
    etmp = [sb.alloc([512], F32) for _ in range(4)]
    tiles = [tuple(sb.alloc([8, 128], BF16) for _ in range(4)) for _ in range(2)]
    n = 0
    for o in range(2):
        for hh in range(2):
            for sc in range(16):
                w_ = win[n % 2]
                a0, a1 = f0[n % 2], f1[n % 2]
                pb0, pb1 = bank[4 + (n % 2) * 2], bank[5 + (n % 2) * 2]
                n += 1
                P.dma(dq(), w_[:], cst["window"][sc * 128:(sc + 1) * 128, hh * 512:(hh + 1) * 512])
                c0 = o * 2048 + hh * 512
                P.mm(pb0[:], hid3b[0:64, sc * 128:(sc + 1) * 128], woutb[0:64, c0:c0 + 512])
                P.mm(pb1[:], hid3b[0:64, sc * 128:(sc + 1) * 128], woutb[0:64, c0 + 1024:c0 + 1536])
                P.tt("dve", hs[:, sc, :], pb0[:], w_[:], ALU.mult)
                P.tt("dve", hd[:, sc, :], pb1[:], w_[:], ALU.mult)
                if sc == 0:
                    P.tt("dve", a0[0:1], hs[0:1, 0, :], hd[0:1, 0, :], ALU.add)
                    P.ts("dve", hs[0:1, 0, :], a0[0:1], 0.5, ALU.mult)
                    P.ts("dve", hd[0:1, 0, :], a0[0:1], 0.5, ALU.mult)

            def epi(fc, eR, oR_, eI, oI_, o=o, hh=hh):
                oR, oI, oRp, oRm = etmp
                sk = skip2[:, o * D + hh * 512:o * D + (hh + 1) * 512]
                hb = [Hst[(hn[0] + i) % 8] for i in range(4)]
                hn[0] += 4
                P.actf(oR[:], oR_[:], AF.Identity, scale=SC)
                P.actf(oI[:], oI_[:], AF.Identity, scale=SC)
                P.tt("dve", oRp[:], oR[:], sk, ALU.add)
                P.tt("pool", oRm[:], oR[:], sk, ALU.subtract)
                P.stt(hb[0][:], eR[:], SC, oRp[:], ALU.mult, ALU.add)
                P.stt(hb[1][:], eI[:], SC, oI[:], ALU.mult, ALU.add)
                P.stt(hb[2][:], eR[:], SC, oRm[:], ALU.mult, ALU.subtract)
                P.stt(hb[3][:], eI[:], -SC, oI[:], ALU.mult, ALU.add)
                for sl in range(4):
                    P.dma("sp", hspec[o * 2 + hh, :, (fc * 4 + sl) * 512:(fc * 4 + sl + 1) * 512], hb[sl][:])

            fwd_dft(hs, hd, epi, tiles)
    sb.release(m1)
    sb.release(m)

    m = sb.mark()
    xsp2 = P.dram_scratch("xspill2", [128, 8, T], F32)
    for kc in range(8):
        P.dma("sp" if kc % 2 == 0 else "act", xsp2[:, kc, :], X[:, kc, :])
    xa = Sub(sb, X.byte0, 8 * T * 4)
    tq2 = [xa.alloc([512], F32) for _ in range(12)]
    tq1x = [xa.alloc([512], F32) for _ in range(4)]
    xtiles = [tuple(xa.alloc([8, 128], BF16) for _ in range(4)) for _ in range(2)]
    xitiles = [(xa.alloc([4, 512], BF16), xa.alloc([4, 512], BF16)) for _ in range(2)]
    u = sb.alloc([16, 512], BF16)
    xgs = [sb.alloc([4, T], BF16) for _ in range(2)]
    cur = {"xg": xgs[0]}
    psT = [ps.at(6 * 2048, [4, 128], BF16), ps.at(7 * 2048, [4, 128], BF16)]
    cnt = {"n": 0, "nt": 0}

    def transposes_to_u(src_fn, dc, eng):
        for g4 in range(4):
            pt = psT[cnt["nt"] % 2]
            cnt["nt"] += 1
            for j in range(4):
                tc = g4 * 4 + j
                P.transpose(pt[:, j, :], src_fn(tc), K.ident[:])
            P.copy(eng, u[:, g4 * 4:(g4 + 1) * 4, dc * 128:(dc + 1) * 128], pt[:])

    def nat2(v):
        return V(v.ap.rearrange("p (n r) -> p n r", r=2), v.rect)

    def perm2(v):
        return V(v.ap.rearrange("p (r n) -> p n r", r=2), v.rect)

    def proj(hh, parts):
        m1 = sb.mark()
        HN = T // 2
        pe_ = sb.alloc([HN + 2], F32)
        po_ = sb.alloc([HN + 2], F32)
        tmp = sb.alloc([T], F32)
        vch = [sb.alloc([T], BF16) for _ in range(2)]
        wt = [sb.alloc([8, 128], BF16) for _ in range(3)]
        for b_ in (pe_, po_):
            P.memset("pool", b_[:, 0:1], 0.0)
            P.memset("pool", b_[:, HN + 1:HN + 2], 0.0)
        for part in parts:
            for dc in range(4):
                col0 = part * 1024 + hh * 512 + dc * 128
                vc = col0 // 128
                n = cnt["n"]
                w = wt[n % 3]
                P.dma("pool", w[:], K.wview(hy_w_in, 0, 0, 8, col0, 128))
                for tb in range(4):
                    pb = bank[4 + (cnt["n"] % 2)]
                    cnt["n"] += 1
                    for kc in range(8):
                        P.mm(pb[:], w[:, kc, :], hT[:, kc, tb * 512:(tb + 1) * 512], start=(kc == 0), stop=(kc == 7))
                    P.copy("act", pe_[:, 1 + tb * 256:1 + (tb + 1) * 256], V(pb.full[:, 0:512:2], pb[:].rect))
                    P.copy("act", po_[:, 1 + tb * 256:1 + (tb + 1) * 256], V(pb.full[:, 1:512:2], pb[:].rect))
                dst = vch[dc % 2] if part == 0 else None
                d_e = (dst[:, 0:HN] if dst is not None else cur["xg"][:, dc, 0:HN])
                d_o = (dst[:, HN:T] if dst is not None else cur["xg"][:, dc, HN:T])
                cb, c0_, c1_, c2_ = K.vcol("cb", vc), K.vcol("cw0", vc), K.vcol("cw1", vc), K.vcol("cw2", vc)
                te, to = tmp[:, 0:HN], tmp[:, HN:T]
                P.actf(te, pe_[:, 1:HN + 1], AF.Identity, bias=cb, scale=c1_)
                P.stt(te, po_[:, 0:HN], c0_, te, ALU.mult, ALU.add)
                P.stt(d_e, po_[:, 1:HN + 1], c2_, te, ALU.mult, ALU.add)
                P.actf(to, po_[:, 1:HN + 1], AF.Identity, bias=cb, scale=c1_)
                P.stt(to, pe_[:, 1:HN + 1], c0_, to, ALU.mult, ALU.add)
                P.stt(d_o, pe_[:, 2:HN + 2], c2_, to, ALU.mult, ALU.add)
                if part == 0:
                    vb = vch[dc % 2]
                    transposes_to_u(lambda tc, vb=vb: vb[:, tc * 128:(tc + 1) * 128], dc, "dve")
        sb.release(m1)

    def conv(hh, o):
        m1 = sb.mark()
        HY = sb.alloc([8, 4, 512], BF16)
        tq = [sb.alloc([512], F32) for _ in range(8)] + tq1x
        mt = sb.mark()
        tiles = xtiles
        xg = cur["xg"]
        P.dma("sp", HY[:], hspec[o * 2 + hh])

        def epi(fc, eR, oR_, eI, oI_):
            xr, xrp, xi, xip, a1, a2, b1, b2, c1, c2, d1, d2 = tq if fc % 2 == 0 else tq2
            P.copy("act", a1[:], oR_[:])
            P.copy("act", a2[:], oI_[:])
            P.tt("dve", xr[:], eR[:], a1[:], ALU.add)
            P.tt("dve", xrp[:], eR[:], a1[:], ALU.subtract)
            P.tt("dve", xi[:], eI[:], a2[:], ALU.add)
            P.stt(xip[:], eI[:], -1.0, a2[:], ALU.mult, ALU.add)
            Hre, Him, Hrp, Hip = (HY[:, fc, s_, :] for s_ in range(4))
            P.tt("dve", a1[:], xr[:], Hre, ALU.mult)
            P.tt("dve", a2[:], xi[:], Him, ALU.mult)
            P.tt("dve", b1[:], xr[:], Him, ALU.mult)
            P.tt("dve", b2[:], xi[:], Hre, ALU.mult)
            P.tt("dve", c1[:], xrp[:], Hrp, ALU.mult)
            P.tt("dve", c2[:], xip[:], Hip, ALU.mult)
            P.tt("dve", d1[:], xrp[:], Hip, ALU.mult)
            P.tt("dve", d2[:], xip[:], Hrp, ALU.mult)
            P.tt("pool", a1[:], a1[:], a2[:], ALU.subtract)
            P.tt("pool", b1[:], b1[:], b2[:], ALU.add)
            P.tt("pool", c1[:], c1[:], c2[:], ALU.subtract)
            P.tt("pool", d1[:], d1[:], d2[:], ALU.add)
            P.tt("pool", HY[:, fc, 0, :], a1[:], c1[:], ALU.add)
            P.tt("pool", HY[:, fc, 1, :], b1[:], d1[:], ALU.subtract)
            P.tt("dve", HY[:, fc, 2, :], a1[:], c1[:], ALU.subtract)
            P.tt("dve", HY[:, fc, 3, :], b1[:], d1[:], ALU.add)

        fwd_dft(u, u, epi, tiles)
        sb.release(mt)
        itiles = xitiles
        isteps = [(tb, fg) for tb in range(4) for fg in range(2)]

        def iload(k):
            tb, fg = isteps[k]
            ct, st = itiles[k % 2]
            P.dma("sp", flat3(ct), cst["Cft"][tb, fg])
            P.dma("act", flat3(st), cst["Sft"][tb, fg])

        iload(0)
        for k, (tb, fg) in enumerate(isteps):
            if k + 1 < len(isteps):
                iload(k + 1)
            ts_ = slice(tb * 512, (tb + 1) * 512)
            sR = 0 if tb < 2 else 2
            pbk = [bank[4 * (tb % 2) + dc] for dc in range(4)]
            ct, st = itiles[k % 2]
            for f4 in range(4):
                fc = fg * 4 + f4
                for dc in range(4):
                    P.mm(pbk[dc][:], HY[:, fc, sR, dc * 128:(dc + 1) * 128], ct[:, f4, :], start=(fc == 0), stop=False)
                for dc in range(4):
                    P.mm(pbk[dc][:], HY[:, fc, sR + 1, dc * 128:(dc + 1) * 128], st[:, f4, :], start=False, stop=(fc == 7))
            if fg == 1:
                for dc in range(4):
                    P.tt("dve", xg[:, dc, ts_], xg[:, dc, ts_], pbk[dc][:], ALU.mult)
        sb.release(m1)

    for hh in range(2):
        cur["xg"] = xgs[hh]
        proj(hh, [0, 1])
        conv(hh, 0)
        for dc in range(4):
            transposes_to_u(lambda tc, dc=dc, xg=xgs[hh]: xg[:, dc, tc * 128:(tc + 1) * 128], dc, "act")
        proj(hh, [2])
        conv(hh, 1)
    m1 = sb.mark()
    wo = [sb.alloc([8, 512], BF16) for _ in range(2)]
    for og in range(2):
        P.dma("pool", wo[og][:], K.wview(hy_w_out, 0, 0, 8, og * 512, 512))
    for kc in range(8):
        P.dma("sp" if kc % 2 == 0 else "act", X[:, kc, :], xsp2[:, kc, :])
    n = 0
    for og in range(2):
        for oc in range(4):
            for tb in range(4):
                po = bank[4 + (n % 4)]
                n += 1
                for kc in range(8):
                    P.mm(po[:], wo[og][:, kc, oc * 128:(oc + 1) * 128], xgs[kc // 4][:, kc % 4, tb * 512:(tb + 1) * 512],
                         start=(kc == 0), stop=(kc == 7))
                par, n0 = tb // 2, (tb % 2) * 512
                xrow = X[:, og * 4 + oc, :]
                xs = V(xrow.ap.rearrange("p (n r) -> p n r", r=2)[:, n0:n0 + 512, par], xrow.rect)
                P.stt(xs, po[:], K.mod[:, 48 + 16 + og * 4 + oc:48 + 17 + og * 4 + oc], xs, ALU.mult, ALU.add)
    sb.release(m1)
    sb.release(m)


_CACHE = {}


def _get_consts():
    if "c" not in _CACHE:
        _CACHE["c"] = host_consts()
    return _CACHE["c"]


def make_in_maps(inp):
    consts = _get_consts()
    maps = []
    vcols = None
    nvec = None
    shared = {}
    for k in ("ada_w", "ffn_w1", "ffn_w3", "ffn_w2", "mix_w_in", "mix_w_out", "hy_w_in", "hy_w_out",
              "hy_f_w1", "hy_f_w2", "hy_f_w3", "hy_f_out", "rwkv_w_up", "rwkv_a_up", "rwkv_g_up"):
        shared[k] = np.ascontiguousarray(inp[k])
    for b in range(NB):
        vp = vec_layout(inp, b)
        vecs = vp.build()
        vcols, nvec = vp.cols, vp.n
        m = {"xT": np.ascontiguousarray(inp["x"][b].T), "vecs": vecs,
             "skip_bc": np.ascontiguousarray(np.broadcast_to(inp["hy_skip"][0].reshape(1, 2 * D), (128, 2 * D)))}
        for k, v in consts.items():
            m["c_" + k] = v
        m.update(shared)
        maps.append(m)
    return maps, vcols, nvec


def kernel(**inputs):
    cfg = inputs.pop("_cfg", {})
    inp = {k: np.asarray(v) for k, v in inputs.items()}
    maps, vcols, nvec = make_in_maps(inp)
    K = build_program(vcols, nvec, cfg)
    names = set(K.din.keys())
    maps = [{k: v for k, v in m.items() if k in names} for m in maps]
    res = run_bass_kernel_spmd(K.nc, maps, core_ids=list(range(NB)))
    out = np.stack([np.ascontiguousarray(r["outT"].T) for r in res.results], 0)
    return out.astype(np.float32)
```
